# Optimizing a Trainium2 kernel written in Bass

```python
import math
import jax
import jax.numpy as jnp
from jax import lax
import numpy as np

D_MODEL = 1024
BATCH = 8
SEQ = 4096
DEPTH = 2

MEM_LEN = 256
MIX_W = 512
N_BRANCH = 3
CONV_K = 3
SB_HEADS = 8
SB_HEAD_DIM = MIX_W // SB_HEADS
SB_BLOCK = 128
RWKV_HEADS = 8
RWKV_HEAD = MIX_W // RWKV_HEADS
DECAY_LORA = 64
ICLR_LORA = 64
GATE_LORA = 128
VRES_LORA = 32
DECAY_SCALE = math.exp(-0.5)
CA_HEADS = 4
CA_HEAD_DIM = D_MODEL // CA_HEADS
FFN_DIM = 2816
FFN_CONV_K = 3
RMS_EPS = 1e-6
GN_EPS = 64e-5

CONV_COLS = 3 * MIX_W
SB_COLS = 3 * MIX_W
GATE_COLS = N_BRANCH * D_MODEL
RWKV_COLS = 3 * MIX_W + DECAY_LORA + ICLR_LORA + GATE_LORA
COMB_COLS = CONV_COLS + SB_COLS + GATE_COLS + RWKV_COLS
OFF_SB = CONV_COLS
OFF_GATE = OFF_SB + SB_COLS
OFF_RWKV = OFF_GATE + GATE_COLS

kernel_name = "hybrid_gated_conv_stickbreak_rwkv7_block"


def rms_norm(x, g):
    xf = x.astype(jnp.float32)
    y = xf * lax.rsqrt(jnp.mean(xf * xf, axis=-1, keepdims=True) + RMS_EPS)
    return (y * g.astype(jnp.float32)).astype(x.dtype)


def causal_dwconv(x, w):
    c = x.shape[-1]
    return lax.conv_general_dilated(
        x, w[:, None, :].astype(x.dtype), window_strides=(1,),
        padding=[(w.shape[0] - 1, 0)], dimension_numbers=("NWC", "WIO", "NWC"),
        feature_group_count=c)


def stick_breaking_attention(q, k, v):
    s_len, dh = q.shape[2], q.shape[3]
    scale = dh ** -0.5
    outs = []
    for i in range(s_len // SB_BLOCK):
        start, end = i * SB_BLOCK, (i + 1) * SB_BLOCK
        q_blk = q[:, :, start:end]
        k_blk, v_blk = k[:, :, :end], v[:, :, :end]
        z = jnp.einsum("bhqd,bhkd->bhqk", q_blk, k_blk).astype(jnp.float32) * scale
        q_pos = start + jnp.arange(SB_BLOCK)
        k_pos = jnp.arange(end)
        mask = k_pos[None, :] < q_pos[:, None]
        log_not = jnp.where(mask, jax.nn.log_sigmoid(-z), 0.0)
        suffix = lax.cumsum(log_not, axis=3, reverse=True) - log_not
        log_w = jax.nn.log_sigmoid(z) + suffix
        att = jnp.where(mask, jnp.exp(log_w), 0.0)
        outs.append(jnp.einsum("bhqk,bhkd->bhqd", att.astype(v.dtype), v_blk))
    return jnp.concatenate(outs, axis=2)


def rwkv7_time_mix(p, mu, w0, w_lora, a0, a_lora, g_lora, k_k, k_a, r_k, lnx_w, lnx_b,
                   v_first, v0, v_lora):
    b, s, _ = p.shape
    f32 = jnp.float32
    p = p.astype(f32)
    prev = jnp.pad(p, ((0, 0), (1, 0), (0, 0)))[:, :-1]
    p = p + mu.astype(f32) * (prev - p)
    w_ = MIX_W
    r, k, v = p[..., :w_], p[..., w_:2 * w_], p[..., 2 * w_:3 * w_]
    o = 3 * w_
    xw = p[..., o:o + DECAY_LORA]; o += DECAY_LORA
    xa = p[..., o:o + ICLR_LORA]; o += ICLR_LORA
    xg = p[..., o:o + GATE_LORA]; o += GATE_LORA
    log_decay = -DECAY_SCALE * jax.nn.sigmoid(w0 + jnp.tanh(xw) @ w_lora)
    a = jax.nn.sigmoid(a0 + xa @ a_lora)
    if v0 is None:
        v_first = v
    else:
        xv = p[..., o:o + VRES_LORA]
        v = v + (v_first - v) * jax.nn.sigmoid(v0 + xv @ v_lora)
    g = jax.nn.sigmoid(xg) @ g_lora

    def heads(t):
        return t.reshape(b, s, RWKV_HEADS, RWKV_HEAD)

    kk = heads(k * k_k)
    kk = kk / jnp.maximum(jnp.linalg.norm(kk, axis=-1, keepdims=True), 1e-12)
    k = k * (1.0 + (a - 1.0) * k_a)
    rh, kh, vh, ah, wh = heads(r), heads(k), heads(v), heads(a), heads(jnp.exp(log_decay))

    def step(state, inp):
        r_t, w_t, k_t, v_t, kk_t, a_t = inp
        sa = jnp.einsum("bhvk,bhk->bhv", state, kk_t)
        state = (state * w_t[:, :, None, :]
                 - sa[..., None] * (kk_t * a_t)[:, :, None, :]
                 + v_t[..., None] * k_t[:, :, None, :])
        return state, jnp.einsum("bhvk,bhk->bhv", state, r_t)

    xs = tuple(jnp.moveaxis(t, 1, 0) for t in (rh, wh, kh, vh, kk, ah))
    state0 = jnp.zeros((b, RWKV_HEADS, RWKV_HEAD, RWKV_HEAD), f32)
    _, out = lax.scan(step, state0, xs)
    out = jnp.moveaxis(out, 0, 1)
    mean = jnp.mean(out, axis=-1, keepdims=True)
    var = jnp.mean(jnp.square(out - mean), axis=-1, keepdims=True)
    out = ((out - mean) * lax.rsqrt(var + GN_EPS)).reshape(b, s, w_) * lnx_w + lnx_b
    bonus = (jnp.sum(rh * kh * r_k, axis=-1, keepdims=True) * vh).reshape(b, s, w_)
    return (out + bonus) * g, v_first


def hybrid_token_mixer(xn, w_comb, conv_w, mu_rwkv, w0, w_lora, a0, a_lora, g_lora, k_k, k_a,
                       r_k, lnx_w, lnx_b, branch_proj, gate_b, w_mix_out, v_first, vres):
    b, s, _ = xn.shape
    mu = mu_rwkv
    v0 = v_lora = None
    if vres is not None:
        w_vres, mu_vres, v0, v_lora = vres
        w_comb = jnp.concatenate([w_comb, w_vres], axis=1)
        mu = jnp.concatenate([mu_rwkv, mu_vres], axis=0)
    proj = xn @ w_comb

    c_b, c_c, c_x = jnp.split(proj[..., :OFF_SB], 3, axis=-1)
    y_conv = c_b * causal_dwconv(c_c * c_x, conv_w)

    q, k, v = jnp.split(proj[..., OFF_SB:OFF_GATE], 3, axis=-1)

    def to_heads(t):
        return t.reshape(b, s, SB_HEADS, SB_HEAD_DIM).transpose(0, 2, 1, 3)

    y_sb = stick_breaking_attention(to_heads(q), to_heads(k), to_heads(v))
    y_sb = y_sb.transpose(0, 2, 1, 3).reshape(b, s, MIX_W)

    y_rwkv, v_first = rwkv7_time_mix(proj[..., OFF_RWKV:], mu, w0, w_lora, a0, a_lora, g_lora,
                                     k_k, k_a, r_k, lnx_w, lnx_b, v_first, v0, v_lora)

    gates = jax.nn.sigmoid(
        proj[..., OFF_GATE:OFF_RWKV].reshape(b, s, N_BRANCH, D_MODEL).astype(jnp.float32)
        + gate_b).astype(xn.dtype)
    ys = jnp.stack([y_conv.astype(xn.dtype), y_sb.astype(xn.dtype), y_rwkv.astype(xn.dtype)],
                   axis=2)
    branches = jnp.einsum("bsnc,ncd->bsnd", ys, branch_proj)
    merged = jnp.einsum("bsnd,bsnd->bsd", gates, branches)
    return merged @ w_mix_out, v_first


def memory_cross_attention(hn, mem_n, wq, wkv, wo):
    b, s, _ = hn.shape
    m = mem_n.shape[1]
    q = (hn @ wq).reshape(b, s, CA_HEADS, CA_HEAD_DIM)
    k, v = jnp.split(mem_n @ wkv, 2, axis=-1)
    k = k.reshape(b, m, CA_HEADS, CA_HEAD_DIM)
    v = v.reshape(b, m, CA_HEADS, CA_HEAD_DIM)
    scores = jnp.einsum("bshd,bmhd->bhsm", q, k).astype(jnp.float32) * (CA_HEAD_DIM ** -0.5)
    probs = jax.nn.softmax(scores, axis=-1).astype(v.dtype)
    out = jnp.einsum("bhsm,bmhd->bshd", probs, v).reshape(b, s, D_MODEL)
    return out @ wo


def conv_ffn(hn, w_up, conv_w, conv_b, w_down):
    u = causal_dwconv(hn @ w_up, conv_w) + conv_b
    gate, val = jnp.split(u, 2, axis=-1)
    return (jax.nn.silu(gate) * val) @ w_down


def setup_inputs(seed: int = 0) -> dict:
    key = jax.random.key(seed)
    ks = iter(jax.random.split(key, 48))

    def nrm(shape, scale):
        return jax.random.normal(next(ks), shape, jnp.float32) * scale

    def gain(shape):
        return 1.0 + nrm(shape, 0.02)

    L, Lv, D, W, F = DEPTH, DEPTH - 1, D_MODEL, MIX_W, FFN_DIM
    return {
        "x": nrm((BATCH, SEQ, D), 1.0),
        "mem": nrm((BATCH, MEM_LEN, D), 1.0),
        "norm_mix": gain((L, D)),
        "w_comb": nrm((L, D, COMB_COLS), D ** -0.5),
        "conv_w": nrm((L, CONV_K, W), 0.5),
        "mu_rwkv": jax.random.uniform(next(ks), (L, RWKV_COLS), jnp.float32),
        "w0": 0.5 + nrm((L, W), 0.5),
        "w_lora": nrm((L, DECAY_LORA, W), 0.1),
        "a0": nrm((L, W), 0.3),
        "a_lora": nrm((L, ICLR_LORA, W), 0.1),
        "g_lora": nrm((L, GATE_LORA, W), GATE_LORA ** -0.5),
        "k_k": 1.0 + nrm((L, W), 0.1),
        "k_a": 1.0 + nrm((L, W), 0.1),
        "r_k": nrm((L, RWKV_HEADS, RWKV_HEAD), 0.1),
        "lnx_w": gain((L, W)),
        "lnx_b": nrm((L, W), 0.02),
        "w_vres": nrm((Lv, D, VRES_LORA), D ** -0.5),
        "mu_vres": jax.random.uniform(next(ks), (Lv, VRES_LORA), jnp.float32),
        "v0": nrm((Lv, W), 0.3),
        "v_lora": nrm((Lv, VRES_LORA, W), 0.1),
        "branch_proj": nrm((L, N_BRANCH, W, D), W ** -0.5),
        "gate_b": nrm((L, N_BRANCH, D), 0.01),
        "w_mix_out": nrm((L, D, D), D ** -0.5),
        "norm_ca": gain((L, D)),
        "norm_mem": gain((L, D)),
        "ca_wq": nrm((L, D, D), D ** -0.5),
        "ca_wkv": nrm((L, D, 2 * D), D ** -0.5),
        "ca_wo": nrm((L, D, D), D ** -0.5),
        "norm_ffn": gain((L, D)),
        "ffn_up": nrm((L, D, 2 * F), D ** -0.5),
        "ffn_conv_w": nrm((L, FFN_CONV_K, 2 * F), 0.5),
        "ffn_conv_b": nrm((L, 2 * F), 0.02),
        "ffn_down": nrm((L, F, D), F ** -0.5),
        "norm_final": gain((D,)),
    }


def reference(x, mem, norm_mix, w_comb, conv_w, mu_rwkv, w0, w_lora, a0, a_lora, g_lora, k_k,
              k_a, r_k, lnx_w, lnx_b, w_vres, mu_vres, v0, v_lora, branch_proj, gate_b,
              w_mix_out, norm_ca, norm_mem, ca_wq, ca_wkv, ca_wo, norm_ffn, ffn_up, ffn_conv_w,
              ffn_conv_b, ffn_down, norm_final):
    h = x
    v_first = None
    for l in range(DEPTH):
        vres = None if l == 0 else (w_vres[l - 1], mu_vres[l - 1], v0[l - 1], v_lora[l - 1])
        mix, v_first = hybrid_token_mixer(
            rms_norm(h, norm_mix[l]), w_comb[l], conv_w[l], mu_rwkv[l], w0[l], w_lora[l], a0[l],
            a_lora[l], g_lora[l], k_k[l], k_a[l], r_k[l], lnx_w[l], lnx_b[l], branch_proj[l],
            gate_b[l], w_mix_out[l], v_first, vres)
        h = h + mix
        h = h + memory_cross_attention(rms_norm(h, norm_ca[l]), rms_norm(mem, norm_mem[l]),
                                       ca_wq[l], ca_wkv[l], ca_wo[l])
        h = h + conv_ffn(rms_norm(h, norm_ffn[l]), ffn_up[l], ffn_conv_w[l], ffn_conv_b[l],
                         ffn_down[l])
    return rms_norm(h, norm_final)
```

```python
import math
from contextlib import ExitStack
import numpy as np
import concourse.bass as bass
import concourse.mybir as mybir
from concourse.bass_utils import run_bass_kernel_spmd

F32 = mybir.dt.float32
BF16 = mybir.dt.bfloat16
AF = mybir.ActivationFunctionType
ALU = mybir.AluOpType
AX = mybir.AxisListType

D = 1024
MEM = 256
W = 512
TT = 512
FFN = 2816
NFC = 44
COMB = 7936
OFF_SB = 1536
OFF_GATE = 3072
OFF_RWKV = 6144
DS = math.exp(-0.5)
CH = 64


class _Op:
    __slots__ = ("eng", "fn", "deps", "epoch", "dma", "need_inc", "count", "sem_key", "idx")


class Sched:
    ENGS = ("pe", "act", "dve", "pool", "sp")
    NQ = 8

    def __init__(self, nc):
        self.nc = nc
        self.ops = []
        self.last_writer = {}
        self.readers = {}
        self.epoch = 0
        self.bar_start = 0

    def add(self, eng, fn, reads=(), writes=(), dma=False):
        op = _Op()
        op.idx = len(self.ops)
        op.eng, op.fn, op.epoch, op.dma = eng, fn, self.epoch, dma
        deps = set()
        for t in reads:
            w = self.last_writer.get(t)
            if w is not None:
                deps.add(w)
        for t in writes:
            deps.update(self.readers.get(t, ()))
            w = self.last_writer.get(t)
            if w is not None:
                deps.add(w)
        for t in reads:
            self.readers.setdefault(t, []).append(op.idx)
        for t in writes:
            self.last_writer[t] = op.idx
            self.readers[t] = []
        deps.discard(op.idx)
        op.deps = deps
        op.need_inc = False
        self.ops.append(op)
        return op.idx

    def pe(self, fn, r=(), w=()):
        return self.add("pe", fn, r, w)

    def act(self, fn, r=(), w=()):
        return self.add("act", fn, r, w)

    def dve(self, fn, r=(), w=()):
        return self.add("dve", fn, r, w)

    def pool(self, fn, r=(), w=()):
        return self.add("pool", fn, r, w)

    def dma(self, eng, fn, r=(), w=()):
        return self.add(eng, fn, r, w, dma=True)

    def barrier(self):
        last = {}
        deps = set()
        for op in self.ops[self.bar_start:]:
            if op.dma:
                deps.add(op.idx)
            else:
                last[op.eng] = op.idx
        deps.update(last.values())
        for eng in self.ENGS:
            i = self.add(eng, None)
            self.ops[i].deps = set(deps)
        self.bar_start = len(self.ops)
        self.last_writer.clear()
        self.readers.clear()

    def emit(self):
        nc = self.nc
        ops = self.ops
        for op in ops:
            for d in op.deps:
                ops[d].need_inc = True
        counters = {}
        dma_counters = {}
        sem_keys = set()
        for op in ops:
            if op.dma:
                j = dma_counters.get(op.eng, 0)
                dma_counters[op.eng] = j + 1
                op.sem_key = ("dma", op.eng, j % self.NQ)
                op.count = 16 * (j // self.NQ + 1)
                sem_keys.add(op.sem_key)
            else:
                key = ("c", op.eng, op.epoch)
                op.sem_key = key
                if op.need_inc and op.fn is not None:
                    counters[key] = counters.get(key, 0) + 1
                    sem_keys.add(key)
                op.count = counters.get(key, 0)
        final_waits = {}
        for op in ops:
            if op.dma:
                k = op.sem_key
                final_waits[k] = max(final_waits.get(k, 0), op.count)
        self.n_sems = len(sem_keys)
        with ExitStack() as es:
            sems = {}
            for i, k in enumerate(sorted(sem_keys, key=str)):
                sems[k] = es.enter_context(nc.semaphore("s%d" % i))
            block = es.enter_context(nc.Block())

            def run(engname, e):
                waited = {}
                for op in ops:
                    if op.eng != engname:
                        continue
                    need = {}
                    for d in op.deps:
                        dop = ops[d]
                        if dop.count <= 0:
                            continue
                        k = dop.sem_key
                        if need.get(k, 0) < dop.count:
                            need[k] = dop.count
                    if op.dma:
                        prev = op.count - 16
                        if prev > 0 and need.get(op.sem_key, 0) < prev:
                            need[op.sem_key] = prev
                    for k in sorted(need, key=str):
                        v = need[k]
                        if waited.get(k, 0) >= v:
                            continue
                        e.wait_ge(sems[k], v)
                        waited[k] = v
                    if op.fn is None:
                        continue
                    if op.dma:
                        op.fn().then_inc(sems[op.sem_key], 16)
                    else:
                        inst = op.fn()
                        if op.need_inc:
                            inst.then_inc(sems[op.sem_key], 1)
                if engname == "sp":
                    for k in sorted(final_waits, key=str):
                        v = final_waits[k]
                        if waited.get(k, 0) < v:
                            e.wait_ge(sems[k], v)

            @block.tensor
            def _(e):
                run("pe", e)

            @block.scalar
            def _(e):
                run("act", e)

            @block.vector
            def _(e):
                run("dve", e)

            @block.gpsimd
            def _(e):
                run("pool", e)

            @block.sync
            def _(e):
                run("sp", e)


class Arena:
    def __init__(self, tensor, nf32):
        self.t = tensor
        self.n = nf32
        self.off = 0
        self.peak = 0
        self.uid = 0

    def reset(self):
        self.off = 0

    def alloc(self, free, dt, parts=128):
        n = 1
        for f in free:
            n *= f
        nf = n if dt == F32 else (n + 1) // 2
        nf = (nf + 7) // 8 * 8
        assert self.off + nf <= self.n, ("arena overflow", self.off, nf, self.n)
        ap = self.t[0:parts, self.off:self.off + nf]
        self.off += nf
        self.peak = max(self.peak, self.off)
        if dt != F32:
            ap = ap.bitcast(dt)
        ap = ap[:, 0:n]
        if len(free) == 2:
            ap = ap.rearrange("p (a b) -> p a b", b=free[1])
        elif len(free) == 3:
            ap = ap.rearrange("p (a b c) -> p a b c", b=free[1], c=free[2])
        self.uid += 1
        return ap, "ar%d" % self.uid


def _cvec_layout(depth):
    off = {}
    n = 0

    def put(name, cols):
        nonlocal n
        off[name] = n
        n += cols

    for l in range(depth):
        for nm in ("norm_mix", "norm_ca", "norm_mem", "norm_ffn"):
            put((nm, l), 8)
        put(("conv_w", l), 12)
        put(("gate_b", l), 24)
        put(("mu", l), 14)
        for nm in ("w0", "a0", "k_k", "k_a", "r_k", "lnx_w", "lnx_b"):
            put((nm, l), 4)
        put(("ffn_cw", l), 3 * NFC)
        put(("ffn_cb", l), NFC)
    put(("norm_final", 0), 8)
    put(("v0", 1), 4)
    put(("mu_vres", 1), 1)
    return off, n


def _col(v, p=128):
    v = np.asarray(v, np.float32).reshape(-1)
    return np.ascontiguousarray(v.reshape(-1, p).T)


def _build_cvec(inp, depth):
    off, n = _cvec_layout(depth)
    cv = np.zeros((128, n), np.float32)

    def put(key, arr):
        cv[:arr.shape[0], off[key]:off[key] + arr.shape[1]] = arr

    for l in range(depth):
        for nm in ("norm_mix", "norm_ca", "norm_mem", "norm_ffn"):
            put((nm, l), _col(inp[nm][l]))
        put(("conv_w", l), np.concatenate([_col(inp["conv_w"][l][t]) for t in range(3)], axis=1))
        put(("gate_b", l), np.concatenate([_col(inp["gate_b"][l][k]) for k in range(3)], axis=1))
        put(("mu", l), _col(inp["mu_rwkv"][l]))
        for nm in ("w0", "a0", "k_k", "k_a", "lnx_w", "lnx_b"):
            put((nm, l), _col(inp[nm][l]))
        put(("r_k", l), _col(inp["r_k"][l].reshape(-1)))
        put(("ffn_cw", l), np.concatenate([_col(inp["ffn_conv_w"][l][t]) for t in range(3)], axis=1))
        put(("ffn_cb", l), _col(inp["ffn_conv_b"][l]))
    put(("norm_final", 0), _col(inp["norm_final"]))
    if depth > 1:
        put(("v0", 1), _col(inp["v0"][0]))
        put(("mu_vres", 1), np.asarray(inp["mu_vres"][0], np.float32).reshape(32, 1))
    return cv


def _build_consts():
    c = {}
    c["ident"] = np.eye(128, dtype=np.float32)
    s = np.arange(128)
    c["uincl"] = (s[:, None] >= s[None, :]).astype(np.float32)
    c["lstr"] = (s[:, None] < s[None, :]).astype(np.float32)
    c["ones"] = np.ones((128, 128), np.float32)
    blk = np.zeros((128, 128), np.float32)
    blk[:64, :64] = 1
    blk[64:, 64:] = 1
    c["blk"] = blk
    m = np.where(s[:, None] < s[None, :], 0.0, -1.0e5).astype(np.float32)
    c["sbmask"] = np.tile(m, (1, 4))
    u = np.arange(64)
    st = (u[:, None] < u[None, :]).astype(np.float32)
    inc = (u[:, None] <= u[None, :]).astype(np.float32)
    lo = (u[:, None] > u[None, :]).astype(np.float32)
    rm = np.zeros((128, 2048), np.float32)
    rm[:64, 0:512] = np.tile(st, (1, 8))
    rm[:64, 512:1024] = np.tile(inc, (1, 8))
    rm[:64, 1024:1536] = np.tile(lo, (1, 8))
    rm[:64, 1536:2048] = np.tile(np.eye(64, dtype=np.float32), (1, 8))
    c["rmask"] = rm
    return c


def build_program(T=4096, depth=2, use_sb=True, use_rwkv=True, debug=False):
    NT = T // TT
    nc = bass.Bass("TRN2", target_bir_lowering=False)
    cv_off, cv_n = _cvec_layout(depth)

    def din(name, shape, dt=F32):
        return nc.dram_tensor(name, list(shape), dt, kind="ExternalInput").ap()

    def dscr(name, shape, dt):
        return nc.dram_tensor(name, list(shape), dt, kind="Internal").ap()

    xT = din("xT", [D, T])
    memT = din("memT", [D, MEM])
    cvec_d = din("cvec", [128, cv_n])
    w_comb = din("w_comb", [depth, D, COMB])
    w_vres = din("w_vres", [1, D, 32])
    lora_wa = din("lora_wa", [depth, 128, W])
    g_lora = din("g_lora", [depth, 128, W])
    v_lora = din("v_lora", [1, 32, W])
    branch_proj = din("branch_proj", [depth, 3, W, D])
    w_mix_out = din("w_mix_out", [depth, D, D])
    ca_wq = din("ca_wq", [depth, D, D])
    ca_wkv = din("ca_wkv", [depth, D, 2 * D])
    ca_wo = din("ca_wo", [depth, D, D])
    ffn_up = din("ffn_up", [depth, D, 2 * FFN])
    ffn_down = din("ffn_down", [depth, FFN, D])
    c_ident = din("c_ident", [128, 128])
    c_uincl = din("c_uincl", [128, 128])
    c_lstr = din("c_lstr", [128, 128])
    c_ones = din("c_ones", [128, 128])
    c_blk = din("c_blk", [128, 128])
    c_sbmask = din("c_sbmask", [128, 512])
    c_rmask = din("c_rmask", [128, 2048])
    outT = nc.dram_tensor("outT", [D, T], F32, kind="ExternalOutput").ap()
    dbg = None
    if debug:
        dbg = nc.dram_tensor("dbg", [8, D, T], F32, kind="ExternalOutput").ap()

    kcache = dscr("kcache", [depth, 128, 4, T], BF16)
    vcache = dscr("vcache", [depth, T // 128, 128, W], BF16)
    memK_d = dscr("memK", [depth, 128, 8, MEM], BF16)
    memV_d = dscr("memV", [depth, 128, 2, D], BF16)

    with ExitStack() as es:
        def sb(name, shape, dt, ):
            return es.enter_context(nc.sbuf_tensor(name + "_s", shape, dt))

        S = Sched(nc)
        PS = es.enter_context(nc.psum_tensor("PS", [128, 4096], F32))
        PSt = ["ps%d" % i for i in range(8)]

        def bank(i, parts=128, lo=0, hi=512):
            return PS[0:parts, i * 512 + lo:i * 512 + hi]

        hT = sb("hT", [128, 8, TT], F32)
        xn = sb("xn", [128, 8, TT], BF16)
        merged = sb("merged", [128, 8, TT], F32)
        mergedb = sb("mergedb", [128, 8, TT], BF16)
        NWB = 3
        wbuf = [sb("wbuf%d" % i, [128, 8 * 512], BF16) for i in range(NWB)]
        cvec = sb("cvec", [128, cv_n], F32)
        ident = sb("ident", [128, 128], F32)
        uincl = sb("uincl", [128, 128], BF16)
        lstr = sb("lstr", [128, 128], BF16)
        onesb = sb("onesb", [128, 128], BF16)
        blk = sb("blk", [128, 128], F32)
        sbmask = sb("sbmask", [128, 512], F32)
        rmask = sb("rmask", [128, 2048], F32)
        conv_carry = sb("conv_carry", [128, depth * 4 * 2], F32)
        ffn_carry = sb("ffn_carry", [128, depth * NFC * 2], F32)
        rw_carry = sb("rw_carry", [128, depth * 16], F32)
        rw_state = sb("rw_state", [128, depth * 4 * 64], F32)
        vfirst = sb("vfirst", [128, 4, TT], F32)
        omka = sb("omka", [128, depth * 4], F32)
        ones64 = sb("ones64", [128, 64], F32)
        identb = sb("identb", [128, 128], BF16)
        rw_sbf = sb("rw_sbf", [128, depth * 4 * 64], BF16)
        ARENA_F32 = 26 * 1024
        scr = sb("scr", [128, ARENA_F32], F32)
        ar = Arena(scr, ARENA_F32)

        def cvc(key, j=0, n=1, parts=128):
            o = cv_off[key] + j
            return cvec[0:parts, o:o + n]

        S.dma("sp", lambda: nc.sync.dma_start(out=cvec[:], in_=cvec_d), w=["cvec"])
        S.dma("sp", lambda: nc.sync.dma_start(out=ident[:], in_=c_ident), w=["ident"])
        S.dma("sp", lambda: nc.sync.dma_start(out=blk[:], in_=c_blk), w=["blk"])
        S.dma("sp", lambda: nc.sync.dma_start(out=sbmask[:], in_=c_sbmask), w=["sbmask"])
        S.dma("sp", lambda: nc.sync.dma_start(out=rmask[:], in_=c_rmask), w=["rmask"])
        S.dma("pool", lambda: nc.gpsimd.dma_start(out=uincl[:], in_=c_uincl), w=["uincl"])
        S.dma("pool", lambda: nc.gpsimd.dma_start(out=lstr[:], in_=c_lstr), w=["lstr"])
        S.dma("pool", lambda: nc.gpsimd.dma_start(out=onesb[:], in_=c_ones), w=["onesb"])
        S.dve(lambda: nc.vector.memset(conv_carry[:], 0.0), w=["conv_carry"])
        S.dve(lambda: nc.vector.memset(ffn_carry[:], 0.0), w=["ffn_carry"])
        S.dve(lambda: nc.vector.memset(rw_carry[:], 0.0), w=["rw_carry"])
        S.dve(lambda: nc.vector.memset(rw_state[:], 0.0), w=["rw_state"])
        S.dve(lambda: nc.vector.memset(ones64[:], 1.0), w=["ones64"])
        S.dve(lambda: nc.vector.memset(rw_sbf[:], 0.0), w=["rw_sbf"])
        S.dma("pool", lambda: nc.gpsimd.dma_start(out=identb[:], in_=c_ident), w=["identb"])
        for l in range(depth):
            S.dve(lambda l=l: nc.vector.tensor_scalar(out=omka[:, l * 4:l * 4 + 4], in0=cvc(("k_a", l), 0, 4),
                                                      scalar1=-1.0, scalar2=1.0, op0=ALU.mult, op1=ALU.add),
                  r=["cvec"], w=["omka"])

        wstate = {"i": 0}

        def load_w(src_ap, view):
            i = wstate["i"] % NWB
            wstate["i"] += 1
            a, b = src_ap.shape[1], src_ap.shape[2]
            dst = wbuf[i][:, 0:a * b].rearrange("p (a b) -> p a b", b=b)
            tok = "wbuf%d" % i
            S.dma("pool", lambda: nc.gpsimd.dma_start(out=dst, in_=src_ap), w=[tok])
            return dst, tok

        def slab(wd, col0, ncols=512):
            return load_w(wd[:, col0:col0 + ncols].rearrange("(c p) n -> p c n", p=128), None)

        def proj_fm(out_ps, sl, tok, c0, m, ptok, rhs=None, rtok="xn", n=TT, nk=8):
            rhs = xn if rhs is None else rhs

            def f():
                ins = None
                for c in range(nk):
                    ins = nc.tensor.matmul(out_ps, lhsT=sl[:, c, c0:c0 + m], rhs=rhs[:, c, 0:n],
                                           start=(c == 0), stop=(c == nk - 1))
                return ins
            S.pe(f, r=[tok, rtok], w=[ptok])

        def rmsnorm(src, stok, gkey, dst, dtok, n=TT, pb=7):
            sq, sqt = ar.alloc((8, n), BF16)
            rstd, rt = ar.alloc((n,), F32)
            S.act(lambda: nc.scalar.activation(out=sq, in_=src[:, :, 0:n], func=AF.Square), r=[stok], w=[sqt])

            def f():
                ins = None
                for c in range(8):
                    ins = nc.tensor.matmul(bank(pb, hi=n), lhsT=onesb[:, :], rhs=sq[:, c, :], start=(c == 0), stop=(c == 7))
                return ins
            S.pe(f, r=[sqt, "onesb"], w=[PSt[pb]])
            S.act(lambda: nc.scalar.activation(out=rstd, in_=bank(pb, hi=n), func=AF.Ln, scale=1.0 / D, bias=1e-6),
                  r=[PSt[pb]], w=[rt])
            S.act(lambda: nc.scalar.activation(out=rstd, in_=rstd, func=AF.Exp, scale=-0.5), r=[rt], w=[rt])
            for c in range(8):
                S.dve(lambda c=c: nc.vector.scalar_tensor_tensor(out=dst[:, c, 0:n], in0=src[:, c, 0:n], scalar=cvc(gkey, c),
                                                                 in1=rstd, op0=ALU.mult, op1=ALU.mult),
                      r=[stok, rt, "cvec"], w=[dtok])

        def dbg_out(slot, src, stok, i, nchunk=8):
            if dbg is None:
                return
            S.dma("sp", lambda: nc.sync.dma_start(
                out=dbg[slot, 0:nchunk * 128, i * TT:(i + 1) * TT].rearrange("(c p) t -> p c t", p=128), in_=src),
                r=[stok])

        def mem_kv(l):
            ar.reset()
            mt_, mtt = ar.alloc((8, MEM), F32)
            mn, mnt = ar.alloc((8, MEM), BF16)
            S.dma("sp", lambda: nc.sync.dma_start(out=mt_, in_=memT.rearrange("(c p) m -> p c m", p=128)), w=[mtt])
            rmsnorm(mt_, mtt, ("norm_mem", l), mn, mnt, n=MEM)
            kk_, kkt = ar.alloc((8, MEM), BF16)
            vv_, vvt = ar.alloc((2, D), BF16)
            for half in range(2):
                sl, tok = slab(ca_wkv[l], half * 512)
                for dcl in range(4):
                    dc = half * 4 + dcl
                    pb = dcl % 4
                    proj_fm(bank(pb, hi=MEM), sl, tok, dcl * 128, 128, PSt[pb], rhs=mn, rtok=mnt, n=MEM)
                    S.act(lambda dc=dc, pb=pb: nc.scalar.copy(out=kk_[:, dc, :], in_=bank(pb, hi=MEM)), r=[PSt[pb]], w=[kkt])
            for half in range(2):
                sl, tok = slab(ca_wkv[l], D + half * 512)
                for mt in range(2):
                    pb = 4 + mt

                    def f(sl=sl, mt=mt, pb=pb):
                        ins = None
                        for c in range(8):
                            ins = nc.tensor.matmul(bank(pb), lhsT=mn[:, c, mt * 128:(mt + 1) * 128], rhs=sl[:, c, :],
                                                   start=(c == 0), stop=(c == 7))
                        return ins
                    S.pe(f, r=[tok, mnt], w=[PSt[pb]])
                    S.act(lambda mt=mt, half=half, pb=pb: nc.scalar.copy(out=vv_[:, mt, half * 512:(half + 1) * 512], in_=bank(pb)),
                          r=[PSt[pb]], w=[vvt])
            S.dma("sp", lambda l=l: nc.sync.dma_start(out=memK_d[l], in_=kk_), r=[kkt], w=["memK%d" % l])
            S.dma("sp", lambda l=l: nc.sync.dma_start(out=memV_d[l], in_=vv_), r=[vvt], w=["memV%d" % l])
            S.barrier()

        for l in range(depth):
            mem_kv(l)

        def merge_branch(l, n, ys, ystok):
            bp, bptok = load_w(branch_proj[l, n].rearrange("(c p) n -> p c n", p=128), None)
            gsb = [ar.alloc((TT,), F32) for _ in range(2)]
            for half in range(2):
                sl, tok = slab(w_comb[l], OFF_GATE + n * D + half * 512)
                for dcl in range(4):
                    dc = half * 4 + dcl
                    pg = dc % 2
                    pbr = 2 + dc % 2
                    proj_fm(bank(pg), sl, tok, dcl * 128, 128, PSt[pg])
                    gs, gst = gsb[dc % 2]
                    S.act(lambda dc=dc, pg=pg, gs=gs: nc.scalar.activation(out=gs, in_=bank(pg), func=AF.Sigmoid,
                                                                          bias=cvc(("gate_b", l), n * 8 + dc)),
                          r=[PSt[pg], "cvec"], w=[gst])
                    proj_fm(bank(pbr), bp, bptok, dc * 128, 128, PSt[pbr], rhs=ys, rtok=ystok, nk=4)
                    if n == 0:
                        S.dve(lambda dc=dc, pbr=pbr, gs=gs: nc.vector.tensor_tensor(out=merged[:, dc, :], in0=bank(pbr), in1=gs, op=ALU.mult),
                              r=[PSt[pbr], gst], w=["merged%d" % dc])
                    else:
                        S.dve(lambda dc=dc, pbr=pbr, gs=gs: nc.vector.tensor_tensor(out=gs, in0=bank(pbr), in1=gs, op=ALU.mult),
                              r=[PSt[pbr], gst], w=[gst])
                        dst = mergedb if n == 2 else merged
                        dtk = ("mergedb%d" if n == 2 else "merged%d") % dc
                        S.dve(lambda dc=dc, gs=gs, dst=dst: nc.vector.tensor_tensor(out=dst[:, dc, :], in0=merged[:, dc, :], in1=gs, op=ALU.add),
                               r=[gst, "merged%d" % dc], w=[dtk])

        def out_proj_residual(wd, act, atok, nk=8):
            for half in range(2):
                sl, tok = slab(wd, half * 512)
                for dcl in range(4):
                    dc = half * 4 + dcl
                    pb = 4 + dc % 2
                    proj_fm(bank(pb), sl, tok, dcl * 128, 128, PSt[pb], rhs=act, rtok=atok, nk=nk)
                    S.dve(lambda dc=dc, pb=pb: nc.vector.tensor_tensor(out=hT[:, dc, :], in0=hT[:, dc, :], in1=bank(pb), op=ALU.add),
                          r=[PSt[pb], "hT"], w=["hT"])

        def layer_tile(i, l):
            if True:
                S.epoch = 1 + i * depth + l
                ar.reset()
                rmsnorm(hT, "hT", ("norm_mix", l), xn, "xn")
                slb, tb = slab(w_comb[l], 0)
                slc, tcx = slab(w_comb[l], 512)
                slx, tx = slab(w_comb[l], 1024)
                yconv, yct = ar.alloc((4, TT), BF16)
                for j in range(4):
                    proj_fm(bank(0), slb, tb, j * 128, 128, PSt[0])
                    proj_fm(bank(1), slc, tcx, j * 128, 128, PSt[1])
                    proj_fm(bank(2), slx, tx, j * 128, 128, PSt[2])
                    cxb, cxt = ar.alloc((TT + 2,), F32)
                    tmp, tmt = ar.alloc((TT,), F32)
                    cc = conv_carry[:, (l * 4 + j) * 2:(l * 4 + j) * 2 + 2]
                    S.dve(lambda cxb=cxb, cc=cc: nc.vector.tensor_copy(out=cxb[:, 0:2], in_=cc), r=["conv_carry%d_%d" % (l, j)], w=[cxt])
                    S.act(lambda tmp=tmp: nc.scalar.copy(out=tmp, in_=bank(1)), r=[PSt[1]], w=[tmt])
                    S.dve(lambda cxb=cxb, tmp=tmp: nc.vector.tensor_tensor(out=cxb[:, 2:TT + 2], in0=tmp, in1=bank(2), op=ALU.mult),
                          r=[tmt, PSt[2], cxt], w=[cxt])
                    S.dve(lambda cxb=cxb, cc=cc: nc.vector.tensor_copy(out=cc, in_=cxb[:, TT:TT + 2]), r=[cxt], w=["conv_carry%d_%d" % (l, j)])
                    S.dve(lambda cxb=cxb, tmp=tmp, j=j: nc.vector.tensor_scalar(out=tmp, in0=cxb[:, 0:TT], scalar1=cvc(("conv_w", l), j), scalar2=None, op0=ALU.mult),
                          r=[cxt, tmt, "cvec"], w=[tmt])
                    for tap in (1, 2):
                        S.dve(lambda cxb=cxb, tmp=tmp, j=j, tap=tap: nc.vector.scalar_tensor_tensor(
                            out=tmp, in0=cxb[:, tap:tap + TT], scalar=cvc(("conv_w", l), tap * 4 + j), in1=tmp, op0=ALU.mult, op1=ALU.add),
                            r=[cxt, tmt, "cvec"], w=[tmt])
                    S.dve(lambda tmp=tmp, j=j: nc.vector.tensor_tensor(out=yconv[:, j, :], in0=tmp, in1=bank(0), op=ALU.mult),
                          r=[tmt, PSt[0]], w=[yct])
                merge_branch(l, 0, yconv, yct)
                S.barrier()

                ar.reset()
                ysb, ysbt = ar.alloc((4, TT), BF16)
                if use_sb:
                    sb_attention(nc, S, ar, bank, PSt, PS, l, i, xn, w_comb, kcache, vcache, ysb, ysbt, slab, proj_fm,
                                 uincl, lstr, sbmask)
                else:
                    S.dve(lambda: nc.vector.memset(ysb, 0.0), w=[ysbt])
                merge_branch(l, 1, ysb, ysbt)
                S.barrier()

                ar.reset()
                yrw, yrwt = ar.alloc((4, TT), BF16)
                if use_rwkv:
                    rwkv_branch(nc, S, ar, bank, PSt, PS, l, i, xn, w_comb, w_vres, lora_wa, g_lora, v_lora, yrw, yrwt, slab, load_w,
                                proj_fm, cvc, dict(ident=ident, blk=blk, rmask=rmask, rw_carry=rw_carry, rw_state=rw_state,
                                                   vfirst=vfirst, omka=omka, ones64=ones64, identb=identb, rw_sbf=rw_sbf))
                else:
                    S.dve(lambda: nc.vector.memset(yrw, 0.0), w=[yrwt])
                merge_branch(l, 2, yrw, yrwt)
                S.barrier()
                ar.reset()
                out_proj_residual(w_mix_out[l], mergedb, "mergedb_all")
                S.barrier()

                ar.reset()
                rmsnorm(hT, "hT", ("norm_ca", l), xn, "xn")
                qca, qcat = ar.alloc((8, TT), BF16)
                mK, mKt = ar.alloc((8, MEM), BF16)
                mV, mVt = ar.alloc((2, D), BF16)
                oT, oTt = ar.alloc((8, TT), BF16)
                S.dma("sp", lambda l=l: nc.sync.dma_start(out=mK, in_=memK_d[l]), r=["memK%d" % l], w=[mKt])
                S.dma("sp", lambda l=l: nc.sync.dma_start(out=mV, in_=memV_d[l]), r=["memV%d" % l], w=[mVt])
                for half in range(2):
                    sl, tok = slab(ca_wq[l], half * 512)
                    for dcl in range(4):
                        dc = half * 4 + dcl
                        pb = dc % 2
                        proj_fm(bank(pb), sl, tok, dcl * 128, 128, PSt[pb])
                        S.act(lambda dc=dc, pb=pb: nc.scalar.copy(out=qca[:, dc, :], in_=bank(pb)), r=[PSt[pb]], w=[qcat])
                for hh in range(4):
                    ee, eet = ar.alloc((2, TT), BF16)
                    for mt in range(2):
                        pb = 2 + mt

                        def f(hh=hh, mt=mt, pb=pb):
                            ins = None
                            for cl in range(2):
                                c = 2 * hh + cl
                                ins = nc.tensor.matmul(bank(pb), lhsT=mK[:, c, mt * 128:(mt + 1) * 128], rhs=qca[:, c, :],
                                                       start=(cl == 0), stop=(cl == 1))
                            return ins
                        S.pe(f, r=[mKt, qcat], w=[PSt[pb]])
                        S.act(lambda ee=ee, mt=mt, pb=pb: nc.scalar.activation(out=ee[:, mt, :], in_=bank(pb), func=AF.Exp, scale=1.0 / 16.0),
                              r=[PSt[pb]], w=[eet])

                    def fs(ee=ee):
                        nc.tensor.matmul(bank(4), lhsT=onesb[:, :], rhs=ee[:, 0, :], start=True, stop=False)
                        return nc.tensor.matmul(bank(4), lhsT=onesb[:, :], rhs=ee[:, 1, :], start=False, stop=True)
                    S.pe(fs, r=[eet, "onesb"], w=[PSt[4]])
                    rs, rst = ar.alloc((TT,), F32)
                    S.dve(lambda rs=rs: nc.vector.reciprocal(out=rs, in_=bank(4)), r=[PSt[4]], w=[rst])
                    for dv in range(2):
                        pb = 5 + dv

                        def fo(hh=hh, dv=dv, pb=pb, ee=ee):
                            c0 = hh * 256 + dv * 128
                            nc.tensor.matmul(bank(pb), lhsT=mV[:, 0, c0:c0 + 128], rhs=ee[:, 0, :], start=True, stop=False)
                            return nc.tensor.matmul(bank(pb), lhsT=mV[:, 1, c0:c0 + 128], rhs=ee[:, 1, :], start=False, stop=True)
                        S.pe(fo, r=[mVt, eet], w=[PSt[pb]])
                        S.dve(lambda hh=hh, dv=dv, pb=pb, rs=rs: nc.vector.tensor_tensor(out=oT[:, 2 * hh + dv, :], in0=bank(pb), in1=rs, op=ALU.mult),
                              r=[PSt[pb], rst], w=[oTt])
                out_proj_residual(ca_wo[l], oT, oTt)
                S.barrier()

                ar.reset()
                rmsnorm(hT, "hT", ("norm_ffn", l), xn, "xn")
                sg, sgt0 = ar.alloc((22, TT), BF16)
                NFB = 4
                bufs = [ar.alloc((TT + 2,), F32) for _ in range(NFB)]
                us = [ar.alloc((TT,), F32) for _ in range(NFB)]
                for s11 in range(11):
                    sl, tok = slab(ffn_up[l], s11 * 512)
                    for k4 in range(4):
                        cc_ = s11 * 4 + k4
                        pb = cc_ % 4
                        buf, bft = bufs[cc_ % NFB]
                        u, ut = us[cc_ % NFB]
                        car = ffn_carry[:, (l * NFC + cc_) * 2:(l * NFC + cc_) * 2 + 2]
                        proj_fm(bank(pb), sl, tok, k4 * 128, 128, PSt[pb])
                        S.dve(lambda buf=buf, car=car: nc.vector.tensor_copy(out=buf[:, 0:2], in_=car), r=["ffn_carry%d_%d" % (l, cc_)], w=[bft])
                        S.act(lambda buf=buf, pb=pb: nc.scalar.copy(out=buf[:, 2:TT + 2], in_=bank(pb)), r=[PSt[pb], bft], w=[bft])
                        S.dve(lambda buf=buf, car=car: nc.vector.tensor_copy(out=car, in_=buf[:, TT:TT + 2]), r=[bft], w=["ffn_carry%d_%d" % (l, cc_)])
                        S.dve(lambda buf=buf, u=u, cc_=cc_: nc.vector.tensor_scalar(
                            out=u, in0=buf[:, 0:TT], scalar1=cvc(("ffn_cw", l), cc_), scalar2=cvc(("ffn_cb", l), cc_), op0=ALU.mult, op1=ALU.add),
                            r=[bft, "cvec"], w=[ut])
                        for tap in (1, 2):
                            S.dve(lambda buf=buf, u=u, cc_=cc_, tap=tap: nc.vector.scalar_tensor_tensor(
                                out=u, in0=buf[:, tap:tap + TT], scalar=cvc(("ffn_cw", l), tap * NFC + cc_), in1=u, op0=ALU.mult, op1=ALU.add),
                                r=[bft, ut, "cvec"], w=[ut])
                        if cc_ < 22:
                            S.act(lambda u=u, cc_=cc_: nc.scalar.activation(out=sg[:, cc_, :], in_=u, func=AF.Silu), r=[ut], w=["sg%d" % cc_])
                        else:
                            jj = cc_ - 22
                            S.dve(lambda u=u, jj=jj: nc.vector.tensor_tensor(out=sg[:, jj, :], in0=sg[:, jj, :], in1=u, op=ALU.mult),
                                   r=[ut, "sg%d" % jj], w=["sg%d" % jj])
                S.barrier()
                for dc in range(8):
                    sl, tok = load_w(ffn_down[l][:, dc * 128:(dc + 1) * 128].rearrange("(c p) n -> p c n", p=128), None)
                    pb = 4 + dc % 2
                    proj_fm(bank(pb), sl, tok, 0, 128, PSt[pb], rhs=sg, rtok="sg_all", nk=22)
                    S.dve(lambda dc=dc, pb=pb: nc.vector.tensor_tensor(out=hT[:, dc, :], in0=hT[:, dc, :], in1=bank(pb), op=ALU.add),
                          r=[PSt[pb], "hT"], w=["hT"])
                S.barrier()
        def tile_begin(i):
            tsl = slice(i * TT, (i + 1) * TT)
            S.dma("sp", lambda: nc.sync.dma_start(out=hT[:], in_=xT[:, tsl].rearrange("(c p) t -> p c t", p=128)), w=["hT"])

        def tile_end(i):
            tsl = slice(i * TT, (i + 1) * TT)
            ar.reset()
            of, oft = ar.alloc((8, TT), F32)
            rmsnorm(hT, "hT", ("norm_final", 0), of, oft)
            S.dma("sp", lambda: nc.sync.dma_start(out=outT[:, tsl].rearrange("(c p) t -> p c t", p=128), in_=of),
                  r=[oft])
            S.barrier()

        for i in range(NT):
            tile_begin(i)
            for l in range(depth):
                layer_tile(i, l)
            tile_end(i)
        S.emit()
        nc._arena_peak = ar.peak
    return nc


def sb_attention(nc, S, ar, bank, PSt, PS, l, i, xn, w_comb, kcache, vcache, ysb, ysbt, slab, proj_fm, uincl, lstr, sbmask):
    qT, qTt = ar.alloc((4, TT), BF16)
    kT, kTt = ar.alloc((4, TT), BF16)
    vt, vtt = ar.alloc((4, W), BF16)
    slq, tq = slab(w_comb[l], OFF_SB)
    slk, tk = slab(w_comb[l], OFF_SB + 512)
    slv, tv = slab(w_comb[l], OFF_SB + 1024)
    for j in range(4):
        proj_fm(bank(j % 2), slq, tq, j * 128, 128, PSt[j % 2])
        S.act(lambda j=j: nc.scalar.copy(out=qT[:, j, :], in_=bank(j % 2)), r=[PSt[j % 2]], w=[qTt])
        proj_fm(bank(2 + j % 2), slk, tk, j * 128, 128, PSt[2 + j % 2])
        S.act(lambda j=j: nc.scalar.copy(out=kT[:, j, :], in_=bank(2 + j % 2)), r=[PSt[2 + j % 2]], w=[kTt])
    for sub in range(4):
        pb = 4 + sub % 2

        def f(sub=sub, pb=pb):
            ins = None
            for c in range(8):
                ins = nc.tensor.matmul(bank(pb), lhsT=xn[:, c, sub * 128:(sub + 1) * 128], rhs=slv[:, c, :], start=(c == 0), stop=(c == 7))
            return ins
        S.pe(f, r=[tv, "xn"], w=[PSt[pb]])
        S.act(lambda sub=sub, pb=pb: nc.scalar.copy(out=vt[:, sub, :], in_=bank(pb)), r=[PSt[pb]], w=[vtt])
    if True:
        S.dma("sp", lambda: nc.sync.dma_start(out=kcache[l, :, :, i * TT:(i + 1) * TT], in_=kT), r=[kTt], w=["kcache"])
        S.dma("sp", lambda: nc.sync.dma_start(out=vcache[l, 4 * i:4 * i + 4].rearrange("s p w -> p s w"), in_=vt), r=[vtt], w=["vcache"])
    NKV = 3
    kst = [ar.alloc((4, 128), BF16) for _ in range(NKV)]
    vst = [ar.alloc((W,), BF16) for _ in range(NKV)]
    zm = [ar.alloc((512,), F32) for _ in range(2)]
    NE = 3
    LA = 2
    Eb = [[ar.alloc((512,), F32) for g in range(2)] for _ in range(NE)]
    SPb = [[ar.alloc((512,), BF16) for g in range(2)] for _ in range(NE)]
    Xb = [ar.alloc((512,), F32) for _ in range(2)]
    attb = [[ar.alloc((512,), BF16) for g in range(2)] for _ in range(2)]
    kvi = [0]
    for qb in range(4):
        I = 4 * i + qb
        qs = slice(qb * 128, (qb + 1) * 128)
        steps = list(range(I, -1, -1))
        n = len(steps)
        info = {}

        def SZ(s_, I=I, qs=qs, steps=steps, info=info):
            c = steps[s_]
            if c >= 4 * i:
                cl = c - 4 * i
                Kc = kT[:, :, cl * 128:(cl + 1) * 128]
                Vc = vt[:, cl, :]
                kt_, vt_ = kTt, vtt
            else:
                (Kc, kt_), (Vc, vt_) = kst[kvi[0] % NKV], vst[kvi[0] % NKV]
                kvi[0] += 1
                S.dma("sp", lambda Kc=Kc, c=c: nc.sync.dma_start(out=Kc, in_=kcache[l, :, :, c * 128:(c + 1) * 128]), r=["kcache"], w=[kt_])
                S.dma("sp", lambda Vc=Vc, c=c: nc.sync.dma_start(out=Vc, in_=vcache[l, c]), r=["vcache"], w=[vt_])
            for g in range(2):
                ztoks = [PSt[2 * g], PSt[2 * g + 1]]

                def fz(g=g, Kc=Kc, qs=qs):
                    ins = None
                    for e in range(2):
                        for jl in range(2):
                            j = 2 * g + jl
                            lo = jl * 128
                            ins = nc.tensor.matmul(bank(2 * g + e, lo=lo, hi=lo + 128),
                                                   lhsT=Kc[e * 64:(e + 1) * 64, j, :], rhs=qT[e * 64:(e + 1) * 64, j, qs],
                                                   start=True, stop=True)
                    return ins
                S.pe(fz, r=[kt_, qTt], w=ztoks)
                zsrc = PS[:, 2 * g * 512:(2 * g + 2) * 512].rearrange("p (b x) -> p b x", b=2)[:, :, 0:256]
                E, Et = Eb[s_ % NE][g]
                SP, SPt = SPb[s_ % NE][g]
                E3 = E.rearrange("p (b x) -> p b x", b=2)
                if c == I:
                    z, zt = zm[g]
                    z3 = z.rearrange("p (b x) -> p b x", b=2)
                    S.dve(lambda z3=z3, zsrc=zsrc: nc.vector.tensor_tensor(out=z3, in0=zsrc, in1=sbmask[:, :].rearrange("p (b x) -> p b x", b=2), op=ALU.add),
                          r=ztoks + ["sbmask"], w=[zt])
                    S.act(lambda E=E, z=z: nc.scalar.activation(out=E, in_=z, func=AF.Exp, scale=0.125), r=[zt], w=[Et])
                else:
                    S.act(lambda E3=E3, zsrc=zsrc: nc.scalar.activation(out=E3, in_=zsrc, func=AF.Exp, scale=0.125), r=ztoks, w=[Et])
                S.act(lambda E=E, SP=SP: nc.scalar.activation(out=SP, in_=E, func=AF.Ln, bias=1.0), r=[Et], w=[SPt])
            info[s_] = (c, Vc, vt_)

        for s_ in range(min(LA, n)):
            SZ(s_)
        for k in range(n):
            pc, Vc, vt_ = info[k]
            for g in range(2):
                pa = 4 + g
                SP, SPt = SPb[k % NE][g]
                S.pe(lambda SP=SP, pa=pa, pc=pc, I=I: nc.tensor.matmul(bank(pa), lhsT=uincl[:, :], rhs=SP, start=(pc == I), stop=(pc == 0), skip_group_check=True),
                     r=[SPt, "uincl", PSt[pa]], w=[PSt[pa]])
            for g in range(2):
                pa = 4 + g
                X, Xt = Xb[g]
                S.act(lambda X=X, pa=pa: nc.scalar.activation(out=X, in_=bank(pa), func=AF.Exp, scale=-1.0), r=[PSt[pa]], w=[Xt])
            if k + LA < n:
                SZ(k + LA)
            if pc > 0:
                for g in range(2):
                    pa = 4 + g
                    SP, SPt = SPb[k % NE][g]
                    S.pe(lambda SP=SP, pa=pa: nc.tensor.matmul(bank(pa), lhsT=lstr[:, :], rhs=SP, start=False, stop=False, skip_group_check=True),
                         r=[SPt, "lstr", PSt[pa]], w=[PSt[pa]])
            for g in range(2):
                E, Et = Eb[k % NE][g]
                X, Xt = Xb[g]
                att, att_t = attb[k % 2][g]
                S.dve(lambda att=att, E=E, X=X: nc.vector.tensor_tensor(out=att, in0=E, in1=X, op=ALU.mult), r=[Et, Xt], w=[att_t])
            for g in range(2):
                att, att_t = attb[k % 2][g]
                po = 6 + g

                def fo(g=g, pc=pc, att=att, Vc=Vc, po=po, I=I):
                    ins = None
                    for e in range(2):
                        for jl in range(2):
                            h = 4 * g + 2 * jl + e
                            ins = nc.tensor.matmul(bank(po, lo=jl * 128, hi=jl * 128 + 128)[e * 64:(e + 1) * 64, :],
                                                   lhsT=Vc[:, h * 64:(h + 1) * 64], rhs=att[:, (e * 2 + jl) * 128:(e * 2 + jl) * 128 + 128],
                                                   start=(pc == I and jl == 0), stop=(pc == 0), skip_group_check=True)
                    return ins
                S.pe(fo, r=[att_t, vt_, PSt[po]], w=[PSt[po]])
                if pc == 0:
                    S.act(lambda g=g, po=po, qs=qs: nc.scalar.copy(out=ysb[:, 2 * g:2 * g + 2, qs],
                                                                    in_=bank(po, hi=256).rearrange("p (a b) -> p a b", b=128)),
                          r=[PSt[po]], w=[ysbt])


def bc_last(x, n):
    return bass.AP(x.tensor, x.offset, [list(x.ap[0]), list(x.ap[1]), [0, n]])


def rwkv_branch(nc, S, ar, bank, PSt, PS, l, i, xn, w_comb, w_vres, lora_wa, g_lora, v_lora, yrw, yrwt, slab, load_w,
                proj_fm, cvc, cs_):
    ident, blk, rmask = cs_["ident"], cs_["blk"], cs_["rmask"]
    rw_carry, rw_state, vfirst, omka, ones64 = cs_["rw_carry"], cs_["rw_state"], cs_["vfirst"], cs_["omka"], cs_["ones64"]
    NCH = TT // CH
    m_st = rmask[0:64, 0:512]
    m_inc = rmask[0:64, 512:1024]
    m_lo = rmask[0:64, 1024:1536]
    m_id = rmask[0:64, 1536:2048]
    Sst = rw_state[:, l * 256:(l + 1) * 256].rearrange("p (j v) -> p j v", v=64)
    identb, rw_sbf = cs_["identb"], cs_["rw_sbf"]
    Sbf = rw_sbf[:, l * 256:(l + 1) * 256].rearrange("p (j v) -> p j v", v=64)

    rt, rtt = ar.alloc((4, TT), BF16)
    kpt, kptt = ar.alloc((4, TT), BF16)
    kt, ktt = ar.alloc((4, TT), BF16)
    bt, btt = ar.alloc((4, TT), BF16)
    vT, vTt = ar.alloc((4, TT), BF16)
    bonus, bont = ar.alloc((4, TT), BF16)
    gC, gCt = ar.alloc((4, NCH), F32)
    xwa, xwat = ar.alloc((TT,), BF16)
    sxg, sxgt = ar.alloc((TT,), BF16)
    xv, xvt = ar.alloc((TT,), BF16)
    lwa, lwat = ar.alloc((W,), BF16)
    glo, glot = ar.alloc((W,), BF16)
    vlo, vlot = ar.alloc((W,), BF16)
    mark = ar.off
    S.dma("pool", lambda: nc.gpsimd.dma_start(out=lwa, in_=lora_wa[l]), w=[lwat])
    S.dma("pool", lambda: nc.gpsimd.dma_start(out=glo, in_=g_lora[l]), w=[glot])
    if l > 0:
        S.dma("pool", lambda: nc.gpsimd.dma_start(out=vlo[0:32, :], in_=v_lora[0]), w=[vlot])

    bufs = [ar.alloc((TT + 1,), F32) for _ in range(3)]
    tmps = [ar.alloc((TT,), F32) for _ in range(3)]
    st = {"n": 0}

    def lerp(pb, parts, cc, mu_ap, dst, dtok):
        k = st["n"] % 3
        st["n"] += 1
        buf, bft = bufs[k]
        tmp, tmt = tmps[k]
        car = rw_carry[0:parts, l * 16 + cc:l * 16 + cc + 1]
        S.dve(lambda: nc.vector.tensor_copy(out=buf[0:parts, 0:1], in_=car), r=["rw_carry%d_%d" % (l, cc)], w=[bft])
        S.act(lambda: nc.scalar.copy(out=buf[0:parts, 1:TT + 1], in_=bank(pb, parts=parts)), r=[PSt[pb], bft], w=[bft])
        S.dve(lambda: nc.vector.tensor_copy(out=car, in_=buf[0:parts, TT:TT + 1]), r=[bft], w=["rw_carry%d_%d" % (l, cc)])
        S.dve(lambda: nc.vector.tensor_tensor(out=tmp[0:parts, :], in0=buf[0:parts, 0:TT], in1=buf[0:parts, 1:TT + 1], op=ALU.subtract),
              r=[bft], w=[tmt])
        S.dve(lambda: nc.vector.scalar_tensor_tensor(out=dst, in0=tmp[0:parts, :], scalar=mu_ap, in1=buf[0:parts, 1:TT + 1],
                                                     op0=ALU.mult, op1=ALU.add), r=[tmt, bft, "cvec"], w=[dtok])

    sle, te = load_w(w_comb[l][:, OFF_RWKV + 1536:OFF_RWKV + 1792].rearrange("(c p) n -> p c n", p=128), None)
    ex, ext = ar.alloc((TT,), F32)
    proj_fm(bank(0), sle, te, 0, 128, PSt[0])
    lerp(0, 128, 12, cvc(("mu", l), 12), ex, ext)
    S.act(lambda: nc.scalar.activation(out=xwa[0:64, :], in_=ex[0:64, :], func=AF.Tanh), r=[ext], w=[xwat])
    S.act(lambda: nc.scalar.copy(out=xwa[64:128, :], in_=ex[64:128, :]), r=[ext], w=[xwat])
    ex2, ext2 = ar.alloc((TT,), F32)
    proj_fm(bank(1), sle, te, 128, 128, PSt[1])
    lerp(1, 128, 13, cvc(("mu", l), 13), ex2, ext2)
    S.act(lambda: nc.scalar.activation(out=sxg, in_=ex2, func=AF.Sigmoid), r=[ext2], w=[sxgt])
    if l > 0:
        slv_, tv_ = load_w(w_vres[0].rearrange("(c p) n -> p c n", p=128), None)
        ex3, ext3 = ar.alloc((TT,), F32)
        proj_fm(bank(2, parts=32), slv_, tv_, 0, 32, PSt[2])
        lerp(2, 32, 14, cvc(("mu_vres", 1), 0, 1, parts=32), ex3[0:32, :], ext3)
        S.act(lambda: nc.scalar.copy(out=xv[0:32, :], in_=ex3[0:32, :]), r=[ext3], w=[xvt])

    slr, tr = slab(w_comb[l], OFF_RWKV)
    slk, tk = slab(w_comb[l], OFF_RWKV + 512)
    slv, tv = slab(w_comb[l], OFF_RWKV + 1024)
    names = ["r", "k", "v", "sg", "a", "lw", "L", "kk", "t1", "t2", "kap", "kp", "b", "eL", "eNL"]
    dbl = ("r", "k", "v", "sg", "a", "lw", "L")
    T0 = {nm: ar.alloc((TT,), F32) for nm in names}
    T1 = {nm: (ar.alloc((TT,), F32) if nm in dbl else T0[nm]) for nm in names}
    cur = {"T": T0}

    def A(nm):
        return cur["T"][nm][0]

    def K_(nm):
        return cur["T"][nm][1]

    for j in range(4):
        cur["T"] = T0 if j % 2 == 0 else T1
        A = (lambda T: (lambda nm: T[nm][0]))(cur["T"])
        proj_fm(bank(0), slr, tr, j * 128, 128, PSt[0])
        lerp(0, 128, j, cvc(("mu", l), j), A("r"), K_("r"))
        proj_fm(bank(1), slk, tk, j * 128, 128, PSt[1])
        lerp(1, 128, 4 + j, cvc(("mu", l), 4 + j), A("k"), K_("k"))
        proj_fm(bank(2), slv, tv, j * 128, 128, PSt[2])
        lerp(2, 128, 8 + j, cvc(("mu", l), 8 + j), A("v"), K_("v"))
        js = slice(j * 128, (j + 1) * 128)
        S.pe(lambda js=js: nc.tensor.matmul(bank(3), lhsT=lwa[0:64, js], rhs=xwa[0:64, :], start=True, stop=True), r=[lwat, xwat], w=[PSt[3]])
        S.act(lambda j=j, A=A: nc.scalar.activation(out=A("sg"), in_=bank(3), func=AF.Sigmoid, bias=cvc(("w0", l), j)), r=[PSt[3], "cvec"], w=[K_("sg")])
        S.dve(lambda A=A: nc.vector.tensor_scalar(out=A("lw"), in0=A("sg"), scalar1=-DS, scalar2=None, op0=ALU.mult), r=[K_("sg")], w=[K_("lw")])
        for ch in range(NCH):
            cs = slice(ch * CH, (ch + 1) * CH)
            S.dve(lambda cs=cs, A=A: nc.vector.tensor_tensor_scan(out=A("L")[:, cs], data0=ones64[:, :], data1=A("lw")[:, cs], initial=0.0,
                                                             op0=ALU.mult, op1=ALU.add), r=[K_("lw"), "ones64"], w=[K_("L")])
        S.pe(lambda js=js: nc.tensor.matmul(bank(4), lhsT=lwa[64:128, js], rhs=xwa[64:128, :], start=True, stop=True), r=[lwat, xwat], w=[PSt[4]])
        S.act(lambda j=j, A=A: nc.scalar.activation(out=A("a"), in_=bank(4), func=AF.Sigmoid, bias=cvc(("a0", l), j)), r=[PSt[4], "cvec"], w=[K_("a")])
        if l > 0:
            S.pe(lambda js=js: nc.tensor.matmul(bank(5), lhsT=vlo[0:32, js], rhs=xv[0:32, :], start=True, stop=True), r=[vlot, xvt], w=[PSt[5]])
            S.act(lambda j=j, A=A: nc.scalar.activation(out=A("t1"), in_=bank(5), func=AF.Sigmoid, bias=cvc(("v0", 1), j)), r=[PSt[5], "cvec"], w=[K_("t1")])
            S.dve(lambda j=j, A=A: nc.vector.tensor_tensor(out=A("t2"), in0=vfirst[:, j, :], in1=A("v"), op=ALU.subtract), r=["vfirst", K_("v")], w=[K_("t2")])
            S.dve(lambda A=A: nc.vector.tensor_tensor(out=A("t2"), in0=A("t2"), in1=A("t1"), op=ALU.mult), r=[K_("t2"), K_("t1")], w=[K_("t2")])
            S.dve(lambda j=j, A=A: nc.vector.tensor_tensor(out=vT[:, j, :], in0=A("v"), in1=A("t2"), op=ALU.add), r=[K_("v"), K_("t2")], w=[vTt])
        else:
            S.act(lambda j=j, A=A: nc.scalar.copy(out=vT[:, j, :], in_=A("v")), r=[K_("v")], w=[vTt])
            S.act(lambda j=j, A=A: nc.scalar.copy(out=vfirst[:, j, :], in_=A("v")), r=[K_("v")], w=["vfirst"])
        S.dve(lambda j=j, A=A: nc.vector.tensor_scalar(out=A("kk"), in0=A("k"), scalar1=cvc(("k_k", l), j), scalar2=None, op0=ALU.mult), r=[K_("k"), "cvec"], w=[K_("kk")])
        S.dve(lambda A=A: nc.vector.tensor_tensor(out=A("t1"), in0=A("kk"), in1=A("kk"), op=ALU.mult), r=[K_("kk"), K_("t1")], w=[K_("t1")])
        S.pe(lambda A=A: nc.tensor.matmul(bank(6), lhsT=blk[:, :], rhs=A("t1"), start=True, stop=True), r=["blk", K_("t1")], w=[PSt[6]])
        S.act(lambda A=A: nc.scalar.activation(out=A("t2"), in_=bank(6), func=AF.Ln, bias=1e-24), r=[PSt[6], K_("t2")], w=[K_("t2")])
        S.act(lambda A=A: nc.scalar.activation(out=A("t2"), in_=A("t2"), func=AF.Exp, scale=-0.5), r=[K_("t2")], w=[K_("t2")])
        S.dve(lambda A=A: nc.vector.tensor_tensor(out=A("kap"), in0=A("kk"), in1=A("t2"), op=ALU.mult), r=[K_("kk"), K_("t2")], w=[K_("kap")])
        S.dve(lambda j=j, A=A: nc.vector.tensor_scalar(out=A("t1"), in0=A("a"), scalar1=cvc(("k_a", l), j), scalar2=omka[:, l * 4 + j:l * 4 + j + 1],
                                                  op0=ALU.mult, op1=ALU.add), r=[K_("a"), "cvec", "omka", K_("t1")], w=[K_("t1")])
        S.dve(lambda A=A: nc.vector.tensor_tensor(out=A("kp"), in0=A("k"), in1=A("t1"), op=ALU.mult), r=[K_("k"), K_("t1")], w=[K_("kp")])
        S.dve(lambda A=A: nc.vector.tensor_tensor(out=A("b"), in0=A("kap"), in1=A("a"), op=ALU.mult), r=[K_("kap"), K_("a")], w=[K_("b")])
        S.dve(lambda j=j, A=A: nc.vector.scalar_tensor_tensor(out=A("t2"), in0=A("r"), scalar=cvc(("r_k", l), j), in1=A("kp"), op0=ALU.mult, op1=ALU.mult),
              r=[K_("r"), K_("kp"), "cvec", K_("t2")], w=[K_("t2")])
        S.pe(lambda A=A: nc.tensor.matmul(bank(7), lhsT=blk[:, :], rhs=A("t2"), start=True, stop=True), r=["blk", K_("t2")], w=[PSt[7]])
        S.dve(lambda j=j: nc.vector.tensor_tensor(out=bonus[:, j, :], in0=bank(7), in1=vT[:, j, :], op=ALU.mult), r=[PSt[7], vTt], w=[bont])
        S.act(lambda A=A: nc.scalar.activation(out=A("eL"), in_=A("L"), func=AF.Exp), r=[K_("L")], w=[K_("eL")])
        S.act(lambda A=A: nc.scalar.activation(out=A("eNL"), in_=A("L"), func=AF.Exp, scale=-1.0), r=[K_("L")], w=[K_("eNL")])
        S.dve(lambda A=A: nc.vector.tensor_tensor(out=A("t1"), in0=A("L"), in1=A("lw"), op=ALU.subtract), r=[K_("L"), K_("lw"), K_("t1")], w=[K_("t1")])
        S.act(lambda A=A: nc.scalar.activation(out=A("t1"), in_=A("t1"), func=AF.Exp), r=[K_("t1")], w=[K_("t1")])
        S.dve(lambda j=j, A=A: nc.vector.tensor_copy(out=gC[:, j, :], in_=A("eL").rearrange("p (c x) -> p c x", x=CH)[:, :, CH - 1]), r=[K_("eL")], w=[gCt])
        S.dve(lambda j=j, A=A: nc.vector.tensor_tensor(out=rt[:, j, :], in0=A("r"), in1=A("eL"), op=ALU.mult), r=[K_("r"), K_("eL")], w=[rtt])
        S.dve(lambda j=j, A=A: nc.vector.tensor_tensor(out=kpt[:, j, :], in0=A("kap"), in1=A("t1"), op=ALU.mult), r=[K_("kap"), K_("t1")], w=[kptt])
        S.dve(lambda j=j, A=A: nc.vector.tensor_tensor(out=kt[:, j, :], in0=A("kp"), in1=A("eNL"), op=ALU.mult), r=[K_("kp"), K_("eNL")], w=[ktt])
        S.dve(lambda j=j, A=A: nc.vector.tensor_tensor(out=bt[:, j, :], in0=A("b"), in1=A("eNL"), op=ALU.mult), r=[K_("b"), K_("eNL")], w=[btt])
    S.barrier()
    ar.off = mark

    def t64(dt=BF16):
        return ar.alloc((512,), dt)

    P0, P0t = t64()
    P0T, P0Tt = t64()
    P1, P1t = t64()
    P1T, P1Tt = t64()
    MT, MTt = t64()
    AkkT, AkkTt = t64()
    ArkT, ArkTt = t64()
    nArbT, nArbTt = t64()
    Vt, Vtt = t64()
    Kto, Ktot = t64()
    nBto, nBtot = t64()
    KPto, KPtot = t64()
    X1, X1t = t64()
    W1, W1t = t64(F32)
    KtT, KtTt = ar.alloc((4, 64), BF16)
    U, Ut = t64()
    Ors, Orst = t64(F32)
    O, Ot = t64(F32)
    Osq, Osqt = t64(F32)
    Obf, Obft = t64(BF16)
    stat, statt = ar.alloc((32,), F32)
    tmpS, tmpSt = ar.alloc((4, 64), F32)
    gnT, gnTt = ar.alloc((4, TT), F32)

    def esplit(b0):
        return PS[0:64, b0 * 512:(b0 + 2) * 512].rearrange("p (e x) -> p e x", e=2)[:, :, 0:256]

    def v3(x):
        return x[0:64, :].rearrange("p (e x) -> p e x", e=2)

    def hbv(x, hb):
        return x[0:64, hb * 64:(hb + 1) * 64]

    def amat(b0, lhs, rhs, cs, ltok, rtok):
        def f():
            ins = None
            for e in range(2):
                for j in range(4):
                    ins = nc.tensor.matmul(bank(b0 + e, parts=64, lo=j * 64, hi=j * 64 + 64),
                                           lhsT=lhs[e * 64:(e + 1) * 64, j, cs], rhs=rhs[e * 64:(e + 1) * 64, j, cs], start=True, stop=True)
            return ins
        S.pe(f, r=[ltok, rtok], w=[PSt[b0], PSt[b0 + 1]])

    def bmm(b, lhs, rhs, ltok, rtok):
        def f():
            ins = None
            for hb in range(8):
                ins = nc.tensor.matmul(bank(b, parts=64, lo=hb * 64, hi=hb * 64 + 64), lhsT=hbv(lhs, hb), rhs=hbv(rhs, hb), start=True, stop=True)
            return ins
        S.pe(f, r=[ltok, rtok], w=[PSt[b]])

    def tpose(b, src, stok, cs):
        def f():
            ins = None
            ov = bank(b, parts=64).rearrange("p (e j k) -> p e j k", e=2, j=4)
            for j in range(4):
                ins = nc.tensor.matmul(ov[:, :, j, :], lhsT=src[:, j, cs], rhs=identb[:, :].rearrange("p (e k) -> p e k", e=2), start=True, stop=True)
            return ins
        S.pe(f, r=[stok, "identb"], w=[PSt[b]])

    for ch in range(NCH):
        cs = slice(ch * CH, (ch + 1) * CH)
        amat(0, bt, kpt, cs, btt, kptt)
        amat(2, kpt, bt, cs, kptt, btt)
        S.dve(lambda: nc.vector.tensor_tensor(out=v3(P0T), in0=esplit(0), in1=v3(m_st), op=ALU.mult), r=[PSt[0], PSt[1], "rmask"], w=[P0Tt])
        S.dve(lambda: nc.vector.tensor_tensor(out=v3(P0), in0=esplit(2), in1=v3(m_lo), op=ALU.mult), r=[PSt[2], PSt[3], "rmask"], w=[P0t])
        S.dve(lambda: nc.vector.scalar_tensor_tensor(out=MT[0:64, :], in0=P0T[0:64, :], scalar=-1.0, in1=m_id, op0=ALU.mult, op1=ALU.add),
              r=[P0Tt, "rmask"], w=[MTt])
        Pc, Pct, PcT, PcTt = P0, P0t, P0T, P0Tt
        Pn, Pnt, PnT, PnTt = P1, P1t, P1T, P1Tt
        for k in range(1, 6):
            bmm(4, PcT, Pc, PcTt, Pct)
            if k < 5:
                bmm(5, Pc, PcT, Pct, PcTt)
            S.act(lambda Pn=Pn: nc.scalar.copy(out=Pn[0:64, :], in_=bank(4, parts=64)), r=[PSt[4]], w=[Pnt])
            if k < 5:
                S.dve(lambda PnT=PnT: nc.vector.tensor_copy(out=PnT[0:64, :], in_=bank(5, parts=64)), r=[PSt[5]], w=[PnTt])
            bmm(6, Pn, MT, Pnt, MTt)
            S.dve(lambda: nc.vector.tensor_tensor(out=MT[0:64, :], in0=MT[0:64, :], in1=bank(6, parts=64), op=ALU.add), r=[PSt[6], MTt], w=[MTt])
            Pc, Pct, PcT, PcTt, Pn, Pnt, PnT, PnTt = Pn, Pnt, PnT, PnTt, Pc, Pct, PcT, PcTt
        amat(0, kt, kpt, cs, ktt, kptt)
        S.dve(lambda: nc.vector.tensor_tensor(out=v3(AkkT), in0=esplit(0), in1=v3(m_st), op=ALU.mult), r=[PSt[0], PSt[1], "rmask"], w=[AkkTt])
        amat(2, kt, rt, cs, ktt, rtt)
        S.dve(lambda: nc.vector.tensor_tensor(out=v3(ArkT), in0=esplit(2), in1=v3(m_inc), op=ALU.mult), r=[PSt[2], PSt[3], "rmask"], w=[ArkTt])
        amat(0, bt, rt, cs, btt, rtt)
        S.dve(lambda: nc.vector.scalar_tensor_tensor(out=v3(nArbT), in0=esplit(0), scalar=-1.0, in1=v3(m_inc), op0=ALU.mult, op1=ALU.mult),
              r=[PSt[0], PSt[1], "rmask"], w=[nArbTt])
        tpose(4, vT, vTt, cs)
        S.act(lambda: nc.scalar.copy(out=Vt[0:64, :], in_=bank(4, parts=64)), r=[PSt[4]], w=[Vtt])
        tpose(5, kt, ktt, cs)
        S.act(lambda: nc.scalar.copy(out=Kto[0:64, :], in_=bank(5, parts=64)), r=[PSt[5]], w=[Ktot])
        tpose(6, bt, btt, cs)
        S.act(lambda: nc.scalar.mul(out=nBto[0:64, :], in_=bank(6, parts=64), mul=-1.0), r=[PSt[6]], w=[nBtot])
        tpose(7, kpt, kptt, cs)
        S.act(lambda: nc.scalar.copy(out=KPto[0:64, :], in_=bank(7, parts=64)), r=[PSt[7]], w=[KPtot])
        bmm(0, AkkT, Vt, AkkTt, Vtt)
        S.act(lambda: nc.scalar.copy(out=X1[0:64, :], in_=bank(0, parts=64)), r=[PSt[0]], w=[X1t])
        bmm(1, MT, X1, MTt, X1t)
        S.act(lambda: nc.scalar.copy(out=W1[0:64, :], in_=bank(1, parts=64)), r=[PSt[1]], w=[W1t])

        def fk():
            ins = None
            for e in range(2):
                for j in range(4):
                    hb = e * 4 + j
                    ins = nc.tensor.matmul(PS[e * 64:(e + 1) * 64, 2 * 512 + j * 64:2 * 512 + j * 64 + 64], lhsT=hbv(KPto, hb), rhs=hbv(MT, hb),
                                           start=True, stop=True)
            return ins
        S.pe(fk, r=[KPtot, MTt], w=[PSt[2]])
        S.act(lambda: nc.scalar.copy(out=KtT, in_=bank(2, hi=256).rearrange("p (j t) -> p j t", t=64)), r=[PSt[2]], w=[KtTt])
        def fu():
            ins = None
            for e in range(2):
                for j in range(4):
                    ins = nc.tensor.matmul(bank(4 + e, parts=64, lo=j * 64, hi=j * 64 + 64), lhsT=KtT[e * 64:(e + 1) * 64, j, :],
                                           rhs=Sbf[e * 64:(e + 1) * 64, j, :], start=True, stop=True)
            return ins
        S.pe(fu, r=[KtTt, "rw_sbf"], w=[PSt[4], PSt[5]])
        S.dve(lambda: nc.vector.tensor_tensor(out=v3(U), in0=esplit(4), in1=v3(W1), op=ALU.add), r=[PSt[4], PSt[5], W1t], w=[Ut])
        def fo1(cs=cs):
            ins = None
            for e in range(2):
                for j in range(4):
                    ins = nc.tensor.matmul(bank(6 + e, parts=64, lo=j * 64, hi=j * 64 + 64), lhsT=rt[e * 64:(e + 1) * 64, j, cs],
                                           rhs=Sbf[e * 64:(e + 1) * 64, j, :], start=True, stop=True)
            return ins
        S.pe(fo1, r=[rtt, "rw_sbf"], w=[PSt[6], PSt[7]])
        S.act(lambda: nc.scalar.copy(out=v3(Ors), in_=esplit(6)), r=[PSt[6], PSt[7]], w=[Orst])

        def fo2():
            ins = None
            for hb in range(8):
                nc.tensor.matmul(bank(0, parts=64, lo=hb * 64, hi=hb * 64 + 64), lhsT=hbv(ArkT, hb), rhs=hbv(Vt, hb), start=(hb == 0), stop=False,
                                 skip_group_check=True)
                ins = nc.tensor.matmul(bank(0, parts=64, lo=hb * 64, hi=hb * 64 + 64), lhsT=hbv(nArbT, hb), rhs=hbv(U, hb), start=False, stop=True,
                                       skip_group_check=True)
            return ins
        S.pe(fo2, r=[ArkTt, Vtt, nArbTt, Ut], w=[PSt[0]])
        S.dve(lambda: nc.vector.tensor_tensor(out=O[0:64, :].rearrange("p (j e v) -> p e j v", j=4, e=2),
                                              in0=bank(0, parts=64).rearrange("p (e j v) -> p e j v", e=2, j=4),
                                              in1=Ors[0:64, :].rearrange("p (e j v) -> p e j v", e=2, j=4), op=ALU.add), r=[PSt[0], Orst], w=[Ot])
        def fs():
            ins = None
            for e in range(2):
                for j in range(4):
                    hb = e * 4 + j
                    o = PS[e * 64:(e + 1) * 64, 1 * 512 + j * 64:1 * 512 + j * 64 + 64]
                    nc.tensor.matmul(o, lhsT=hbv(Kto, hb), rhs=hbv(Vt, hb), start=(j == 0), stop=False, skip_group_check=True)
                    ins = nc.tensor.matmul(o, lhsT=hbv(nBto, hb), rhs=hbv(U, hb), start=False, stop=True, skip_group_check=True)
            return ins
        S.pe(fs, r=[Ktot, Vtt, nBtot, Ut], w=[PSt[1]])
        S.dve(lambda: nc.vector.tensor_tensor(out=tmpS, in0=Sst, in1=bank(1, hi=256).rearrange("p (j v) -> p j v", v=64), op=ALU.add),
              r=["rw_state", PSt[1]], w=[tmpSt])
        S.dve(lambda ch=ch: nc.vector.tensor_tensor(out=Sst, in0=tmpS, in1=bc_last(gC[:, :, ch], 64), op=ALU.mult), r=[tmpSt, gCt, "rw_state"], w=["rw_state"])
        S.act(lambda: nc.scalar.copy(out=Sbf, in_=Sst), r=["rw_state"], w=["rw_sbf"])
        O3 = O[0:64, :].rearrange("p (h v) -> p h v", v=64)
        Q3 = Osq[0:64, :].rearrange("p (h v) -> p h v", v=64)
        mean = stat[0:64, 0:8]
        ex2_ = stat[0:64, 8:16]
        var = stat[0:64, 16:24]
        rstd = stat[0:64, 24:32]
        S.dve(lambda: nc.vector.tensor_reduce(out=mean, in_=O3, op=ALU.add, axis=AX.X), r=[Ot], w=[statt])
        S.act(lambda: nc.scalar.activation(out=Osq[0:64, :], in_=O[0:64, :], func=AF.Square), r=[Ot], w=[Osqt])
        S.dve(lambda: nc.vector.tensor_reduce(out=ex2_, in_=Q3, op=ALU.add, axis=AX.X), r=[Osqt, statt], w=[statt])
        S.dve(lambda: nc.vector.tensor_scalar(out=mean, in0=mean, scalar1=1.0 / 64, scalar2=None, op0=ALU.mult), r=[statt], w=[statt])
        S.dve(lambda: nc.vector.tensor_tensor(out=var, in0=mean, in1=mean, op=ALU.mult), r=[statt], w=[statt])
        S.dve(lambda: nc.vector.scalar_tensor_tensor(out=var, in0=ex2_, scalar=1.0 / 64, in1=var, op0=ALU.mult, op1=ALU.subtract), r=[statt], w=[statt])
        S.act(lambda: nc.scalar.activation(out=rstd, in_=var, func=AF.Ln, bias=64e-5), r=[statt], w=[statt])
        S.act(lambda: nc.scalar.activation(out=rstd, in_=rstd, func=AF.Exp, scale=-0.5), r=[statt], w=[statt])
        S.dve(lambda: nc.vector.tensor_tensor(out=O3, in0=O3, in1=bc_last(mean, 64), op=ALU.subtract), r=[Ot, statt], w=[Ot])
        S.dve(lambda: nc.vector.tensor_tensor(out=Obf[0:64, :].rearrange("p (h v) -> p h v", v=64), in0=O3, in1=bc_last(rstd, 64), op=ALU.mult),
              r=[Ot, statt], w=[Obft])
        def fT():
            ins = None
            for j in range(4):
                ins = nc.tensor.matmul(bank(3, lo=j * 64, hi=j * 64 + 64), lhsT=Obf[0:64, j * 128:(j + 1) * 128], rhs=identb[0:64, 0:64], start=True, stop=True)
            return ins
        S.pe(fT, r=[Obft, "identb"], w=[PSt[3]])
        S.act(lambda cs=cs: nc.scalar.copy(out=gnT[:, :, cs], in_=bank(3, hi=256).rearrange("p (j t) -> p j t", t=64)), r=[PSt[3]], w=[gnTt])
    for j in range(4):
        js = slice(j * 128, (j + 1) * 128)
        pb = 4 + j % 2
        S.pe(lambda js=js, pb=pb: nc.tensor.matmul(bank(pb), lhsT=glo[:, js], rhs=sxg, start=True, stop=True), r=[glot, sxgt], w=[PSt[pb]])
        S.dve(lambda j=j: nc.vector.tensor_scalar(out=gnT[:, j, :], in0=gnT[:, j, :], scalar1=cvc(("lnx_w", l), j), scalar2=cvc(("lnx_b", l), j),
                                                  op0=ALU.mult, op1=ALU.add), r=[gnTt, "cvec"], w=[gnTt])
        S.dve(lambda j=j: nc.vector.tensor_tensor(out=gnT[:, j, :], in0=gnT[:, j, :], in1=bonus[:, j, :], op=ALU.add), r=[gnTt, bont], w=[gnTt])
        S.dve(lambda j=j, pb=pb: nc.vector.tensor_tensor(out=yrw[:, j, :], in0=gnT[:, j, :], in1=bank(pb), op=ALU.mult), r=[gnTt, PSt[pb]], w=[yrwt])


_CACHE = {}


def _prep_inputs(inp, b, depth, T):
    f = lambda a: np.ascontiguousarray(np.asarray(a, np.float32))
    m = {}
    m["xT"] = f(np.asarray(inp["x"][b])[:T].T)
    m["memT"] = f(np.asarray(inp["mem"][b]).T)
    return m


def _shared_inputs(inp, depth):
    f = lambda a: np.ascontiguousarray(np.asarray(a, np.float32))
    m = {}
    m["cvec"] = _build_cvec(inp, depth)
    for k in ("w_comb", "branch_proj", "w_mix_out", "ca_wq", "ca_wkv", "ca_wo", "ffn_up", "ffn_down", "g_lora"):
        m[k] = f(np.asarray(inp[k])[:depth])
    m["w_vres"] = f(inp["w_vres"])
    m["v_lora"] = f(inp["v_lora"])
    m["lora_wa"] = f(np.concatenate([np.asarray(inp["w_lora"])[:depth], np.asarray(inp["a_lora"])[:depth]], axis=1))
    for k, v in _build_consts().items():
        m["c_" + k] = v
    return m


def kernel(**inputs):
    depth, T = 2, 4096
    key = (depth, T)
    if key not in _CACHE:
        _CACHE[key] = build_program(T=T, depth=depth)
    nc = _CACHE[key]
    shared = _shared_inputs(inputs, depth)
    in_maps = []
    for b in range(8):
        m = dict(shared)
        m.update(_prep_inputs(inputs, b, depth, T))
        in_maps.append(m)
    res = run_bass_kernel_spmd(nc, in_maps, core_ids=list(range(8)))
    out = np.stack([np.asarray(r["outT"]).T for r in res.results], axis=0)
    return np.ascontiguousarray(out.astype(np.float32))
```

```python
import math
from contextlib import ExitStack
import numpy as np
import concourse.bass as bass
import concourse.mybir as mybir
from concourse.bass_utils import run_bass_kernel_spmd

F32 = mybir.dt.float32
BF16 = mybir.dt.bfloat16
AF = mybir.ActivationFunctionType
ALU = mybir.AluOpType
AX = mybir.AxisListType

D = 1024
MEM = 256
W = 512
TT = 512
FFN = 2816
NFC = 44
COMB = 7936
OFF_SB = 1536
OFF_GATE = 3072
OFF_RWKV = 6144
DS = math.exp(-0.5)
CH = 64


class _Op:
    __slots__ = ("eng", "fn", "deps", "epoch", "dma", "need_inc", "count", "sem_key", "idx")


class Sched:
    ENGS = ("pe", "act", "dve", "pool", "sp")
    NQ = 8

    def __init__(self, nc):
        self.nc = nc
        self.ops = []
        self.last_writer = {}
        self.readers = {}
        self.epoch = 0
        self.bar_start = 0

    def add(self, eng, fn, reads=(), writes=(), dma=False):
        op = _Op()
        op.idx = len(self.ops)
        op.eng, op.fn, op.epoch, op.dma = eng, fn, self.epoch, dma
        deps = set()
        for t in reads:
            w = self.last_writer.get(t)
            if w is not None:
                deps.add(w)
        for t in writes:
            deps.update(self.readers.get(t, ()))
            w = self.last_writer.get(t)
            if w is not None:
                deps.add(w)
        for t in reads:
            self.readers.setdefault(t, []).append(op.idx)
        for t in writes:
            self.last_writer[t] = op.idx
            self.readers[t] = []
        deps.discard(op.idx)
        op.deps = deps
        op.need_inc = False
        self.ops.append(op)
        return op.idx

    def pe(self, fn, r=(), w=()):
        return self.add("pe", fn, r, w)

    def act(self, fn, r=(), w=()):
        return self.add("act", fn, r, w)

    def dve(self, fn, r=(), w=()):
        return self.add("dve", fn, r, w)

    def pool(self, fn, r=(), w=()):
        return self.add("pool", fn, r, w)

    def dma(self, eng, fn, r=(), w=()):
        return self.add(eng, fn, r, w, dma=True)

    def barrier(self):
        last = {}
        deps = set()
        for op in self.ops[self.bar_start:]:
            if op.dma:
                deps.add(op.idx)
            else:
                last[op.eng] = op.idx
        deps.update(last.values())
        for eng in self.ENGS:
            i = self.add(eng, None)
            self.ops[i].deps = set(deps)
        self.bar_start = len(self.ops)
        self.last_writer.clear()
        self.readers.clear()

    def emit(self):
        nc = self.nc
        ops = self.ops
        for op in ops:
            for d in op.deps:
                ops[d].need_inc = True
        counters = {}
        dma_counters = {}
        sem_keys = set()
        for op in ops:
            if op.dma:
                j = dma_counters.get(op.eng, 0)
                dma_counters[op.eng] = j + 1
                op.sem_key = ("dma", op.eng, j % self.NQ)
                op.count = 16 * (j // self.NQ + 1)
                sem_keys.add(op.sem_key)
            else:
                key = ("c", op.eng, op.epoch)
                op.sem_key = key
                if op.need_inc and op.fn is not None:
                    counters[key] = counters.get(key, 0) + 1
                    sem_keys.add(key)
                op.count = counters.get(key, 0)
        final_waits = {}
        for op in ops:
            if op.dma:
                k = op.sem_key
                final_waits[k] = max(final_waits.get(k, 0), op.count)
        self.n_sems = len(sem_keys)
        with ExitStack() as es:
            sems = {}
            for i, k in enumerate(sorted(sem_keys, key=str)):
                sems[k] = es.enter_context(nc.semaphore("s%d" % i))
            block = es.enter_context(nc.Block())

            def run(engname, e):
                waited = {}
                for op in ops:
                    if op.eng != engname:
                        continue
                    need = {}
                    for d in op.deps:
                        dop = ops[d]
                        if dop.count <= 0:
                            continue
                        k = dop.sem_key
                        if need.get(k, 0) < dop.count:
                            need[k] = dop.count
                    if op.dma:
                        prev = op.count - 16
                        if prev > 0 and need.get(op.sem_key, 0) < prev:
                            need[op.sem_key] = prev
                    for k in sorted(need, key=str):
                        v = need[k]
                        if waited.get(k, 0) >= v:
                            continue
                        e.wait_ge(sems[k], v)
                        waited[k] = v
                    if op.fn is None:
                        continue
                    if op.dma:
                        op.fn().then_inc(sems[op.sem_key], 16)
                    else:
                        inst = op.fn()
                        if op.need_inc:
                            inst.then_inc(sems[op.sem_key], 1)
                if engname == "sp":
                    for k in sorted(final_waits, key=str):
                        v = final_waits[k]
                        if waited.get(k, 0) < v:
                            e.wait_ge(sems[k], v)

            @block.tensor
            def _(e):
                run("pe", e)

            @block.scalar
            def _(e):
                run("act", e)

            @block.vector
            def _(e):
                run("dve", e)

            @block.gpsimd
            def _(e):
                run("pool", e)

            @block.sync
            def _(e):
                run("sp", e)


class Arena:
    def __init__(self, tensor, nf32):
        self.t = tensor
        self.n = nf32
        self.off = 0
        self.peak = 0
        self.uid = 0

    def reset(self):
        self.off = 0

    def alloc(self, free, dt, parts=128):
        n = 1
        for f in free:
            n *= f
        nf = n if dt == F32 else (n + 1) // 2
        nf = (nf + 7) // 8 * 8
        assert self.off + nf <= self.n, ("arena overflow", self.off, nf, self.n)
        ap = self.t[0:parts, self.off:self.off + nf]
        self.off += nf
        self.peak = max(self.peak, self.off)
        if dt != F32:
            ap = ap.bitcast(dt)
        ap = ap[:, 0:n]
        if len(free) == 2:
            ap = ap.rearrange("p (a b) -> p a b", b=free[1])
        elif len(free) == 3:
            ap = ap.rearrange("p (a b c) -> p a b c", b=free[1], c=free[2])
        self.uid += 1
        return ap, "ar%d" % self.uid


def _cvec_layout(depth):
    off = {}
    n = 0

    def put(name, cols):
        nonlocal n
        off[name] = n
        n += cols

    for l in range(depth):
        for nm in ("norm_mix", "norm_ca", "norm_mem", "norm_ffn"):
            put((nm, l), 8)
        put(("conv_w", l), 12)
        put(("gate_b", l), 24)
        put(("mu", l), 14)
        for nm in ("w0", "a0", "k_k", "k_a", "r_k", "lnx_w", "lnx_b"):
            put((nm, l), 4)
        put(("ffn_cw", l), 3 * NFC)
        put(("ffn_cb", l), NFC)
    put(("norm_final", 0), 8)
    put(("v0", 1), 4)
    put(("mu_vres", 1), 1)
    return off, n


def _col(v, p=128):
    v = np.asarray(v, np.float32).reshape(-1)
    return np.ascontiguousarray(v.reshape(-1, p).T)


def _build_cvec(inp, depth):
    off, n = _cvec_layout(depth)
    cv = np.zeros((128, n), np.float32)

    def put(key, arr):
        cv[:arr.shape[0], off[key]:off[key] + arr.shape[1]] = arr

    for l in range(depth):
        for nm in ("norm_mix", "norm_ca", "norm_mem", "norm_ffn"):
            put((nm, l), _col(inp[nm][l]))
        put(("conv_w", l), np.concatenate([_col(inp["conv_w"][l][t]) for t in range(3)], axis=1))
        put(("gate_b", l), np.concatenate([_col(inp["gate_b"][l][k]) for k in range(3)], axis=1))
        put(("mu", l), _col(inp["mu_rwkv"][l]))
        for nm in ("w0", "a0", "k_k", "k_a", "lnx_w", "lnx_b"):
            put((nm, l), _col(inp[nm][l]))
        put(("r_k", l), _col(inp["r_k"][l].reshape(-1)))
        put(("ffn_cw", l), np.concatenate([_col(inp["ffn_conv_w"][l][t]) for t in range(3)], axis=1))
        put(("ffn_cb", l), _col(inp["ffn_conv_b"][l]))
    put(("norm_final", 0), _col(inp["norm_final"]))
    if depth > 1:
        put(("v0", 1), _col(inp["v0"][0]))
        put(("mu_vres", 1), np.asarray(inp["mu_vres"][0], np.float32).reshape(32, 1))
    return cv


def _build_consts():
    c = {}
    c["ident"] = np.eye(128, dtype=np.float32)
    s = np.arange(128)
    c["uincl"] = (s[:, None] >= s[None, :]).astype(np.float32)
    c["lstr"] = (s[:, None] < s[None, :]).astype(np.float32)
    c["ones"] = np.ones((128, 128), np.float32)
    blk = np.zeros((128, 128), np.float32)
    blk[:64, :64] = 1
    blk[64:, 64:] = 1
    c["blk"] = blk
    m = np.where(s[:, None] < s[None, :], 0.0, -1.0e5).astype(np.float32)
    c["sbmask"] = np.tile(m, (1, 4))
    u = np.arange(64)
    st = (u[:, None] < u[None, :]).astype(np.float32)
    inc = (u[:, None] <= u[None, :]).astype(np.float32)
    lo = (u[:, None] > u[None, :]).astype(np.float32)
    rm = np.zeros((128, 2048), np.float32)
    rm[:64, 0:512] = np.tile(st, (1, 8))
    rm[:64, 512:1024] = np.tile(inc, (1, 8))
    rm[:64, 1024:1536] = np.tile(lo, (1, 8))
    rm[:64, 1536:2048] = np.tile(np.eye(64, dtype=np.float32), (1, 8))
    c["rmask"] = rm
    return c


def build_program(T=4096, depth=2, use_sb=True, use_rwkv=True, debug=False):
    NT = T // TT
    nc = bass.Bass("TRN2", target_bir_lowering=False)
    cv_off, cv_n = _cvec_layout(depth)

    def din(name, shape, dt=F32):
        return nc.dram_tensor(name, list(shape), dt, kind="ExternalInput").ap()

    def dscr(name, shape, dt):
        return nc.dram_tensor(name, list(shape), dt, kind="Internal").ap()

    xT = din("xT", [D, T])
    memT = din("memT", [D, MEM])
    cvec_d = din("cvec", [128, cv_n])
    w_comb = din("w_comb", [depth, D, COMB])
    w_vres = din("w_vres", [1, D, 32])
    lora_wa = din("lora_wa", [depth, 128, W])
    g_lora = din("g_lora", [depth, 128, W])
    v_lora = din("v_lora", [1, 32, W])
    branch_proj = din("branch_proj", [depth, 3, W, D])
    w_mix_out = din("w_mix_out", [depth, D, D])
    ca_wq = din("ca_wq", [depth, D, D])
    ca_wkv = din("ca_wkv", [depth, D, 2 * D])
    ca_wo = din("ca_wo", [depth, D, D])
    ffn_up = din("ffn_up", [depth, D, 2 * FFN])
    ffn_down = din("ffn_down", [depth, FFN, D])
    c_ident = din("c_ident", [128, 128])
    c_uincl = din("c_uincl", [128, 128])
    c_lstr = din("c_lstr", [128, 128])
    c_ones = din("c_ones", [128, 128])
    c_blk = din("c_blk", [128, 128])
    c_sbmask = din("c_sbmask", [128, 512])
    c_rmask = din("c_rmask", [128, 2048])
    outT = nc.dram_tensor("outT", [D, T], F32, kind="ExternalOutput").ap()
    dbg = None
    if debug:
        dbg = nc.dram_tensor("dbg", [8, D, T], F32, kind="ExternalOutput").ap()

    kcache = dscr("kcache", [depth, 128, 4, T], BF16)
    vcache = dscr("vcache", [depth, T // 128, 128, W], BF16)
    memK_d = dscr("memK", [depth, 128, 8, MEM], BF16)
    memV_d = dscr("memV", [depth, 128, 2, D], BF16)

    with ExitStack() as es:
        def sb(name, shape, dt, ):
            return es.enter_context(nc.sbuf_tensor(name + "_s", shape, dt))

        S = Sched(nc)
        PS = es.enter_context(nc.psum_tensor("PS", [128, 4096], F32))
        PSt = ["ps%d" % i for i in range(8)]

        def bank(i, parts=128, lo=0, hi=512):
            return PS[0:parts, i * 512 + lo:i * 512 + hi]

        hT = sb("hT", [128, 8, TT], F32)
        xn = sb("xn", [128, 8, TT], BF16)
        merged = sb("merged", [128, 8, TT], F32)
        mergedb = sb("mergedb", [128, 8, TT], BF16)
        NWB = 3
        wbuf = [sb("wbuf%d" % i, [128, 8 * 512], BF16) for i in range(NWB)]
        cvec = sb("cvec", [128, cv_n], F32)
        ident = sb("ident", [128, 128], F32)
        uincl = sb("uincl", [128, 128], BF16)
        lstr = sb("lstr", [128, 128], BF16)
        onesb = sb("onesb", [128, 128], BF16)
        blk = sb("blk", [128, 128], F32)
        sbmask = sb("sbmask", [128, 512], F32)
        rmask = sb("rmask", [128, 2048], F32)
        conv_carry = sb("conv_carry", [128, depth * 4 * 2], F32)
        ffn_carry = sb("ffn_carry", [128, depth * NFC * 2], F32)
        rw_carry = sb("rw_carry", [128, depth * 16], F32)
        rw_state = sb("rw_state", [128, depth * 4 * 64], F32)
        vfirst = sb("vfirst", [128, 4, TT], F32)
        omka = sb("omka", [128, depth * 4], F32)
        ones64 = sb("ones64", [128, 64], F32)
        identb = sb("identb", [128, 128], BF16)
        rw_sbf = sb("rw_sbf", [128, depth * 4 * 64], BF16)
        ARENA_F32 = 26 * 1024
        scr = sb("scr", [128, ARENA_F32], F32)
        ar = Arena(scr, ARENA_F32)

        def cvc(key, j=0, n=1, parts=128):
            o = cv_off[key] + j
            return cvec[0:parts, o:o + n]

        S.dma("sp", lambda: nc.sync.dma_start(out=cvec[:], in_=cvec_d), w=["cvec"])
        S.dma("sp", lambda: nc.sync.dma_start(out=ident[:], in_=c_ident), w=["ident"])
        S.dma("sp", lambda: nc.sync.dma_start(out=blk[:], in_=c_blk), w=["blk"])
        S.dma("sp", lambda: nc.sync.dma_start(out=sbmask[:], in_=c_sbmask), w=["sbmask"])
        S.dma("sp", lambda: nc.sync.dma_start(out=rmask[:], in_=c_rmask), w=["rmask"])
        S.dma("pool", lambda: nc.gpsimd.dma_start(out=uincl[:], in_=c_uincl), w=["uincl"])
        S.dma("pool", lambda: nc.gpsimd.dma_start(out=lstr[:], in_=c_lstr), w=["lstr"])
        S.dma("pool", lambda: nc.gpsimd.dma_start(out=onesb[:], in_=c_ones), w=["onesb"])
        S.dve(lambda: nc.vector.memset(conv_carry[:], 0.0), w=["conv_carry"])
        S.dve(lambda: nc.vector.memset(ffn_carry[:], 0.0), w=["ffn_carry"])
        S.dve(lambda: nc.vector.memset(rw_carry[:], 0.0), w=["rw_carry"])
        S.dve(lambda: nc.vector.memset(rw_state[:], 0.0), w=["rw_state"])
        S.dve(lambda: nc.vector.memset(ones64[:], 1.0), w=["ones64"])
        S.dve(lambda: nc.vector.memset(rw_sbf[:], 0.0), w=["rw_sbf"])
        S.dma("pool", lambda: nc.gpsimd.dma_start(out=identb[:], in_=c_ident), w=["identb"])
        for l in range(depth):
            S.dve(lambda l=l: nc.vector.tensor_scalar(out=omka[:, l * 4:l * 4 + 4], in0=cvc(("k_a", l), 0, 4),
                                                      scalar1=-1.0, scalar2=1.0, op0=ALU.mult, op1=ALU.add),
                  r=["cvec"], w=["omka"])

        wstate = {"i": 0}

        def load_w(src_ap, view):
            i = wstate["i"] % NWB
            wstate["i"] += 1
            a, b = src_ap.shape[1], src_ap.shape[2]
            dst = wbuf[i][:, 0:a * b].rearrange("p (a b) -> p a b", b=b)
            tok = "wbuf%d" % i
            S.dma("pool", lambda: nc.gpsimd.dma_start(out=dst, in_=src_ap), w=[tok])
            return dst, tok

        def slab(wd, col0, ncols=512):
            return load_w(wd[:, col0:col0 + ncols].rearrange("(c p) n -> p c n", p=128), None)

        def proj_fm(out_ps, sl, tok, c0, m, ptok, rhs=None, rtok="xn", n=TT, nk=8):
            rhs = xn if rhs is None else rhs

            def f():
                ins = None
                for c in range(nk):
                    ins = nc.tensor.matmul(out_ps, lhsT=sl[:, c, c0:c0 + m], rhs=rhs[:, c, 0:n],
                                           start=(c == 0), stop=(c == nk - 1))
                return ins
            S.pe(f, r=[tok, rtok], w=[ptok])

        def rmsnorm(src, stok, gkey, dst, dtok, n=TT, pb=7):
            sq, sqt = ar.alloc((8, n), BF16)
            rstd, rt = ar.alloc((n,), F32)
            S.act(lambda: nc.scalar.activation(out=sq, in_=src[:, :, 0:n], func=AF.Square), r=[stok], w=[sqt])

            def f():
                ins = None
                for c in range(8):
                    ins = nc.tensor.matmul(bank(pb, hi=n), lhsT=onesb[:, :], rhs=sq[:, c, :], start=(c == 0), stop=(c == 7))
                return ins
            S.pe(f, r=[sqt, "onesb"], w=[PSt[pb]])
            S.act(lambda: nc.scalar.activation(out=rstd, in_=bank(pb, hi=n), func=AF.Ln, scale=1.0 / D, bias=1e-6),
                  r=[PSt[pb]], w=[rt])
            S.act(lambda: nc.scalar.activation(out=rstd, in_=rstd, func=AF.Exp, scale=-0.5), r=[rt], w=[rt])
            for c in range(8):
                S.dve(lambda c=c: nc.vector.scalar_tensor_tensor(out=dst[:, c, 0:n], in0=src[:, c, 0:n], scalar=cvc(gkey, c),
                                                                 in1=rstd, op0=ALU.mult, op1=ALU.mult),
                      r=[stok, rt, "cvec"], w=[dtok])

        def dbg_out(slot, src, stok, i, nchunk=8):
            if dbg is None:
                return
            S.dma("sp", lambda: nc.sync.dma_start(
                out=dbg[slot, 0:nchunk * 128, i * TT:(i + 1) * TT].rearrange("(c p) t -> p c t", p=128), in_=src),
                r=[stok])

        def mem_kv(l):
            ar.reset()
            mt_, mtt = ar.alloc((8, MEM), F32)
            mn, mnt = ar.alloc((8, MEM), BF16)
            S.dma("sp", lambda: nc.sync.dma_start(out=mt_, in_=memT.rearrange("(c p) m -> p c m", p=128)), w=[mtt])
            rmsnorm(mt_, mtt, ("norm_mem", l), mn, mnt, n=MEM)
            kk_, kkt = ar.alloc((8, MEM), BF16)
            vv_, vvt = ar.alloc((2, D), BF16)
            for half in range(2):
                sl, tok = slab(ca_wkv[l], half * 512)
                for dcl in range(4):
                    dc = half * 4 + dcl
                    pb = dcl % 4
                    proj_fm(bank(pb, hi=MEM), sl, tok, dcl * 128, 128, PSt[pb], rhs=mn, rtok=mnt, n=MEM)
                    S.act(lambda dc=dc, pb=pb: nc.scalar.copy(out=kk_[:, dc, :], in_=bank(pb, hi=MEM)), r=[PSt[pb]], w=[kkt])
            for half in range(2):
                sl, tok = slab(ca_wkv[l], D + half * 512)
                for mt in range(2):
                    pb = 4 + mt

                    def f(sl=sl, mt=mt, pb=pb):
                        ins = None
                        for c in range(8):
                            ins = nc.tensor.matmul(bank(pb), lhsT=mn[:, c, mt * 128:(mt + 1) * 128], rhs=sl[:, c, :],
                                                   start=(c == 0), stop=(c == 7))
                        return ins
                    S.pe(f, r=[tok, mnt], w=[PSt[pb]])
                    S.act(lambda mt=mt, half=half, pb=pb: nc.scalar.copy(out=vv_[:, mt, half * 512:(half + 1) * 512], in_=bank(pb)),
                          r=[PSt[pb]], w=[vvt])
            S.dma("sp", lambda l=l: nc.sync.dma_start(out=memK_d[l], in_=kk_), r=[kkt], w=["memK%d" % l])
            S.dma("sp", lambda l=l: nc.sync.dma_start(out=memV_d[l], in_=vv_), r=[vvt], w=["memV%d" % l])
            S.barrier()

        for l in range(depth):
            mem_kv(l)

        def merge_branch(l, n, ys, ystok):
            bp, bptok = load_w(branch_proj[l, n].rearrange("(c p) n -> p c n", p=128), None)
            gsb = [ar.alloc((TT,), F32) for _ in range(2)]
            for half in range(2):
                sl, tok = slab(w_comb[l], OFF_GATE + n * D + half * 512)
                for dcl in range(4):
                    dc = half * 4 + dcl
                    pg = dc % 2
                    pbr = 2 + dc % 2
                    proj_fm(bank(pg), sl, tok, dcl * 128, 128, PSt[pg])
                    gs, gst = gsb[dc % 2]
                    S.act(lambda dc=dc, pg=pg, gs=gs: nc.scalar.activation(out=gs, in_=bank(pg), func=AF.Sigmoid,
                                                                          bias=cvc(("gate_b", l), n * 8 + dc)),
                          r=[PSt[pg], "cvec"], w=[gst])
                    proj_fm(bank(pbr), bp, bptok, dc * 128, 128, PSt[pbr], rhs=ys, rtok=ystok, nk=4)
                    if n == 0:
                        S.dve(lambda dc=dc, pbr=pbr, gs=gs: nc.vector.tensor_tensor(out=merged[:, dc, :], in0=bank(pbr), in1=gs, op=ALU.mult),
                              r=[PSt[pbr], gst], w=["merged%d" % dc])
                    else:
                        S.dve(lambda dc=dc, pbr=pbr, gs=gs: nc.vector.tensor_tensor(out=gs, in0=bank(pbr), in1=gs, op=ALU.mult),
                              r=[PSt[pbr], gst], w=[gst])
                        dst = mergedb if n == 2 else merged
                        dtk = ("mergedb%d" if n == 2 else "merged%d") % dc
                        S.dve(lambda dc=dc, gs=gs, dst=dst: nc.vector.tensor_tensor(out=dst[:, dc, :], in0=merged[:, dc, :], in1=gs, op=ALU.add),
                               r=[gst, "merged%d" % dc], w=[dtk])

        def out_proj_residual(wd, act, atok, nk=8):
            for half in range(2):
                sl, tok = slab(wd, half * 512)
                for dcl in range(4):
                    dc = half * 4 + dcl
                    pb = 4 + dc % 2
                    proj_fm(bank(pb), sl, tok, dcl * 128, 128, PSt[pb], rhs=act, rtok=atok, nk=nk)
                    S.dve(lambda dc=dc, pb=pb: nc.vector.tensor_tensor(out=hT[:, dc, :], in0=hT[:, dc, :], in1=bank(pb), op=ALU.add),
                          r=[PSt[pb], "hT"], w=["hT"])

        def layer_tile(i, l):
            if True:
                S.epoch = 1 + i * depth + l
                ar.reset()
                rmsnorm(hT, "hT", ("norm_mix", l), xn, "xn")
                slb, tb = slab(w_comb[l], 0)
                slc, tcx = slab(w_comb[l], 512)
                slx, tx = slab(w_comb[l], 1024)
                yconv, yct = ar.alloc((4, TT), BF16)
                for j in range(4):
                    proj_fm(bank(0), slb, tb, j * 128, 128, PSt[0])
                    proj_fm(bank(1), slc, tcx, j * 128, 128, PSt[1])
                    proj_fm(bank(2), slx, tx, j * 128, 128, PSt[2])
                    cxb, cxt = ar.alloc((TT + 2,), F32)
                    tmp, tmt = ar.alloc((TT,), F32)
                    cc = conv_carry[:, (l * 4 + j) * 2:(l * 4 + j) * 2 + 2]
                    S.dve(lambda cxb=cxb, cc=cc: nc.vector.tensor_copy(out=cxb[:, 0:2], in_=cc), r=["conv_carry%d_%d" % (l, j)], w=[cxt])
                    S.act(lambda tmp=tmp: nc.scalar.copy(out=tmp, in_=bank(1)), r=[PSt[1]], w=[tmt])
                    S.dve(lambda cxb=cxb, tmp=tmp: nc.vector.tensor_tensor(out=cxb[:, 2:TT + 2], in0=tmp, in1=bank(2), op=ALU.mult),
                          r=[tmt, PSt[2], cxt], w=[cxt])
                    S.dve(lambda cxb=cxb, cc=cc: nc.vector.tensor_copy(out=cc, in_=cxb[:, TT:TT + 2]), r=[cxt], w=["conv_carry%d_%d" % (l, j)])
                    S.dve(lambda cxb=cxb, tmp=tmp, j=j: nc.vector.tensor_scalar(out=tmp, in0=cxb[:, 0:TT], scalar1=cvc(("conv_w", l), j), scalar2=None, op0=ALU.mult),
                          r=[cxt, tmt, "cvec"], w=[tmt])
                    for tap in (1, 2):
                        S.dve(lambda cxb=cxb, tmp=tmp, j=j, tap=tap: nc.vector.scalar_tensor_tensor(
                            out=tmp, in0=cxb[:, tap:tap + TT], scalar=cvc(("conv_w", l), tap * 4 + j), in1=tmp, op0=ALU.mult, op1=ALU.add),
                            r=[cxt, tmt, "cvec"], w=[tmt])
                    S.dve(lambda tmp=tmp, j=j: nc.vector.tensor_tensor(out=yconv[:, j, :], in0=tmp, in1=bank(0), op=ALU.mult),
                          r=[tmt, PSt[0]], w=[yct])
                merge_branch(l, 0, yconv, yct)
                S.barrier()

                ar.reset()
                ysb, ysbt = ar.alloc((4, TT), BF16)
                if use_sb:
                    sb_attention(nc, S, ar, bank, PSt, PS, l, i, xn, w_comb, kcache, vcache, ysb, ysbt, slab, proj_fm,
                                 uincl, lstr, sbmask)
                else:
                    S.dve(lambda: nc.vector.memset(ysb, 0.0), w=[ysbt])
                merge_branch(l, 1, ysb, ysbt)
                S.barrier()

                ar.reset()
                yrw, yrwt = ar.alloc((4, TT), BF16)
                if use_rwkv:
                    rwkv_branch(nc, S, ar, bank, PSt, PS, l, i, xn, w_comb, w_vres, lora_wa, g_lora, v_lora, yrw, yrwt, slab, load_w,
                                proj_fm, cvc, dict(ident=ident, blk=blk, rmask=rmask, rw_carry=rw_carry, rw_state=rw_state,
                                                   vfirst=vfirst, omka=omka, ones64=ones64, identb=identb, rw_sbf=rw_sbf))
                else:
                    S.dve(lambda: nc.vector.memset(yrw, 0.0), w=[yrwt])
                merge_branch(l, 2, yrw, yrwt)
                S.barrier()
                ar.reset()
                out_proj_residual(w_mix_out[l], mergedb, "mergedb_all")
                S.barrier()

                ar.reset()
                rmsnorm(hT, "hT", ("norm_ca", l), xn, "xn")
                qca, qcat = ar.alloc((8, TT), BF16)
                mK, mKt = ar.alloc((8, MEM), BF16)
                mV, mVt = ar.alloc((2, D), BF16)
                oT, oTt = ar.alloc((8, TT), BF16)
                S.dma("sp", lambda l=l: nc.sync.dma_start(out=mK, in_=memK_d[l]), r=["memK%d" % l], w=[mKt])
                S.dma("sp", lambda l=l: nc.sync.dma_start(out=mV, in_=memV_d[l]), r=["memV%d" % l], w=[mVt])
                for half in range(2):
                    sl, tok = slab(ca_wq[l], half * 512)
                    for dcl in range(4):
                        dc = half * 4 + dcl
                        pb = dc % 2
                        proj_fm(bank(pb), sl, tok, dcl * 128, 128, PSt[pb])
                        S.act(lambda dc=dc, pb=pb: nc.scalar.copy(out=qca[:, dc, :], in_=bank(pb)), r=[PSt[pb]], w=[qcat])
                for hh in range(4):
                    ee, eet = ar.alloc((2, TT), BF16)
                    for mt in range(2):
                        pb = 2 + mt

                        def f(hh=hh, mt=mt, pb=pb):
                            ins = None
                            for cl in range(2):
                                c = 2 * hh + cl
                                ins = nc.tensor.matmul(bank(pb), lhsT=mK[:, c, mt * 128:(mt + 1) * 128], rhs=qca[:, c, :],
                                                       start=(cl == 0), stop=(cl == 1))
                            return ins
                        S.pe(f, r=[mKt, qcat], w=[PSt[pb]])
                        S.act(lambda ee=ee, mt=mt, pb=pb: nc.scalar.activation(out=ee[:, mt, :], in_=bank(pb), func=AF.Exp, scale=1.0 / 16.0),
                              r=[PSt[pb]], w=[eet])

                    def fs(ee=ee):
                        nc.tensor.matmul(bank(4), lhsT=onesb[:, :], rhs=ee[:, 0, :], start=True, stop=False)
                        return nc.tensor.matmul(bank(4), lhsT=onesb[:, :], rhs=ee[:, 1, :], start=False, stop=True)
                    S.pe(fs, r=[eet, "onesb"], w=[PSt[4]])
                    rs, rst = ar.alloc((TT,), F32)
                    S.dve(lambda rs=rs: nc.vector.reciprocal(out=rs, in_=bank(4)), r=[PSt[4]], w=[rst])
                    for dv in range(2):
                        pb = 5 + dv

                        def fo(hh=hh, dv=dv, pb=pb, ee=ee):
                            c0 = hh * 256 + dv * 128
                            nc.tensor.matmul(bank(pb), lhsT=mV[:, 0, c0:c0 + 128], rhs=ee[:, 0, :], start=True, stop=False)
                            return nc.tensor.matmul(bank(pb), lhsT=mV[:, 1, c0:c0 + 128], rhs=ee[:, 1, :], start=False, stop=True)
                        S.pe(fo, r=[mVt, eet], w=[PSt[pb]])
                        S.dve(lambda hh=hh, dv=dv, pb=pb, rs=rs: nc.vector.tensor_tensor(out=oT[:, 2 * hh + dv, :], in0=bank(pb), in1=rs, op=ALU.mult),
                              r=[PSt[pb], rst], w=[oTt])
                out_proj_residual(ca_wo[l], oT, oTt)
                S.barrier()

                ar.reset()
                rmsnorm(hT, "hT", ("norm_ffn", l), xn, "xn")
                sg, sgt0 = ar.alloc((22, TT), BF16)
                NFB = 4
                bufs = [ar.alloc((TT + 2,), F32) for _ in range(NFB)]
                us = [ar.alloc((TT,), F32) for _ in range(NFB)]
                pend = []

                def carry_in(c2):
                    buf2, bft2 = bufs[c2 % NFB]
                    car2 = ffn_carry[:, (l * NFC + c2) * 2:(l * NFC + c2) * 2 + 2]
                    S.dve(lambda: nc.vector.tensor_copy(out=buf2[:, 0:2], in_=car2), r=["ffn_carry%d_%d" % (l, c2)], w=[bft2 + "c"])

                for s11 in range(11):
                    sl, tok = slab(ffn_up[l], s11 * 512)
                    for k4 in range(4):
                        cc_ = s11 * 4 + k4
                        pb = cc_ % 4
                        buf, bft = bufs[cc_ % NFB]
                        u, ut = us[cc_ % NFB]
                        car = ffn_carry[:, (l * NFC + cc_) * 2:(l * NFC + cc_) * 2 + 2]
                        bfc = bft + "c"
                        proj_fm(bank(pb), sl, tok, k4 * 128, 128, PSt[pb])
                        if cc_ == 0:
                            carry_in(0)
                        S.act(lambda buf=buf, pb=pb: nc.scalar.copy(out=buf[:, 2:TT + 2], in_=bank(pb)), r=[PSt[pb]], w=[bft])
                        S.dve(lambda buf=buf, car=car: nc.vector.tensor_copy(out=car, in_=buf[:, TT:TT + 2]), r=[bft, bfc], w=["ffn_carry%d_%d" % (l, cc_)])
                        if cc_ + 1 < NFC:
                            carry_in(cc_ + 1)
                        S.act(lambda buf=buf, u=u, cc_=cc_: nc.scalar.activation(
                            out=u, in_=buf[:, 0:TT], func=AF.Identity, scale=cvc(("ffn_cw", l), cc_), bias=cvc(("ffn_cb", l), cc_)),
                            r=[bft, bfc, "cvec"], w=[ut])
                        for tap in (1, 2):
                            S.dve(lambda buf=buf, u=u, cc_=cc_, tap=tap: nc.vector.scalar_tensor_tensor(
                                out=u, in0=buf[:, tap:tap + TT], scalar=cvc(("ffn_cw", l), tap * NFC + cc_), in1=u, op0=ALU.mult, op1=ALU.add),
                                r=[bft, bfc, ut, "cvec"], w=[ut])

                        def tail(u=u, ut=ut, cc_=cc_):
                            if cc_ < 22:
                                S.act(lambda: nc.scalar.activation(out=sg[:, cc_, :], in_=u, func=AF.Silu), r=[ut], w=["sg%d" % cc_])
                            else:
                                jj = cc_ - 22
                                S.dve(lambda: nc.vector.tensor_tensor(out=sg[:, jj, :], in0=sg[:, jj, :], in1=u, op=ALU.mult),
                                      r=[ut, "sg%d" % jj], w=["sg%d" % jj])
                        pend.append(tail)
                        if len(pend) > 1:
                            pend.pop(0)()
                while pend:
                    pend.pop(0)()
                S.barrier()
                for dc in range(8):
                    sl, tok = load_w(ffn_down[l][:, dc * 128:(dc + 1) * 128].rearrange("(c p) n -> p c n", p=128), None)
                    pb = 4 + dc % 2
                    proj_fm(bank(pb), sl, tok, 0, 128, PSt[pb], rhs=sg, rtok="sg_all", nk=22)
                    S.dve(lambda dc=dc, pb=pb: nc.vector.tensor_tensor(out=hT[:, dc, :], in0=hT[:, dc, :], in1=bank(pb), op=ALU.add),
                          r=[PSt[pb], "hT"], w=["hT"])
                S.barrier()
        def tile_begin(i):
            tsl = slice(i * TT, (i + 1) * TT)
            S.dma("sp", lambda: nc.sync.dma_start(out=hT[:], in_=xT[:, tsl].rearrange("(c p) t -> p c t", p=128)), w=["hT"])

        def tile_end(i):
            tsl = slice(i * TT, (i + 1) * TT)
            ar.reset()
            of, oft = ar.alloc((8, TT), F32)
            rmsnorm(hT, "hT", ("norm_final", 0), of, oft)
            S.dma("sp", lambda: nc.sync.dma_start(out=outT[:, tsl].rearrange("(c p) t -> p c t", p=128), in_=of),
                  r=[oft])
            S.barrier()

        for i in range(NT):
            tile_begin(i)
            for l in range(depth):
                layer_tile(i, l)
            tile_end(i)
        S.emit()
        nc._arena_peak = ar.peak
    return nc


def sb_attention(nc, S, ar, bank, PSt, PS, l, i, xn, w_comb, kcache, vcache, ysb, ysbt, slab, proj_fm, uincl, lstr, sbmask):
    qT, qTt = ar.alloc((4, TT), BF16)
    kT, kTt = ar.alloc((4, TT), BF16)
    vt, vtt = ar.alloc((4, W), BF16)
    slq, tq = slab(w_comb[l], OFF_SB)
    slk, tk = slab(w_comb[l], OFF_SB + 512)
    slv, tv = slab(w_comb[l], OFF_SB + 1024)
    for j in range(4):
        proj_fm(bank(j % 2), slq, tq, j * 128, 128, PSt[j % 2])
        S.act(lambda j=j: nc.scalar.copy(out=qT[:, j, :], in_=bank(j % 2)), r=[PSt[j % 2]], w=[qTt])
        proj_fm(bank(2 + j % 2), slk, tk, j * 128, 128, PSt[2 + j % 2])
        S.act(lambda j=j: nc.scalar.copy(out=kT[:, j, :], in_=bank(2 + j % 2)), r=[PSt[2 + j % 2]], w=[kTt])
    for sub in range(4):
        pb = 4 + sub % 2

        def f(sub=sub, pb=pb):
            ins = None
            for c in range(8):
                ins = nc.tensor.matmul(bank(pb), lhsT=xn[:, c, sub * 128:(sub + 1) * 128], rhs=slv[:, c, :], start=(c == 0), stop=(c == 7))
            return ins
        S.pe(f, r=[tv, "xn"], w=[PSt[pb]])
        S.act(lambda sub=sub, pb=pb: nc.scalar.copy(out=vt[:, sub, :], in_=bank(pb)), r=[PSt[pb]], w=[vtt])
    if True:
        S.dma("sp", lambda: nc.sync.dma_start(out=kcache[l, :, :, i * TT:(i + 1) * TT], in_=kT), r=[kTt], w=["kcache"])
        S.dma("sp", lambda: nc.sync.dma_start(out=vcache[l, 4 * i:4 * i + 4].rearrange("s p w -> p s w"), in_=vt), r=[vtt], w=["vcache"])
    NKV = 3
    kst = [ar.alloc((4, 128), BF16) for _ in range(NKV)]
    vst = [ar.alloc((W,), BF16) for _ in range(NKV)]
    zm = [ar.alloc((512,), F32) for _ in range(2)]
    NE = 3
    LA = 2
    Eb = [[ar.alloc((512,), F32) for g in range(2)] for _ in range(NE)]
    SPb = [[ar.alloc((512,), BF16) for g in range(2)] for _ in range(NE)]
    Xb = [ar.alloc((512,), F32) for _ in range(2)]
    attb = [[ar.alloc((512,), BF16) for g in range(2)] for _ in range(2)]
    kvi = [0]
    for qb in range(4):
        I = 4 * i + qb
        qs = slice(qb * 128, (qb + 1) * 128)
        steps = list(range(I, -1, -1))
        n = len(steps)
        info = {}

        def SZ(s_, I=I, qs=qs, steps=steps, info=info):
            c = steps[s_]
            if c >= 4 * i:
                cl = c - 4 * i
                Kc = kT[:, :, cl * 128:(cl + 1) * 128]
                Vc = vt[:, cl, :]
                kt_, vt_ = kTt, vtt
            else:
                (Kc, kt_), (Vc, vt_) = kst[kvi[0] % NKV], vst[kvi[0] % NKV]
                kvi[0] += 1
                S.dma("sp", lambda Kc=Kc, c=c: nc.sync.dma_start(out=Kc, in_=kcache[l, :, :, c * 128:(c + 1) * 128]), r=["kcache"], w=[kt_])
                S.dma("sp", lambda Vc=Vc, c=c: nc.sync.dma_start(out=Vc, in_=vcache[l, c]), r=["vcache"], w=[vt_])
            for g in range(2):
                ztoks = [PSt[2 * g], PSt[2 * g + 1]]

                def fz(g=g, Kc=Kc, qs=qs):
                    ins = None
                    for e in range(2):
                        for jl in range(2):
                            j = 2 * g + jl
                            lo = jl * 128
                            ins = nc.tensor.matmul(bank(2 * g + e, lo=lo, hi=lo + 128),
                                                   lhsT=Kc[e * 64:(e + 1) * 64, j, :], rhs=qT[e * 64:(e + 1) * 64, j, qs],
                                                   start=True, stop=True)
                    return ins
                S.pe(fz, r=[kt_, qTt], w=ztoks)
                zsrc = PS[:, 2 * g * 512:(2 * g + 2) * 512].rearrange("p (b x) -> p b x", b=2)[:, :, 0:256]
                E, Et = Eb[s_ % NE][g]
                SP, SPt = SPb[s_ % NE][g]
                E3 = E.rearrange("p (b x) -> p b x", b=2)
                if c == I:
                    z, zt = zm[g]
                    z3 = z.rearrange("p (b x) -> p b x", b=2)
                    S.dve(lambda z3=z3, zsrc=zsrc: nc.vector.tensor_tensor(out=z3, in0=zsrc, in1=sbmask[:, :].rearrange("p (b x) -> p b x", b=2), op=ALU.add),
                          r=ztoks + ["sbmask"], w=[zt])
                    S.act(lambda E=E, z=z: nc.scalar.activation(out=E, in_=z, func=AF.Exp, scale=0.125), r=[zt], w=[Et])
                else:
                    S.act(lambda E3=E3, zsrc=zsrc: nc.scalar.activation(out=E3, in_=zsrc, func=AF.Exp, scale=0.125), r=ztoks, w=[Et])
                S.act(lambda E=E, SP=SP: nc.scalar.activation(out=SP, in_=E, func=AF.Ln, bias=1.0), r=[Et], w=[SPt])
            info[s_] = (c, Vc, vt_)

        for s_ in range(min(LA, n)):
            SZ(s_)
        for k in range(n):
            pc, Vc, vt_ = info[k]
            for g in range(2):
                pa = 4 + g
                SP, SPt = SPb[k % NE][g]
                S.pe(lambda SP=SP, pa=pa, pc=pc, I=I: nc.tensor.matmul(bank(pa), lhsT=uincl[:, :], rhs=SP, start=(pc == I), stop=(pc == 0), skip_group_check=True),
                     r=[SPt, "uincl", PSt[pa]], w=[PSt[pa]])
            for g in range(2):
                pa = 4 + g
                X, Xt = Xb[g]
                S.act(lambda X=X, pa=pa: nc.scalar.activation(out=X, in_=bank(pa), func=AF.Exp, scale=-1.0), r=[PSt[pa]], w=[Xt])
            if k + LA < n:
                SZ(k + LA)
            if pc > 0:
                for g in range(2):
                    pa = 4 + g
                    SP, SPt = SPb[k % NE][g]
                    S.pe(lambda SP=SP, pa=pa: nc.tensor.matmul(bank(pa), lhsT=lstr[:, :], rhs=SP, start=False, stop=False, skip_group_check=True),
                         r=[SPt, "lstr", PSt[pa]], w=[PSt[pa]])
            for g in range(2):
                E, Et = Eb[k % NE][g]
                X, Xt = Xb[g]
                att, att_t = attb[k % 2][g]
                S.dve(lambda att=att, E=E, X=X: nc.vector.tensor_tensor(out=att, in0=E, in1=X, op=ALU.mult), r=[Et, Xt], w=[att_t])
            for g in range(2):
                att, att_t = attb[k % 2][g]
                po = 6 + g

                def fo(g=g, pc=pc, att=att, Vc=Vc, po=po, I=I):
                    ins = None
                    for e in range(2):
                        for jl in range(2):
                            h = 4 * g + 2 * jl + e
                            ins = nc.tensor.matmul(bank(po, lo=jl * 128, hi=jl * 128 + 128)[e * 64:(e + 1) * 64, :],
                                                   lhsT=Vc[:, h * 64:(h + 1) * 64], rhs=att[:, (e * 2 + jl) * 128:(e * 2 + jl) * 128 + 128],
                                                   start=(pc == I and jl == 0), stop=(pc == 0), skip_group_check=True)
                    return ins
                S.pe(fo, r=[att_t, vt_, PSt[po]], w=[PSt[po]])
                if pc == 0:
                    S.act(lambda g=g, po=po, qs=qs: nc.scalar.copy(out=ysb[:, 2 * g:2 * g + 2, qs],
                                                                    in_=bank(po, hi=256).rearrange("p (a b) -> p a b", b=128)),
                          r=[PSt[po]], w=[ysbt])


def bc_last(x, n):
    return bass.AP(x.tensor, x.offset, [list(x.ap[0]), list(x.ap[1]), [0, n]])


def rwkv_branch(nc, S, ar, bank, PSt, PS, l, i, xn, w_comb, w_vres, lora_wa, g_lora, v_lora, yrw, yrwt, slab, load_w,
                proj_fm, cvc, cs_):
    ident, blk, rmask = cs_["ident"], cs_["blk"], cs_["rmask"]
    rw_carry, rw_state, vfirst, omka, ones64 = cs_["rw_carry"], cs_["rw_state"], cs_["vfirst"], cs_["omka"], cs_["ones64"]
    NCH = TT // CH
    m_st = rmask[0:64, 0:512]
    m_inc = rmask[0:64, 512:1024]
    m_lo = rmask[0:64, 1024:1536]
    m_id = rmask[0:64, 1536:2048]
    Sst = rw_state[:, l * 256:(l + 1) * 256].rearrange("p (j v) -> p j v", v=64)
    identb, rw_sbf = cs_["identb"], cs_["rw_sbf"]
    Sbf = rw_sbf[:, l * 256:(l + 1) * 256].rearrange("p (j v) -> p j v", v=64)

    rt, rtt = ar.alloc((4, TT), BF16)
    kpt, kptt = ar.alloc((4, TT), BF16)
    kt, ktt = ar.alloc((4, TT), BF16)
    bt, btt = ar.alloc((4, TT), BF16)
    vT, vTt = ar.alloc((4, TT), BF16)
    bonus, bont = ar.alloc((4, TT), BF16)
    gC, gCt = ar.alloc((4, NCH), F32)
    xwa, xwat = ar.alloc((TT,), BF16)
    sxg, sxgt = ar.alloc((TT,), BF16)
    xv, xvt = ar.alloc((TT,), BF16)
    lwa, lwat = ar.alloc((W,), BF16)
    glo, glot = ar.alloc((W,), BF16)
    vlo, vlot = ar.alloc((W,), BF16)
    mark = ar.off
    S.dma("pool", lambda: nc.gpsimd.dma_start(out=lwa, in_=lora_wa[l]), w=[lwat])
    S.dma("pool", lambda: nc.gpsimd.dma_start(out=glo, in_=g_lora[l]), w=[glot])
    if l > 0:
        S.dma("pool", lambda: nc.gpsimd.dma_start(out=vlo[0:32, :], in_=v_lora[0]), w=[vlot])

    bufs = [ar.alloc((TT + 1,), F32) for _ in range(3)]
    tmps = [ar.alloc((TT,), F32) for _ in range(3)]
    st = {"n": 0}

    def lerp(pb, parts, cc, mu_ap, dst, dtok):
        k = st["n"] % 3
        st["n"] += 1
        buf, bft = bufs[k]
        tmp, tmt = tmps[k]
        car = rw_carry[0:parts, l * 16 + cc:l * 16 + cc + 1]
        bfc = bft + "c"
        S.dve(lambda: nc.vector.tensor_copy(out=buf[0:parts, 0:1], in_=car), r=["rw_carry%d_%d" % (l, cc)], w=[bfc])
        S.act(lambda: nc.scalar.copy(out=buf[0:parts, 1:TT + 1], in_=bank(pb, parts=parts)), r=[PSt[pb]], w=[bft])
        S.dve(lambda: nc.vector.tensor_copy(out=car, in_=buf[0:parts, TT:TT + 1]), r=[bft, bfc], w=["rw_carry%d_%d" % (l, cc)])
        S.dve(lambda: nc.vector.tensor_tensor(out=tmp[0:parts, :], in0=buf[0:parts, 0:TT], in1=buf[0:parts, 1:TT + 1], op=ALU.subtract),
              r=[bft, bfc], w=[tmt])
        S.dve(lambda: nc.vector.scalar_tensor_tensor(out=dst, in0=tmp[0:parts, :], scalar=mu_ap, in1=buf[0:parts, 1:TT + 1],
                                                     op0=ALU.mult, op1=ALU.add), r=[tmt, bft, "cvec"], w=[dtok])

    sle, te = load_w(w_comb[l][:, OFF_RWKV + 1536:OFF_RWKV + 1792].rearrange("(c p) n -> p c n", p=128), None)
    ex, ext = ar.alloc((TT,), F32)
    proj_fm(bank(0), sle, te, 0, 128, PSt[0])
    lerp(0, 128, 12, cvc(("mu", l), 12), ex, ext)
    S.act(lambda: nc.scalar.activation(out=xwa[0:64, :], in_=ex[0:64, :], func=AF.Tanh), r=[ext], w=[xwat])
    S.act(lambda: nc.scalar.copy(out=xwa[64:128, :], in_=ex[64:128, :]), r=[ext], w=[xwat])
    ex2, ext2 = ar.alloc((TT,), F32)
    proj_fm(bank(1), sle, te, 128, 128, PSt[1])
    lerp(1, 128, 13, cvc(("mu", l), 13), ex2, ext2)
    S.act(lambda: nc.scalar.activation(out=sxg, in_=ex2, func=AF.Sigmoid), r=[ext2], w=[sxgt])
    if l > 0:
        slv_, tv_ = load_w(w_vres[0].rearrange("(c p) n -> p c n", p=128), None)
        ex3, ext3 = ar.alloc((TT,), F32)
        proj_fm(bank(2, parts=32), slv_, tv_, 0, 32, PSt[2])
        lerp(2, 32, 14, cvc(("mu_vres", 1), 0, 1, parts=32), ex3[0:32, :], ext3)
        S.act(lambda: nc.scalar.copy(out=xv[0:32, :], in_=ex3[0:32, :]), r=[ext3], w=[xvt])

    slr, tr = slab(w_comb[l], OFF_RWKV)
    slk, tk = slab(w_comb[l], OFF_RWKV + 512)
    slv, tv = slab(w_comb[l], OFF_RWKV + 1024)
    names = ["r", "k", "v", "sg", "a", "lw", "L", "kk", "t1", "t2", "kap", "kp", "b", "eL", "eNL"]
    dbl = ("r", "k", "v", "sg", "a", "lw", "L")
    T0 = {nm: ar.alloc((TT,), F32) for nm in names}
    T1 = {nm: (ar.alloc((TT,), F32) if nm in dbl else T0[nm]) for nm in names}
    cur = {"T": T0}

    def A(nm):
        return cur["T"][nm][0]

    def K_(nm):
        return cur["T"][nm][1]

    for j in range(4):
        cur["T"] = T0 if j % 2 == 0 else T1
        A = (lambda T: (lambda nm: T[nm][0]))(cur["T"])
        proj_fm(bank(0), slr, tr, j * 128, 128, PSt[0])
        lerp(0, 128, j, cvc(("mu", l), j), A("r"), K_("r"))
        proj_fm(bank(1), slk, tk, j * 128, 128, PSt[1])
        lerp(1, 128, 4 + j, cvc(("mu", l), 4 + j), A("k"), K_("k"))
        proj_fm(bank(2), slv, tv, j * 128, 128, PSt[2])
        lerp(2, 128, 8 + j, cvc(("mu", l), 8 + j), A("v"), K_("v"))
        js = slice(j * 128, (j + 1) * 128)
        S.pe(lambda js=js: nc.tensor.matmul(bank(3), lhsT=lwa[0:64, js], rhs=xwa[0:64, :], start=True, stop=True), r=[lwat, xwat], w=[PSt[3]])
        S.act(lambda j=j, A=A: nc.scalar.activation(out=A("sg"), in_=bank(3), func=AF.Sigmoid, bias=cvc(("w0", l), j)), r=[PSt[3], "cvec"], w=[K_("sg")])
        S.dve(lambda A=A: nc.vector.tensor_scalar(out=A("lw"), in0=A("sg"), scalar1=-DS, scalar2=None, op0=ALU.mult), r=[K_("sg")], w=[K_("lw")])
        for ch in range(NCH):
            cs = slice(ch * CH, (ch + 1) * CH)
            S.dve(lambda cs=cs, A=A: nc.vector.tensor_tensor_scan(out=A("L")[:, cs], data0=ones64[:, :], data1=A("lw")[:, cs], initial=0.0,
                                                             op0=ALU.mult, op1=ALU.add), r=[K_("lw"), "ones64"], w=[K_("L")])
        S.pe(lambda js=js: nc.tensor.matmul(bank(4), lhsT=lwa[64:128, js], rhs=xwa[64:128, :], start=True, stop=True), r=[lwat, xwat], w=[PSt[4]])
        S.act(lambda j=j, A=A: nc.scalar.activation(out=A("a"), in_=bank(4), func=AF.Sigmoid, bias=cvc(("a0", l), j)), r=[PSt[4], "cvec"], w=[K_("a")])
        if l > 0:
            S.pe(lambda js=js: nc.tensor.matmul(bank(5), lhsT=vlo[0:32, js], rhs=xv[0:32, :], start=True, stop=True), r=[vlot, xvt], w=[PSt[5]])
            S.act(lambda j=j, A=A: nc.scalar.activation(out=A("t1"), in_=bank(5), func=AF.Sigmoid, bias=cvc(("v0", 1), j)), r=[PSt[5], "cvec"], w=[K_("t1")])
            S.dve(lambda j=j, A=A: nc.vector.tensor_tensor(out=A("t2"), in0=vfirst[:, j, :], in1=A("v"), op=ALU.subtract), r=["vfirst", K_("v")], w=[K_("t2")])
            S.dve(lambda A=A: nc.vector.tensor_tensor(out=A("t2"), in0=A("t2"), in1=A("t1"), op=ALU.mult), r=[K_("t2"), K_("t1")], w=[K_("t2")])
            S.dve(lambda j=j, A=A: nc.vector.tensor_tensor(out=vT[:, j, :], in0=A("v"), in1=A("t2"), op=ALU.add), r=[K_("v"), K_("t2")], w=[vTt])
        else:
            S.act(lambda j=j, A=A: nc.scalar.copy(out=vT[:, j, :], in_=A("v")), r=[K_("v")], w=[vTt])
            S.act(lambda j=j, A=A: nc.scalar.copy(out=vfirst[:, j, :], in_=A("v")), r=[K_("v")], w=["vfirst"])
        S.dve(lambda j=j, A=A: nc.vector.tensor_scalar(out=A("kk"), in0=A("k"), scalar1=cvc(("k_k", l), j), scalar2=None, op0=ALU.mult), r=[K_("k"), "cvec"], w=[K_("kk")])
        S.dve(lambda A=A: nc.vector.tensor_tensor(out=A("t1"), in0=A("kk"), in1=A("kk"), op=ALU.mult), r=[K_("kk"), K_("t1")], w=[K_("t1")])
        S.pe(lambda A=A: nc.tensor.matmul(bank(6), lhsT=blk[:, :], rhs=A("t1"), start=True, stop=True), r=["blk", K_("t1")], w=[PSt[6]])
        S.act(lambda A=A: nc.scalar.activation(out=A("t2"), in_=bank(6), func=AF.Ln, bias=1e-24), r=[PSt[6], K_("t2")], w=[K_("t2")])
        S.act(lambda A=A: nc.scalar.activation(out=A("t2"), in_=A("t2"), func=AF.Exp, scale=-0.5), r=[K_("t2")], w=[K_("t2")])
        S.dve(lambda A=A: nc.vector.tensor_tensor(out=A("kap"), in0=A("kk"), in1=A("t2"), op=ALU.mult), r=[K_("kk"), K_("t2")], w=[K_("kap")])
        S.dve(lambda j=j, A=A: nc.vector.tensor_scalar(out=A("t1"), in0=A("a"), scalar1=cvc(("k_a", l), j), scalar2=omka[:, l * 4 + j:l * 4 + j + 1],
                                                  op0=ALU.mult, op1=ALU.add), r=[K_("a"), "cvec", "omka", K_("t1")], w=[K_("t1")])
        S.dve(lambda A=A: nc.vector.tensor_tensor(out=A("kp"), in0=A("k"), in1=A("t1"), op=ALU.mult), r=[K_("k"), K_("t1")], w=[K_("kp")])
        S.dve(lambda A=A: nc.vector.tensor_tensor(out=A("b"), in0=A("kap"), in1=A("a"), op=ALU.mult), r=[K_("kap"), K_("a")], w=[K_("b")])
        S.dve(lambda j=j, A=A: nc.vector.scalar_tensor_tensor(out=A("t2"), in0=A("r"), scalar=cvc(("r_k", l), j), in1=A("kp"), op0=ALU.mult, op1=ALU.mult),
              r=[K_("r"), K_("kp"), "cvec", K_("t2")], w=[K_("t2")])
        S.pe(lambda A=A: nc.tensor.matmul(bank(7), lhsT=blk[:, :], rhs=A("t2"), start=True, stop=True), r=["blk", K_("t2")], w=[PSt[7]])
        S.dve(lambda j=j: nc.vector.tensor_tensor(out=bonus[:, j, :], in0=bank(7), in1=vT[:, j, :], op=ALU.mult), r=[PSt[7], vTt], w=[bont])
        S.act(lambda A=A: nc.scalar.activation(out=A("eL"), in_=A("L"), func=AF.Exp), r=[K_("L")], w=[K_("eL")])
        S.act(lambda A=A: nc.scalar.activation(out=A("eNL"), in_=A("L"), func=AF.Exp, scale=-1.0), r=[K_("L")], w=[K_("eNL")])
        S.dve(lambda A=A: nc.vector.tensor_tensor(out=A("t1"), in0=A("L"), in1=A("lw"), op=ALU.subtract), r=[K_("L"), K_("lw"), K_("t1")], w=[K_("t1")])
        S.act(lambda A=A: nc.scalar.activation(out=A("t1"), in_=A("t1"), func=AF.Exp), r=[K_("t1")], w=[K_("t1")])
        S.dve(lambda j=j, A=A: nc.vector.tensor_copy(out=gC[:, j, :], in_=A("eL").rearrange("p (c x) -> p c x", x=CH)[:, :, CH - 1]), r=[K_("eL")], w=[gCt])
        S.dve(lambda j=j, A=A: nc.vector.tensor_tensor(out=rt[:, j, :], in0=A("r"), in1=A("eL"), op=ALU.mult), r=[K_("r"), K_("eL")], w=[rtt])
        S.dve(lambda j=j, A=A: nc.vector.tensor_tensor(out=kpt[:, j, :], in0=A("kap"), in1=A("t1"), op=ALU.mult), r=[K_("kap"), K_("t1")], w=[kptt])
        S.dve(lambda j=j, A=A: nc.vector.tensor_tensor(out=kt[:, j, :], in0=A("kp"), in1=A("eNL"), op=ALU.mult), r=[K_("kp"), K_("eNL")], w=[ktt])
        S.dve(lambda j=j, A=A: nc.vector.tensor_tensor(out=bt[:, j, :], in0=A("b"), in1=A("eNL"), op=ALU.mult), r=[K_("b"), K_("eNL")], w=[btt])
    S.barrier()
    ar.off = mark

    def t64(dt=BF16):
        return ar.alloc((512,), dt)

    P0, P0t = t64()
    P0T, P0Tt = t64()
    P1, P1t = t64()
    P1T, P1Tt = t64()
    MT, MTt = t64()
    AkkT, AkkTt = t64()
    ArkT, ArkTt = t64()
    nArbT, nArbTt = t64()
    Vt, Vtt = t64()
    Kto, Ktot = t64()
    nBto, nBtot = t64()
    KPto, KPtot = t64()
    X1, X1t = t64()
    W1, W1t = t64(F32)
    KtT, KtTt = ar.alloc((4, 64), BF16)
    U, Ut = t64()
    Ors, Orst = t64(F32)
    O, Ot = t64(F32)
    Osq, Osqt = t64(F32)
    Obf, Obft = t64(BF16)
    stat, statt = ar.alloc((32,), F32)
    tmpS, tmpSt = ar.alloc((4, 64), F32)
    gnT, gnTt = ar.alloc((4, TT), F32)

    def esplit(b0):
        return PS[0:64, b0 * 512:(b0 + 2) * 512].rearrange("p (e x) -> p e x", e=2)[:, :, 0:256]

    def v3(x):
        return x[0:64, :].rearrange("p (e x) -> p e x", e=2)

    def hbv(x, hb):
        return x[0:64, hb * 64:(hb + 1) * 64]

    def amat(b0, lhs, rhs, cs, ltok, rtok):
        def f():
            ins = None
            for e in range(2):
                for j in range(4):
                    ins = nc.tensor.matmul(bank(b0 + e, parts=64, lo=j * 64, hi=j * 64 + 64),
                                           lhsT=lhs[e * 64:(e + 1) * 64, j, cs], rhs=rhs[e * 64:(e + 1) * 64, j, cs], start=True, stop=True)
            return ins
        S.pe(f, r=[ltok, rtok], w=[PSt[b0], PSt[b0 + 1]])

    def bmm(b, lhs, rhs, ltok, rtok):
        def f():
            ins = None
            for hb in range(8):
                ins = nc.tensor.matmul(bank(b, parts=64, lo=hb * 64, hi=hb * 64 + 64), lhsT=hbv(lhs, hb), rhs=hbv(rhs, hb), start=True, stop=True)
            return ins
        S.pe(f, r=[ltok, rtok], w=[PSt[b]])

    def tpose(b, src, stok, cs):
        def f():
            ins = None
            ov = bank(b, parts=64).rearrange("p (e j k) -> p e j k", e=2, j=4)
            for j in range(4):
                ins = nc.tensor.matmul(ov[:, :, j, :], lhsT=src[:, j, cs], rhs=identb[:, :].rearrange("p (e k) -> p e k", e=2), start=True, stop=True)
            return ins
        S.pe(f, r=[stok, "identb"], w=[PSt[b]])

    for ch in range(NCH):
        cs = slice(ch * CH, (ch + 1) * CH)
        amat(0, bt, kpt, cs, btt, kptt)
        amat(2, kpt, bt, cs, kptt, btt)
        S.dve(lambda: nc.vector.tensor_tensor(out=v3(P0T), in0=esplit(0), in1=v3(m_st), op=ALU.mult), r=[PSt[0], PSt[1], "rmask"], w=[P0Tt])
        S.dve(lambda: nc.vector.tensor_tensor(out=v3(P0), in0=esplit(2), in1=v3(m_lo), op=ALU.mult), r=[PSt[2], PSt[3], "rmask"], w=[P0t])
        S.dve(lambda: nc.vector.scalar_tensor_tensor(out=MT[0:64, :], in0=P0T[0:64, :], scalar=-1.0, in1=m_id, op0=ALU.mult, op1=ALU.add),
              r=[P0Tt, "rmask"], w=[MTt])
        Pc, Pct, PcT, PcTt = P0, P0t, P0T, P0Tt
        Pn, Pnt, PnT, PnTt = P1, P1t, P1T, P1Tt
        for k in range(1, 6):
            bmm(4, PcT, Pc, PcTt, Pct)
            if k < 5:
                bmm(5, Pc, PcT, Pct, PcTt)
            S.act(lambda Pn=Pn: nc.scalar.copy(out=Pn[0:64, :], in_=bank(4, parts=64)), r=[PSt[4]], w=[Pnt])
            if k < 5:
                S.dve(lambda PnT=PnT: nc.vector.tensor_copy(out=PnT[0:64, :], in_=bank(5, parts=64)), r=[PSt[5]], w=[PnTt])
            bmm(6, Pn, MT, Pnt, MTt)
            S.dve(lambda: nc.vector.tensor_tensor(out=MT[0:64, :], in0=MT[0:64, :], in1=bank(6, parts=64), op=ALU.add), r=[PSt[6], MTt], w=[MTt])
            Pc, Pct, PcT, PcTt, Pn, Pnt, PnT, PnTt = Pn, Pnt, PnT, PnTt, Pc, Pct, PcT, PcTt
        amat(0, kt, kpt, cs, ktt, kptt)
        S.dve(lambda: nc.vector.tensor_tensor(out=v3(AkkT), in0=esplit(0), in1=v3(m_st), op=ALU.mult), r=[PSt[0], PSt[1], "rmask"], w=[AkkTt])
        amat(2, kt, rt, cs, ktt, rtt)
        S.dve(lambda: nc.vector.tensor_tensor(out=v3(ArkT), in0=esplit(2), in1=v3(m_inc), op=ALU.mult), r=[PSt[2], PSt[3], "rmask"], w=[ArkTt])
        amat(0, bt, rt, cs, btt, rtt)
        S.dve(lambda: nc.vector.scalar_tensor_tensor(out=v3(nArbT), in0=esplit(0), scalar=-1.0, in1=v3(m_inc), op0=ALU.mult, op1=ALU.mult),
              r=[PSt[0], PSt[1], "rmask"], w=[nArbTt])
        tpose(4, vT, vTt, cs)
        S.act(lambda: nc.scalar.copy(out=Vt[0:64, :], in_=bank(4, parts=64)), r=[PSt[4]], w=[Vtt])
        tpose(5, kt, ktt, cs)
        S.act(lambda: nc.scalar.copy(out=Kto[0:64, :], in_=bank(5, parts=64)), r=[PSt[5]], w=[Ktot])
        tpose(6, bt, btt, cs)
        S.act(lambda: nc.scalar.mul(out=nBto[0:64, :], in_=bank(6, parts=64), mul=-1.0), r=[PSt[6]], w=[nBtot])
        tpose(7, kpt, kptt, cs)
        S.act(lambda: nc.scalar.copy(out=KPto[0:64, :], in_=bank(7, parts=64)), r=[PSt[7]], w=[KPtot])
        bmm(0, AkkT, Vt, AkkTt, Vtt)
        S.act(lambda: nc.scalar.copy(out=X1[0:64, :], in_=bank(0, parts=64)), r=[PSt[0]], w=[X1t])
        bmm(1, MT, X1, MTt, X1t)
        S.act(lambda: nc.scalar.copy(out=W1[0:64, :], in_=bank(1, parts=64)), r=[PSt[1]], w=[W1t])

        def fk():
            ins = None
            for e in range(2):
                for j in range(4):
                    hb = e * 4 + j
                    ins = nc.tensor.matmul(PS[e * 64:(e + 1) * 64, 2 * 512 + j * 64:2 * 512 + j * 64 + 64], lhsT=hbv(KPto, hb), rhs=hbv(MT, hb),
                                           start=True, stop=True)
            return ins
        S.pe(fk, r=[KPtot, MTt], w=[PSt[2]])
        S.act(lambda: nc.scalar.copy(out=KtT, in_=bank(2, hi=256).rearrange("p (j t) -> p j t", t=64)), r=[PSt[2]], w=[KtTt])
        def fu():
            ins = None
            for e in range(2):
                for j in range(4):
                    ins = nc.tensor.matmul(bank(4 + e, parts=64, lo=j * 64, hi=j * 64 + 64), lhsT=KtT[e * 64:(e + 1) * 64, j, :],
                                           rhs=Sbf[e * 64:(e + 1) * 64, j, :], start=True, stop=True)
            return ins
        S.pe(fu, r=[KtTt, "rw_sbf"], w=[PSt[4], PSt[5]])
        S.dve(lambda: nc.vector.tensor_tensor(out=v3(U), in0=esplit(4), in1=v3(W1), op=ALU.add), r=[PSt[4], PSt[5], W1t], w=[Ut])
        def fo1(cs=cs):
            ins = None
            for e in range(2):
                for j in range(4):
                    ins = nc.tensor.matmul(bank(6 + e, parts=64, lo=j * 64, hi=j * 64 + 64), lhsT=rt[e * 64:(e + 1) * 64, j, cs],
                                           rhs=Sbf[e * 64:(e + 1) * 64, j, :], start=True, stop=True)
            return ins
        S.pe(fo1, r=[rtt, "rw_sbf"], w=[PSt[6], PSt[7]])
        S.act(lambda: nc.scalar.copy(out=v3(Ors), in_=esplit(6)), r=[PSt[6], PSt[7]], w=[Orst])

        def fo2():
            ins = None
            for hb in range(8):
                nc.tensor.matmul(bank(0, parts=64, lo=hb * 64, hi=hb * 64 + 64), lhsT=hbv(ArkT, hb), rhs=hbv(Vt, hb), start=(hb == 0), stop=False,
                                 skip_group_check=True)
                ins = nc.tensor.matmul(bank(0, parts=64, lo=hb * 64, hi=hb * 64 + 64), lhsT=hbv(nArbT, hb), rhs=hbv(U, hb), start=False, stop=True,
                                       skip_group_check=True)
            return ins
        S.pe(fo2, r=[ArkTt, Vtt, nArbTt, Ut], w=[PSt[0]])
        S.dve(lambda: nc.vector.tensor_tensor(out=O[0:64, :].rearrange("p (j e v) -> p e j v", j=4, e=2),
                                              in0=bank(0, parts=64).rearrange("p (e j v) -> p e j v", e=2, j=4),
                                              in1=Ors[0:64, :].rearrange("p (e j v) -> p e j v", e=2, j=4), op=ALU.add), r=[PSt[0], Orst], w=[Ot])
        def fs():
            ins = None
            for e in range(2):
                for j in range(4):
                    hb = e * 4 + j
                    o = PS[e * 64:(e + 1) * 64, 1 * 512 + j * 64:1 * 512 + j * 64 + 64]
                    nc.tensor.matmul(o, lhsT=hbv(Kto, hb), rhs=hbv(Vt, hb), start=(j == 0), stop=False, skip_group_check=True)
                    ins = nc.tensor.matmul(o, lhsT=hbv(nBto, hb), rhs=hbv(U, hb), start=False, stop=True, skip_group_check=True)
            return ins
        S.pe(fs, r=[Ktot, Vtt, nBtot, Ut], w=[PSt[1]])
        S.dve(lambda: nc.vector.tensor_tensor(out=tmpS, in0=Sst, in1=bank(1, hi=256).rearrange("p (j v) -> p j v", v=64), op=ALU.add),
              r=["rw_state", PSt[1]], w=[tmpSt])
        S.dve(lambda ch=ch: nc.vector.tensor_tensor(out=Sst, in0=tmpS, in1=bc_last(gC[:, :, ch], 64), op=ALU.mult), r=[tmpSt, gCt, "rw_state"], w=["rw_state"])
        S.act(lambda: nc.scalar.copy(out=Sbf, in_=Sst), r=["rw_state"], w=["rw_sbf"])
        O3 = O[0:64, :].rearrange("p (h v) -> p h v", v=64)
        Q3 = Osq[0:64, :].rearrange("p (h v) -> p h v", v=64)
        mean = stat[0:64, 0:8]
        ex2_ = stat[0:64, 8:16]
        var = stat[0:64, 16:24]
        rstd = stat[0:64, 24:32]
        S.dve(lambda: nc.vector.tensor_reduce(out=mean, in_=O3, op=ALU.add, axis=AX.X), r=[Ot], w=[statt])
        S.act(lambda: nc.scalar.activation(out=Osq[0:64, :], in_=O[0:64, :], func=AF.Square), r=[Ot], w=[Osqt])
        S.dve(lambda: nc.vector.tensor_reduce(out=ex2_, in_=Q3, op=ALU.add, axis=AX.X), r=[Osqt, statt], w=[statt])
        S.dve(lambda: nc.vector.tensor_scalar(out=mean, in0=mean, scalar1=1.0 / 64, scalar2=None, op0=ALU.mult), r=[statt], w=[statt])
        S.dve(lambda: nc.vector.tensor_tensor(out=var, in0=mean, in1=mean, op=ALU.mult), r=[statt], w=[statt])
        S.dve(lambda: nc.vector.scalar_tensor_tensor(out=var, in0=ex2_, scalar=1.0 / 64, in1=var, op0=ALU.mult, op1=ALU.subtract), r=[statt], w=[statt])
        S.act(lambda: nc.scalar.activation(out=rstd, in_=var, func=AF.Ln, bias=64e-5), r=[statt], w=[statt])
        S.act(lambda: nc.scalar.activation(out=rstd, in_=rstd, func=AF.Exp, scale=-0.5), r=[statt], w=[statt])
        S.dve(lambda: nc.vector.tensor_tensor(out=O3, in0=O3, in1=bc_last(mean, 64), op=ALU.subtract), r=[Ot, statt], w=[Ot])
        S.dve(lambda: nc.vector.tensor_tensor(out=Obf[0:64, :].rearrange("p (h v) -> p h v", v=64), in0=O3, in1=bc_last(rstd, 64), op=ALU.mult),
              r=[Ot, statt], w=[Obft])
        def fT():
            ins = None
            for j in range(4):
                ins = nc.tensor.matmul(bank(3, lo=j * 64, hi=j * 64 + 64), lhsT=Obf[0:64, j * 128:(j + 1) * 128], rhs=identb[0:64, 0:64], start=True, stop=True)
            return ins
        S.pe(fT, r=[Obft, "identb"], w=[PSt[3]])
        S.act(lambda cs=cs: nc.scalar.copy(out=gnT[:, :, cs], in_=bank(3, hi=256).rearrange("p (j t) -> p j t", t=64)), r=[PSt[3]], w=[gnTt])
    for j in range(4):
        js = slice(j * 128, (j + 1) * 128)
        pb = 4 + j % 2
        S.pe(lambda js=js, pb=pb: nc.tensor.matmul(bank(pb), lhsT=glo[:, js], rhs=sxg, start=True, stop=True), r=[glot, sxgt], w=[PSt[pb]])
        S.dve(lambda j=j: nc.vector.tensor_scalar(out=gnT[:, j, :], in0=gnT[:, j, :], scalar1=cvc(("lnx_w", l), j), scalar2=cvc(("lnx_b", l), j),
                                                  op0=ALU.mult, op1=ALU.add), r=[gnTt, "cvec"], w=[gnTt])
        S.dve(lambda j=j: nc.vector.tensor_tensor(out=gnT[:, j, :], in0=gnT[:, j, :], in1=bonus[:, j, :], op=ALU.add), r=[gnTt, bont], w=[gnTt])
        S.dve(lambda j=j, pb=pb: nc.vector.tensor_tensor(out=yrw[:, j, :], in0=gnT[:, j, :], in1=bank(pb), op=ALU.mult), r=[gnTt, PSt[pb]], w=[yrwt])


_CACHE = {}


def _prep_inputs(inp, b, depth, T):
    f = lambda a: np.ascontiguousarray(np.asarray(a, np.float32))
    m = {}
    m["xT"] = f(np.asarray(inp["x"][b])[:T].T)
    m["memT"] = f(np.asarray(inp["mem"][b]).T)
    return m


def _shared_inputs(inp, depth):
    f = lambda a: np.ascontiguousarray(np.asarray(a, np.float32))
    m = {}
    m["cvec"] = _build_cvec(inp, depth)
    for k in ("w_comb", "branch_proj", "w_mix_out", "ca_wq", "ca_wkv", "ca_wo", "ffn_up", "ffn_down", "g_lora"):
        m[k] = f(np.asarray(inp[k])[:depth])
    m["w_vres"] = f(inp["w_vres"])
    m["v_lora"] = f(inp["v_lora"])
    m["lora_wa"] = f(np.concatenate([np.asarray(inp["w_lora"])[:depth], np.asarray(inp["a_lora"])[:depth]], axis=1))
    for k, v in _build_consts().items():
        m["c_" + k] = v
    return m


def kernel(**inputs):
    depth, T = 2, 4096
    key = (depth, T)
    if key not in _CACHE:
        _CACHE[key] = build_program(T=T, depth=depth)
    nc = _CACHE[key]
    shared = _shared_inputs(inputs, depth)
    in_maps = []
    for b in range(8):
        m = dict(shared)
        m.update(_prep_inputs(inputs, b, depth, T))
        in_maps.append(m)
    res = run_bass_kernel_spmd(nc, in_maps, core_ids=list(range(8)))
    out = np.stack([np.asarray(r["outT"]).T for r in res.results], axis=0)
    return np.ascontiguousarray(out.astype(np.float32))
```

```python
import math
from contextlib import ExitStack
import numpy as np
import concourse.bass as bass
import concourse.mybir as mybir
from concourse.bass_utils import run_bass_kernel_spmd

F32 = mybir.dt.float32
BF16 = mybir.dt.bfloat16
AF = mybir.ActivationFunctionType
ALU = mybir.AluOpType
AX = mybir.AxisListType

D = 1024
MEM = 256
W = 512
TT = 512
FFN = 2816
NFC = 44
COMB = 7936
OFF_SB = 1536
OFF_GATE = 3072
OFF_RWKV = 6144
DS = math.exp(-0.5)
CH = 64


class _Op:
    __slots__ = ("eng", "fn", "deps", "epoch", "dma", "need_inc", "count", "sem_key", "idx")


class Sched:
    ENGS = ("pe", "act", "dve", "pool", "sp")
    NQ = 8

    def __init__(self, nc):
        self.nc = nc
        self.ops = []
        self.last_writer = {}
        self.readers = {}
        self.epoch = 0
        self.bar_start = 0

    def add(self, eng, fn, reads=(), writes=(), dma=False):
        op = _Op()
        op.idx = len(self.ops)
        op.eng, op.fn, op.epoch, op.dma = eng, fn, self.epoch, dma
        deps = set()
        for t in reads:
            w = self.last_writer.get(t)
            if w is not None:
                deps.add(w)
        for t in writes:
            deps.update(self.readers.get(t, ()))
            w = self.last_writer.get(t)
            if w is not None:
                deps.add(w)
        for t in reads:
            self.readers.setdefault(t, []).append(op.idx)
        for t in writes:
            self.last_writer[t] = op.idx
            self.readers[t] = []
        deps.discard(op.idx)
        op.deps = deps
        op.need_inc = False
        self.ops.append(op)
        return op.idx

    def pe(self, fn, r=(), w=()):
        return self.add("pe", fn, r, w)

    def act(self, fn, r=(), w=()):
        return self.add("act", fn, r, w)

    def dve(self, fn, r=(), w=()):
        return self.add("dve", fn, r, w)

    def pool(self, fn, r=(), w=()):
        return self.add("pool", fn, r, w)

    def dma(self, eng, fn, r=(), w=()):
        return self.add(eng, fn, r, w, dma=True)

    def barrier(self):
        last = {}
        deps = set()
        for op in self.ops[self.bar_start:]:
            if op.dma:
                deps.add(op.idx)
            else:
                last[op.eng] = op.idx
        deps.update(last.values())
        for eng in self.ENGS:
            i = self.add(eng, None)
            self.ops[i].deps = set(deps)
        self.bar_start = len(self.ops)
        self.last_writer.clear()
        self.readers.clear()

    def emit(self):
        nc = self.nc
        ops = self.ops
        for op in ops:
            for d in op.deps:
                ops[d].need_inc = True
        counters = {}
        dma_counters = {}
        sem_keys = set()
        for op in ops:
            if op.dma:
                j = dma_counters.get(op.eng, 0)
                dma_counters[op.eng] = j + 1
                op.sem_key = ("dma", op.eng, j % self.NQ)
                op.count = 16 * (j // self.NQ + 1)
                sem_keys.add(op.sem_key)
            else:
                key = ("c", op.eng, op.epoch)
                op.sem_key = key
                if op.need_inc and op.fn is not None:
                    counters[key] = counters.get(key, 0) + 1
                    sem_keys.add(key)
                op.count = counters.get(key, 0)
        final_waits = {}
        for op in ops:
            if op.dma:
                k = op.sem_key
                final_waits[k] = max(final_waits.get(k, 0), op.count)
        self.n_sems = len(sem_keys)
        with ExitStack() as es:
            sems = {}
            for i, k in enumerate(sorted(sem_keys, key=str)):
                sems[k] = es.enter_context(nc.semaphore("s%d" % i))
            block = es.enter_context(nc.Block())

            def run(engname, e):
                waited = {}
                for op in ops:
                    if op.eng != engname:
                        continue
                    need = {}
                    for d in op.deps:
                        dop = ops[d]
                        if dop.count <= 0:
                            continue
                        k = dop.sem_key
                        if need.get(k, 0) < dop.count:
                            need[k] = dop.count
                    if op.dma:
                        prev = op.count - 16
                        if prev > 0 and need.get(op.sem_key, 0) < prev:
                            need[op.sem_key] = prev
                    for k in sorted(need, key=str):
                        v = need[k]
                        if waited.get(k, 0) >= v:
                            continue
                        e.wait_ge(sems[k], v)
                        waited[k] = v
                    if op.fn is None:
                        continue
                    if op.dma:
                        op.fn().then_inc(sems[op.sem_key], 16)
                    else:
                        inst = op.fn()
                        if op.need_inc:
                            inst.then_inc(sems[op.sem_key], 1)
                if engname == "sp":
                    for k in sorted(final_waits, key=str):
                        v = final_waits[k]
                        if waited.get(k, 0) < v:
                            e.wait_ge(sems[k], v)

            @block.tensor
            def _(e):
                run("pe", e)

            @block.scalar
            def _(e):
                run("act", e)

            @block.vector
            def _(e):
                run("dve", e)

            @block.gpsimd
            def _(e):
                run("pool", e)

            @block.sync
            def _(e):
                run("sp", e)


class Arena:
    def __init__(self, tensor, nf32):
        self.t = tensor
        self.n = nf32
        self.off = 0
        self.peak = 0
        self.uid = 0

    def reset(self):
        self.off = 0

    def alloc(self, free, dt, parts=128):
        n = 1
        for f in free:
            n *= f
        nf = n if dt == F32 else (n + 1) // 2
        nf = (nf + 7) // 8 * 8
        assert self.off + nf <= self.n, ("arena overflow", self.off, nf, self.n)
        ap = self.t[0:parts, self.off:self.off + nf]
        self.off += nf
        self.peak = max(self.peak, self.off)
        if dt != F32:
            ap = ap.bitcast(dt)
        ap = ap[:, 0:n]
        if len(free) == 2:
            ap = ap.rearrange("p (a b) -> p a b", b=free[1])
        elif len(free) == 3:
            ap = ap.rearrange("p (a b c) -> p a b c", b=free[1], c=free[2])
        self.uid += 1
        return ap, "ar%d" % self.uid


def _cvec_layout(depth):
    off = {}
    n = 0

    def put(name, cols):
        nonlocal n
        off[name] = n
        n += cols

    for l in range(depth):
        for nm in ("norm_mix", "norm_ca", "norm_mem", "norm_ffn"):
            put((nm, l), 8)
        put(("conv_w", l), 12)
        put(("gate_b", l), 24)
        put(("mu", l), 14)
        for nm in ("w0", "a0", "k_k", "k_a", "r_k", "lnx_w", "lnx_b"):
            put((nm, l), 4)
        put(("ffn_cw", l), 3 * NFC)
        put(("ffn_cb", l), NFC)
    put(("norm_final", 0), 8)
    put(("v0", 1), 4)
    put(("mu_vres", 1), 1)
    return off, n


def _col(v, p=128):
    v = np.asarray(v, np.float32).reshape(-1)
    return np.ascontiguousarray(v.reshape(-1, p).T)


def _build_cvec(inp, depth):
    off, n = _cvec_layout(depth)
    cv = np.zeros((128, n), np.float32)

    def put(key, arr):
        cv[:arr.shape[0], off[key]:off[key] + arr.shape[1]] = arr

    for l in range(depth):
        for nm in ("norm_mix", "norm_ca", "norm_mem", "norm_ffn"):
            put((nm, l), _col(inp[nm][l]))
        put(("conv_w", l), np.concatenate([_col(inp["conv_w"][l][t]) for t in range(3)], axis=1))
        put(("gate_b", l), np.concatenate([_col(inp["gate_b"][l][k]) for k in range(3)], axis=1))
        put(("mu", l), _col(inp["mu_rwkv"][l]))
        for nm in ("w0", "a0", "k_k", "k_a", "lnx_w", "lnx_b"):
            put((nm, l), _col(inp[nm][l]))
        put(("r_k", l), _col(inp["r_k"][l].reshape(-1)))
        put(("ffn_cw", l), np.concatenate([_col(inp["ffn_conv_w"][l][t]) for t in range(3)], axis=1))
        put(("ffn_cb", l), _col(inp["ffn_conv_b"][l]))
    put(("norm_final", 0), _col(inp["norm_final"]))
    if depth > 1:
        put(("v0", 1), _col(inp["v0"][0]))
        put(("mu_vres", 1), np.asarray(inp["mu_vres"][0], np.float32).reshape(32, 1))
    return cv


def _build_consts():
    c = {}
    c["ident"] = np.eye(128, dtype=np.float32)
    s = np.arange(128)
    c["uincl"] = (s[:, None] >= s[None, :]).astype(np.float32)
    c["lstr"] = (s[:, None] < s[None, :]).astype(np.float32)
    c["ones"] = np.ones((128, 128), np.float32)
    blk = np.zeros((128, 128), np.float32)
    blk[:64, :64] = 1
    blk[64:, 64:] = 1
    c["blk"] = blk
    m = np.where(s[:, None] < s[None, :], 0.0, -1.0e5).astype(np.float32)
    c["sbmask"] = np.tile(m, (1, 4))
    u = np.arange(64)
    st = (u[:, None] < u[None, :]).astype(np.float32)
    inc = (u[:, None] <= u[None, :]).astype(np.float32)
    lo = (u[:, None] > u[None, :]).astype(np.float32)
    rm = np.zeros((128, 2048), np.float32)
    rm[:64, 0:512] = np.tile(st, (1, 8))
    rm[:64, 512:1024] = np.tile(inc, (1, 8))
    rm[:64, 1024:1536] = np.tile(lo, (1, 8))
    rm[:64, 1536:2048] = np.tile(np.eye(64, dtype=np.float32), (1, 8))
    c["rmask"] = rm
    return c


def build_program(T=4096, depth=2, use_sb=True, use_rwkv=True, debug=False):
    NT = T // TT
    nc = bass.Bass("TRN2", target_bir_lowering=False)
    cv_off, cv_n = _cvec_layout(depth)

    def din(name, shape, dt=F32):
        return nc.dram_tensor(name, list(shape), dt, kind="ExternalInput").ap()

    def dscr(name, shape, dt):
        return nc.dram_tensor(name, list(shape), dt, kind="Internal").ap()

    xT = din("xT", [D, T])
    memT = din("memT", [D, MEM])
    cvec_d = din("cvec", [128, cv_n])
    w_comb = din("w_comb", [depth, D, COMB])
    w_vres = din("w_vres", [1, D, 32])
    lora_wa = din("lora_wa", [depth, 128, W])
    g_lora = din("g_lora", [depth, 128, W])
    v_lora = din("v_lora", [1, 32, W])
    branch_proj = din("branch_proj", [depth, 3, W, D])
    w_mix_out = din("w_mix_out", [depth, D, D])
    ca_wq = din("ca_wq", [depth, D, D])
    ca_wkv = din("ca_wkv", [depth, D, 2 * D])
    ca_wo = din("ca_wo", [depth, D, D])
    ffn_up = din("ffn_up", [depth, D, 2 * FFN])
    ffn_down = din("ffn_down", [depth, FFN, D])
    c_ident = din("c_ident", [128, 128])
    c_uincl = din("c_uincl", [128, 128])
    c_lstr = din("c_lstr", [128, 128])
    c_ones = din("c_ones", [128, 128])
    c_blk = din("c_blk", [128, 128])
    c_sbmask = din("c_sbmask", [128, 512])
    c_rmask = din("c_rmask", [128, 2048])
    outT = nc.dram_tensor("outT", [D, T], F32, kind="ExternalOutput").ap()
    dbg = None
    if debug:
        dbg = nc.dram_tensor("dbg", [8, D, T], F32, kind="ExternalOutput").ap()

    kcache = dscr("kcache", [depth, 128, 4, T], BF16)
    vcache = dscr("vcache", [depth, T // 128, 128, W], BF16)
    memK_d = dscr("memK", [depth, 128, 8, MEM], BF16)
    memV_d = dscr("memV", [depth, 128, 2, D], BF16)

    with ExitStack() as es:
        def sb(name, shape, dt, ):
            return es.enter_context(nc.sbuf_tensor(name + "_s", shape, dt))

        S = Sched(nc)
        PS = es.enter_context(nc.psum_tensor("PS", [128, 4096], F32))
        PSt = ["ps%d" % i for i in range(8)]

        def bank(i, parts=128, lo=0, hi=512):
            return PS[0:parts, i * 512 + lo:i * 512 + hi]

        hT = sb("hT", [128, 8, TT], F32)
        xn = sb("xn", [128, 8, TT], BF16)
        merged = sb("merged", [128, 8, TT], F32)
        mergedb = sb("mergedb", [128, 8, TT], BF16)
        NWB = 3
        wbuf = [sb("wbuf%d" % i, [128, 8 * 512], BF16) for i in range(NWB)]
        cvec = sb("cvec", [128, cv_n], F32)
        ident = sb("ident", [128, 128], F32)
        uincl = sb("uincl", [128, 128], BF16)
        lstr = sb("lstr", [128, 128], BF16)
        onesb = sb("onesb", [128, 128], BF16)
        blk = sb("blk", [128, 128], F32)
        sbmask = sb("sbmask", [128, 512], F32)
        rmask = sb("rmask", [128, 2048], F32)
        conv_carry = sb("conv_carry", [128, depth * 4 * 2], F32)
        ffn_carry = sb("ffn_carry", [128, depth * NFC * 2], F32)
        rw_carry = sb("rw_carry", [128, depth * 16], F32)
        rw_state = sb("rw_state", [128, depth * 4 * 64], F32)
        vfirst = sb("vfirst", [128, 4, TT], F32)
        omka = sb("omka", [128, depth * 4], F32)
        ones64 = sb("ones64", [128, 64], F32)
        identb = sb("identb", [128, 128], BF16)
        rw_sbf = sb("rw_sbf", [128, depth * 4 * 64], BF16)
        ARENA_F32 = 26 * 1024
        scr = sb("scr", [128, ARENA_F32], F32)
        ar = Arena(scr, ARENA_F32)

        def cvc(key, j=0, n=1, parts=128):
            o = cv_off[key] + j
            return cvec[0:parts, o:o + n]

        S.dma("sp", lambda: nc.sync.dma_start(out=cvec[:], in_=cvec_d), w=["cvec"])
        S.dma("sp", lambda: nc.sync.dma_start(out=ident[:], in_=c_ident), w=["ident"])
        S.dma("sp", lambda: nc.sync.dma_start(out=blk[:], in_=c_blk), w=["blk"])
        S.dma("sp", lambda: nc.sync.dma_start(out=sbmask[:], in_=c_sbmask), w=["sbmask"])
        S.dma("sp", lambda: nc.sync.dma_start(out=rmask[:], in_=c_rmask), w=["rmask"])
        S.dma("pool", lambda: nc.gpsimd.dma_start(out=uincl[:], in_=c_uincl), w=["uincl"])
        S.dma("pool", lambda: nc.gpsimd.dma_start(out=lstr[:], in_=c_lstr), w=["lstr"])
        S.dma("pool", lambda: nc.gpsimd.dma_start(out=onesb[:], in_=c_ones), w=["onesb"])
        S.dve(lambda: nc.vector.memset(conv_carry[:], 0.0), w=["conv_carry"])
        S.dve(lambda: nc.vector.memset(ffn_carry[:], 0.0), w=["ffn_carry"])
        S.dve(lambda: nc.vector.memset(rw_carry[:], 0.0), w=["rw_carry"])
        S.dve(lambda: nc.vector.memset(rw_state[:], 0.0), w=["rw_state"])
        S.dve(lambda: nc.vector.memset(ones64[:], 1.0), w=["ones64"])
        S.dve(lambda: nc.vector.memset(rw_sbf[:], 0.0), w=["rw_sbf"])
        S.dma("pool", lambda: nc.gpsimd.dma_start(out=identb[:], in_=c_ident), w=["identb"])
        for l in range(depth):
            S.dve(lambda l=l: nc.vector.tensor_scalar(out=omka[:, l * 4:l * 4 + 4], in0=cvc(("k_a", l), 0, 4),
                                                      scalar1=-1.0, scalar2=1.0, op0=ALU.mult, op1=ALU.add),
                  r=["cvec"], w=["omka"])

        wstate = {"i": 0}

        def load_w(src_ap, view):
            i = wstate["i"] % NWB
            wstate["i"] += 1
            a, b = src_ap.shape[1], src_ap.shape[2]
            dst = wbuf[i][:, 0:a * b].rearrange("p (a b) -> p a b", b=b)
            tok = "wbuf%d" % i
            S.dma("pool", lambda: nc.gpsimd.dma_start(out=dst, in_=src_ap), w=[tok])
            return dst, tok

        def slab(wd, col0, ncols=512):
            return load_w(wd[:, col0:col0 + ncols].rearrange("(c p) n -> p c n", p=128), None)

        def proj_fm(out_ps, sl, tok, c0, m, ptok, rhs=None, rtok="xn", n=TT, nk=8):
            rhs = xn if rhs is None else rhs

            def f():
                ins = None
                for c in range(nk):
                    ins = nc.tensor.matmul(out_ps, lhsT=sl[:, c, c0:c0 + m], rhs=rhs[:, c, 0:n],
                                           start=(c == 0), stop=(c == nk - 1))
                return ins
            S.pe(f, r=[tok, rtok], w=[ptok])

        def rmsnorm(src, stok, gkey, dst, dtok, n=TT, pb=7):
            sq, sqt = ar.alloc((8, n), BF16)
            rstd, rt = ar.alloc((n,), F32)
            S.act(lambda: nc.scalar.activation(out=sq, in_=src[:, :, 0:n], func=AF.Square), r=[stok], w=[sqt])

            def f():
                ins = None
                for c in range(8):
                    ins = nc.tensor.matmul(bank(pb, hi=n), lhsT=onesb[:, :], rhs=sq[:, c, :], start=(c == 0), stop=(c == 7))
                return ins
            S.pe(f, r=[sqt, "onesb"], w=[PSt[pb]])
            S.act(lambda: nc.scalar.activation(out=rstd, in_=bank(pb, hi=n), func=AF.Ln, scale=1.0 / D, bias=1e-6),
                  r=[PSt[pb]], w=[rt])
            S.act(lambda: nc.scalar.activation(out=rstd, in_=rstd, func=AF.Exp, scale=-0.5), r=[rt], w=[rt])
            for c in range(8):
                S.dve(lambda c=c: nc.vector.scalar_tensor_tensor(out=dst[:, c, 0:n], in0=src[:, c, 0:n], scalar=cvc(gkey, c),
                                                                 in1=rstd, op0=ALU.mult, op1=ALU.mult),
                      r=[stok, rt, "cvec"], w=[dtok])

        def dbg_out(slot, src, stok, i, nchunk=8):
            if dbg is None:
                return
            S.dma("sp", lambda: nc.sync.dma_start(
                out=dbg[slot, 0:nchunk * 128, i * TT:(i + 1) * TT].rearrange("(c p) t -> p c t", p=128), in_=src),
                r=[stok])

        def mem_kv(l):
            ar.reset()
            mt_, mtt = ar.alloc((8, MEM), F32)
            mn, mnt = ar.alloc((8, MEM), BF16)
            S.dma("sp", lambda: nc.sync.dma_start(out=mt_, in_=memT.rearrange("(c p) m -> p c m", p=128)), w=[mtt])
            rmsnorm(mt_, mtt, ("norm_mem", l), mn, mnt, n=MEM)
            kk_, kkt = ar.alloc((8, MEM), BF16)
            vv_, vvt = ar.alloc((2, D), BF16)
            for half in range(2):
                sl, tok = slab(ca_wkv[l], half * 512)
                for dcl in range(4):
                    dc = half * 4 + dcl
                    pb = dcl % 4
                    proj_fm(bank(pb, hi=MEM), sl, tok, dcl * 128, 128, PSt[pb], rhs=mn, rtok=mnt, n=MEM)
                    S.act(lambda dc=dc, pb=pb: nc.scalar.copy(out=kk_[:, dc, :], in_=bank(pb, hi=MEM)), r=[PSt[pb]], w=[kkt])
            for half in range(2):
                sl, tok = slab(ca_wkv[l], D + half * 512)
                for mt in range(2):
                    pb = 4 + mt

                    def f(sl=sl, mt=mt, pb=pb):
                        ins = None
                        for c in range(8):
                            ins = nc.tensor.matmul(bank(pb), lhsT=mn[:, c, mt * 128:(mt + 1) * 128], rhs=sl[:, c, :],
                                                   start=(c == 0), stop=(c == 7))
                        return ins
                    S.pe(f, r=[tok, mnt], w=[PSt[pb]])
                    S.act(lambda mt=mt, half=half, pb=pb: nc.scalar.copy(out=vv_[:, mt, half * 512:(half + 1) * 512], in_=bank(pb)),
                          r=[PSt[pb]], w=[vvt])
            S.dma("sp", lambda l=l: nc.sync.dma_start(out=memK_d[l], in_=kk_), r=[kkt], w=["memK%d" % l])
            S.dma("sp", lambda l=l: nc.sync.dma_start(out=memV_d[l], in_=vv_), r=[vvt], w=["memV%d" % l])
            S.barrier()

        for l in range(depth):
            mem_kv(l)

        def merge_branch(l, n, ys, ystok):
            bp, bptok = load_w(branch_proj[l, n].rearrange("(c p) n -> p c n", p=128), None)
            gsb = [ar.alloc((TT,), F32) for _ in range(2)]
            for half in range(2):
                sl, tok = slab(w_comb[l], OFF_GATE + n * D + half * 512)
                for dcl in range(4):
                    dc = half * 4 + dcl
                    pg = dc % 2
                    pbr = 2 + dc % 2
                    proj_fm(bank(pg), sl, tok, dcl * 128, 128, PSt[pg])
                    gs, gst = gsb[dc % 2]
                    S.act(lambda dc=dc, pg=pg, gs=gs: nc.scalar.activation(out=gs, in_=bank(pg), func=AF.Sigmoid,
                                                                          bias=cvc(("gate_b", l), n * 8 + dc)),
                          r=[PSt[pg], "cvec"], w=[gst])
                    proj_fm(bank(pbr), bp, bptok, dc * 128, 128, PSt[pbr], rhs=ys, rtok=ystok, nk=4)
                    if n == 0:
                        S.dve(lambda dc=dc, pbr=pbr, gs=gs: nc.vector.tensor_tensor(out=merged[:, dc, :], in0=bank(pbr), in1=gs, op=ALU.mult),
                              r=[PSt[pbr], gst], w=["merged%d" % dc])
                    else:
                        S.dve(lambda dc=dc, pbr=pbr, gs=gs: nc.vector.tensor_tensor(out=gs, in0=bank(pbr), in1=gs, op=ALU.mult),
                              r=[PSt[pbr], gst], w=[gst])
                        dst = mergedb if n == 2 else merged
                        dtk = ("mergedb%d" if n == 2 else "merged%d") % dc
                        S.dve(lambda dc=dc, gs=gs, dst=dst: nc.vector.tensor_tensor(out=dst[:, dc, :], in0=merged[:, dc, :], in1=gs, op=ALU.add),
                               r=[gst, "merged%d" % dc], w=[dtk])

        def out_proj_residual(wd, act, atok, nk=8):
            for half in range(2):
                sl, tok = slab(wd, half * 512)
                for dcl in range(4):
                    dc = half * 4 + dcl
                    pb = 4 + dc % 2
                    proj_fm(bank(pb), sl, tok, dcl * 128, 128, PSt[pb], rhs=act, rtok=atok, nk=nk)
                    S.dve(lambda dc=dc, pb=pb: nc.vector.tensor_tensor(out=hT[:, dc, :], in0=hT[:, dc, :], in1=bank(pb), op=ALU.add),
                          r=[PSt[pb], "hT"], w=["hT"])

        def layer_tile(i, l):
            if True:
                S.epoch = 1 + i * depth + l
                ar.reset()
                rmsnorm(hT, "hT", ("norm_mix", l), xn, "xn")
                slb, tb = slab(w_comb[l], 0)
                slc, tcx = slab(w_comb[l], 512)
                slx, tx = slab(w_comb[l], 1024)
                yconv, yct = ar.alloc((4, TT), BF16)
                for j in range(4):
                    proj_fm(bank(0), slb, tb, j * 128, 128, PSt[0])
                    proj_fm(bank(1), slc, tcx, j * 128, 128, PSt[1])
                    proj_fm(bank(2), slx, tx, j * 128, 128, PSt[2])
                    cxb, cxt = ar.alloc((TT + 2,), F32)
                    tmp, tmt = ar.alloc((TT,), F32)
                    cc = conv_carry[:, (l * 4 + j) * 2:(l * 4 + j) * 2 + 2]
                    S.dve(lambda cxb=cxb, cc=cc: nc.vector.tensor_copy(out=cxb[:, 0:2], in_=cc), r=["conv_carry%d_%d" % (l, j)], w=[cxt])
                    S.act(lambda tmp=tmp: nc.scalar.copy(out=tmp, in_=bank(1)), r=[PSt[1]], w=[tmt])
                    S.dve(lambda cxb=cxb, tmp=tmp: nc.vector.tensor_tensor(out=cxb[:, 2:TT + 2], in0=tmp, in1=bank(2), op=ALU.mult),
                          r=[tmt, PSt[2], cxt], w=[cxt])
                    S.dve(lambda cxb=cxb, cc=cc: nc.vector.tensor_copy(out=cc, in_=cxb[:, TT:TT + 2]), r=[cxt], w=["conv_carry%d_%d" % (l, j)])
                    S.dve(lambda cxb=cxb, tmp=tmp, j=j: nc.vector.tensor_scalar(out=tmp, in0=cxb[:, 0:TT], scalar1=cvc(("conv_w", l), j), scalar2=None, op0=ALU.mult),
                          r=[cxt, tmt, "cvec"], w=[tmt])
                    for tap in (1, 2):
                        S.dve(lambda cxb=cxb, tmp=tmp, j=j, tap=tap: nc.vector.scalar_tensor_tensor(
                            out=tmp, in0=cxb[:, tap:tap + TT], scalar=cvc(("conv_w", l), tap * 4 + j), in1=tmp, op0=ALU.mult, op1=ALU.add),
                            r=[cxt, tmt, "cvec"], w=[tmt])
                    S.dve(lambda tmp=tmp, j=j: nc.vector.tensor_tensor(out=yconv[:, j, :], in0=tmp, in1=bank(0), op=ALU.mult),
                          r=[tmt, PSt[0]], w=[yct])
                merge_branch(l, 0, yconv, yct)
                S.barrier()

                ar.reset()
                ysb, ysbt = ar.alloc((4, TT), BF16)
                if use_sb:
                    sb_attention(nc, S, ar, bank, PSt, PS, l, i, xn, w_comb, kcache, vcache, ysb, ysbt, slab, proj_fm,
                                 uincl, lstr, sbmask)
                else:
                    S.dve(lambda: nc.vector.memset(ysb, 0.0), w=[ysbt])
                merge_branch(l, 1, ysb, ysbt)
                S.barrier()

                ar.reset()
                yrw, yrwt = ar.alloc((4, TT), BF16)
                if use_rwkv:
                    rwkv_branch(nc, S, ar, bank, PSt, PS, l, i, xn, w_comb, w_vres, lora_wa, g_lora, v_lora, yrw, yrwt, slab, load_w,
                                proj_fm, cvc, dict(ident=ident, blk=blk, rmask=rmask, rw_carry=rw_carry, rw_state=rw_state,
                                                   vfirst=vfirst, omka=omka, ones64=ones64, identb=identb, rw_sbf=rw_sbf))
                else:
                    S.dve(lambda: nc.vector.memset(yrw, 0.0), w=[yrwt])
                merge_branch(l, 2, yrw, yrwt)
                S.barrier()
                ar.reset()
                out_proj_residual(w_mix_out[l], mergedb, "mergedb_all")
                S.barrier()

                ar.reset()
                rmsnorm(hT, "hT", ("norm_ca", l), xn, "xn")
                qca, qcat = ar.alloc((8, TT), BF16)
                mK, mKt = ar.alloc((8, MEM), BF16)
                mV, mVt = ar.alloc((2, D), BF16)
                oT, oTt = ar.alloc((8, TT), BF16)
                S.dma("sp", lambda l=l: nc.sync.dma_start(out=mK, in_=memK_d[l]), r=["memK%d" % l], w=[mKt])
                S.dma("sp", lambda l=l: nc.sync.dma_start(out=mV, in_=memV_d[l]), r=["memV%d" % l], w=[mVt])
                for half in range(2):
                    sl, tok = slab(ca_wq[l], half * 512)
                    for dcl in range(4):
                        dc = half * 4 + dcl
                        pb = dc % 2
                        proj_fm(bank(pb), sl, tok, dcl * 128, 128, PSt[pb])
                        S.act(lambda dc=dc, pb=pb: nc.scalar.copy(out=qca[:, dc, :], in_=bank(pb)), r=[PSt[pb]], w=[qcat])
                for hh in range(4):
                    ee, eet = ar.alloc((2, TT), BF16)
                    for mt in range(2):
                        pb = 2 + mt

                        def f(hh=hh, mt=mt, pb=pb):
                            ins = None
                            for cl in range(2):
                                c = 2 * hh + cl
                                ins = nc.tensor.matmul(bank(pb), lhsT=mK[:, c, mt * 128:(mt + 1) * 128], rhs=qca[:, c, :],
                                                       start=(cl == 0), stop=(cl == 1))
                            return ins
                        S.pe(f, r=[mKt, qcat], w=[PSt[pb]])
                        S.act(lambda ee=ee, mt=mt, pb=pb: nc.scalar.activation(out=ee[:, mt, :], in_=bank(pb), func=AF.Exp, scale=1.0 / 16.0),
                              r=[PSt[pb]], w=[eet])

                    def fs(ee=ee):
                        nc.tensor.matmul(bank(4), lhsT=onesb[:, :], rhs=ee[:, 0, :], start=True, stop=False)
                        return nc.tensor.matmul(bank(4), lhsT=onesb[:, :], rhs=ee[:, 1, :], start=False, stop=True)
                    S.pe(fs, r=[eet, "onesb"], w=[PSt[4]])
                    rs, rst = ar.alloc((TT,), F32)
                    S.dve(lambda rs=rs: nc.vector.reciprocal(out=rs, in_=bank(4)), r=[PSt[4]], w=[rst])
                    for dv in range(2):
                        pb = 5 + dv

                        def fo(hh=hh, dv=dv, pb=pb, ee=ee):
                            c0 = hh * 256 + dv * 128
                            nc.tensor.matmul(bank(pb), lhsT=mV[:, 0, c0:c0 + 128], rhs=ee[:, 0, :], start=True, stop=False)
                            return nc.tensor.matmul(bank(pb), lhsT=mV[:, 1, c0:c0 + 128], rhs=ee[:, 1, :], start=False, stop=True)
                        S.pe(fo, r=[mVt, eet], w=[PSt[pb]])
                        S.dve(lambda hh=hh, dv=dv, pb=pb, rs=rs: nc.vector.tensor_tensor(out=oT[:, 2 * hh + dv, :], in0=bank(pb), in1=rs, op=ALU.mult),
                              r=[PSt[pb], rst], w=[oTt])
                out_proj_residual(ca_wo[l], oT, oTt)
                S.barrier()

                ar.reset()
                rmsnorm(hT, "hT", ("norm_ffn", l), xn, "xn")
                sg, sgt0 = ar.alloc((22, TT), BF16)
                NFB = 4
                bufs = [ar.alloc((TT + 2,), F32) for _ in range(NFB)]
                us = [ar.alloc((TT,), F32) for _ in range(NFB)]
                pend = []

                def carry_in(c2):
                    buf2, bft2 = bufs[c2 % NFB]
                    car2 = ffn_carry[:, (l * NFC + c2) * 2:(l * NFC + c2) * 2 + 2]
                    S.dve(lambda: nc.vector.tensor_copy(out=buf2[:, 0:2], in_=car2), r=["ffn_carry%d_%d" % (l, c2)], w=[bft2 + "c"])

                for s11 in range(11):
                    sl, tok = slab(ffn_up[l], s11 * 512)
                    for k4 in range(4):
                        cc_ = s11 * 4 + k4
                        pb = cc_ % 4
                        buf, bft = bufs[cc_ % NFB]
                        u, ut = us[cc_ % NFB]
                        car = ffn_carry[:, (l * NFC + cc_) * 2:(l * NFC + cc_) * 2 + 2]
                        bfc = bft + "c"
                        proj_fm(bank(pb), sl, tok, k4 * 128, 128, PSt[pb])
                        if cc_ == 0:
                            carry_in(0)
                        S.act(lambda buf=buf, pb=pb: nc.scalar.copy(out=buf[:, 2:TT + 2], in_=bank(pb)), r=[PSt[pb]], w=[bft])
                        S.dve(lambda buf=buf, car=car: nc.vector.tensor_copy(out=car, in_=buf[:, TT:TT + 2]), r=[bft, bfc], w=["ffn_carry%d_%d" % (l, cc_)])
                        if cc_ + 1 < NFC:
                            carry_in(cc_ + 1)
                        S.act(lambda buf=buf, u=u, cc_=cc_: nc.scalar.activation(
                            out=u, in_=buf[:, 0:TT], func=AF.Identity, scale=cvc(("ffn_cw", l), cc_), bias=cvc(("ffn_cb", l), cc_)),
                            r=[bft, bfc, "cvec"], w=[ut])
                        for tap in (1, 2):
                            S.dve(lambda buf=buf, u=u, cc_=cc_, tap=tap: nc.vector.scalar_tensor_tensor(
                                out=u, in0=buf[:, tap:tap + TT], scalar=cvc(("ffn_cw", l), tap * NFC + cc_), in1=u, op0=ALU.mult, op1=ALU.add),
                                r=[bft, bfc, ut, "cvec"], w=[ut])

                        def tail(u=u, ut=ut, cc_=cc_):
                            if cc_ < 22:
                                S.act(lambda: nc.scalar.activation(out=sg[:, cc_, :], in_=u, func=AF.Silu), r=[ut], w=["sg%d" % cc_])
                            else:
                                jj = cc_ - 22
                                S.dve(lambda: nc.vector.tensor_tensor(out=sg[:, jj, :], in0=sg[:, jj, :], in1=u, op=ALU.mult),
                                      r=[ut, "sg%d" % jj], w=["sg%d" % jj])
                        pend.append(tail)
                        if len(pend) > 1:
                            pend.pop(0)()
                while pend:
                    pend.pop(0)()
                S.barrier()
                for dc in range(8):
                    sl, tok = load_w(ffn_down[l][:, dc * 128:(dc + 1) * 128].rearrange("(c p) n -> p c n", p=128), None)
                    pb = 4 + dc % 2
                    proj_fm(bank(pb), sl, tok, 0, 128, PSt[pb], rhs=sg, rtok="sg_all", nk=22)
                    S.dve(lambda dc=dc, pb=pb: nc.vector.tensor_tensor(out=hT[:, dc, :], in0=hT[:, dc, :], in1=bank(pb), op=ALU.add),
                          r=[PSt[pb], "hT"], w=["hT"])
                S.barrier()
        def tile_begin(i):
            tsl = slice(i * TT, (i + 1) * TT)
            S.dma("sp", lambda: nc.sync.dma_start(out=hT[:], in_=xT[:, tsl].rearrange("(c p) t -> p c t", p=128)), w=["hT"])

        def tile_end(i):
            tsl = slice(i * TT, (i + 1) * TT)
            ar.reset()
            of, oft = ar.alloc((8, TT), F32)
            rmsnorm(hT, "hT", ("norm_final", 0), of, oft)
            S.dma("sp", lambda: nc.sync.dma_start(out=outT[:, tsl].rearrange("(c p) t -> p c t", p=128), in_=of),
                  r=[oft])
            S.barrier()

        for i in range(NT):
            tile_begin(i)
            for l in range(depth):
                layer_tile(i, l)
            tile_end(i)
        S.emit()
        nc._arena_peak = ar.peak
    return nc


def sb_attention(nc, S, ar, bank, PSt, PS, l, i, xn, w_comb, kcache, vcache, ysb, ysbt, slab, proj_fm, uincl, lstr, sbmask):
    qT, qTt = ar.alloc((4, TT), BF16)
    kT, kTt = ar.alloc((4, TT), BF16)
    vt, vtt = ar.alloc((4, W), BF16)
    slq, tq = slab(w_comb[l], OFF_SB)
    slk, tk = slab(w_comb[l], OFF_SB + 512)
    slv, tv = slab(w_comb[l], OFF_SB + 1024)
    for j in range(4):
        proj_fm(bank(j % 2), slq, tq, j * 128, 128, PSt[j % 2])
        S.act(lambda j=j: nc.scalar.copy(out=qT[:, j, :], in_=bank(j % 2)), r=[PSt[j % 2]], w=[qTt])
        proj_fm(bank(2 + j % 2), slk, tk, j * 128, 128, PSt[2 + j % 2])
        S.act(lambda j=j: nc.scalar.copy(out=kT[:, j, :], in_=bank(2 + j % 2)), r=[PSt[2 + j % 2]], w=[kTt])
    for sub in range(4):
        pb = 4 + sub % 2

        def f(sub=sub, pb=pb):
            ins = None
            for c in range(8):
                ins = nc.tensor.matmul(bank(pb), lhsT=xn[:, c, sub * 128:(sub + 1) * 128], rhs=slv[:, c, :], start=(c == 0), stop=(c == 7))
            return ins
        S.pe(f, r=[tv, "xn"], w=[PSt[pb]])
        S.act(lambda sub=sub, pb=pb: nc.scalar.copy(out=vt[:, sub, :], in_=bank(pb)), r=[PSt[pb]], w=[vtt])
    if True:
        S.dma("sp", lambda: nc.sync.dma_start(out=kcache[l, :, :, i * TT:(i + 1) * TT], in_=kT), r=[kTt], w=["kcache"])
        S.dma("sp", lambda: nc.sync.dma_start(out=vcache[l, 4 * i:4 * i + 4].rearrange("s p w -> p s w"), in_=vt), r=[vtt], w=["vcache"])
    NKV = 3
    kst = [ar.alloc((4, 128), BF16) for _ in range(NKV)]
    vst = [ar.alloc((W,), BF16) for _ in range(NKV)]
    zm = [ar.alloc((512,), F32) for _ in range(2)]
    NE = 3
    LA = 2
    Eb = [[ar.alloc((512,), F32) for g in range(2)] for _ in range(NE)]
    SPb = [[ar.alloc((512,), BF16) for g in range(2)] for _ in range(NE)]
    Xb = [ar.alloc((512,), F32) for _ in range(2)]
    attb = [[ar.alloc((512,), BF16) for g in range(2)] for _ in range(2)]
    kvi = [0]
    for qb in range(4):
        I = 4 * i + qb
        qs = slice(qb * 128, (qb + 1) * 128)
        steps = list(range(I, -1, -1))
        n = len(steps)
        info = {}

        def SZ(s_, I=I, qs=qs, steps=steps, info=info):
            c = steps[s_]
            if c >= 4 * i:
                cl = c - 4 * i
                Kc = kT[:, :, cl * 128:(cl + 1) * 128]
                Vc = vt[:, cl, :]
                kt_, vt_ = kTt, vtt
            else:
                (Kc, kt_), (Vc, vt_) = kst[kvi[0] % NKV], vst[kvi[0] % NKV]
                kvi[0] += 1
                S.dma("sp", lambda Kc=Kc, c=c: nc.sync.dma_start(out=Kc, in_=kcache[l, :, :, c * 128:(c + 1) * 128]), r=["kcache"], w=[kt_])
                S.dma("sp", lambda Vc=Vc, c=c: nc.sync.dma_start(out=Vc, in_=vcache[l, c]), r=["vcache"], w=[vt_])
            for g in range(2):
                ztoks = [PSt[2 * g], PSt[2 * g + 1]]

                def fz(g=g, Kc=Kc, qs=qs):
                    ins = None
                    for e in range(2):
                        for jl in range(2):
                            j = 2 * g + jl
                            lo = jl * 128
                            ins = nc.tensor.matmul(bank(2 * g + e, lo=lo, hi=lo + 128),
                                                   lhsT=Kc[e * 64:(e + 1) * 64, j, :], rhs=qT[e * 64:(e + 1) * 64, j, qs],
                                                   start=True, stop=True)
                    return ins
                S.pe(fz, r=[kt_, qTt], w=ztoks)
                zsrc = PS[:, 2 * g * 512:(2 * g + 2) * 512].rearrange("p (b x) -> p b x", b=2)[:, :, 0:256]
                E, Et = Eb[s_ % NE][g]
                SP, SPt = SPb[s_ % NE][g]
                E3 = E.rearrange("p (b x) -> p b x", b=2)
                if c == I:
                    z, zt = zm[g]
                    z3 = z.rearrange("p (b x) -> p b x", b=2)
                    S.dve(lambda z3=z3, zsrc=zsrc: nc.vector.tensor_tensor(out=z3, in0=zsrc, in1=sbmask[:, :].rearrange("p (b x) -> p b x", b=2), op=ALU.add),
                          r=ztoks + ["sbmask"], w=[zt])
                    S.act(lambda E=E, z=z: nc.scalar.activation(out=E, in_=z, func=AF.Exp, scale=0.125), r=[zt], w=[Et])
                else:
                    S.act(lambda E3=E3, zsrc=zsrc: nc.scalar.activation(out=E3, in_=zsrc, func=AF.Exp, scale=0.125), r=ztoks, w=[Et])
                S.act(lambda E=E, SP=SP: nc.scalar.activation(out=SP, in_=E, func=AF.Ln, bias=1.0), r=[Et], w=[SPt])
            info[s_] = (c, Vc, vt_)

        for s_ in range(min(LA, n)):
            SZ(s_)
        for k in range(n):
            pc, Vc, vt_ = info[k]
            for g in range(2):
                pa = 4 + g
                SP, SPt = SPb[k % NE][g]
                S.pe(lambda SP=SP, pa=pa, pc=pc, I=I: nc.tensor.matmul(bank(pa), lhsT=uincl[:, :], rhs=SP, start=(pc == I), stop=(pc == 0), skip_group_check=True),
                     r=[SPt, "uincl", PSt[pa]], w=[PSt[pa]])
            for g in range(2):
                pa = 4 + g
                X, Xt = Xb[g]
                S.act(lambda X=X, pa=pa: nc.scalar.activation(out=X, in_=bank(pa), func=AF.Exp, scale=-1.0), r=[PSt[pa]], w=[Xt])
            if k + LA < n:
                SZ(k + LA)
            if pc > 0:
                for g in range(2):
                    pa = 4 + g
                    SP, SPt = SPb[k % NE][g]
                    S.pe(lambda SP=SP, pa=pa: nc.tensor.matmul(bank(pa), lhsT=lstr[:, :], rhs=SP, start=False, stop=False, skip_group_check=True),
                         r=[SPt, "lstr", PSt[pa]], w=[PSt[pa]])
            for g in range(2):
                E, Et = Eb[k % NE][g]
                X, Xt = Xb[g]
                att, att_t = attb[k % 2][g]
                S.dve(lambda att=att, E=E, X=X: nc.vector.tensor_tensor(out=att, in0=E, in1=X, op=ALU.mult), r=[Et, Xt], w=[att_t])
            for g in range(2):
                att, att_t = attb[k % 2][g]
                po = 6 + g

                def fo(g=g, pc=pc, att=att, Vc=Vc, po=po, I=I):
                    ins = None
                    for e in range(2):
                        for jl in range(2):
                            h = 4 * g + 2 * jl + e
                            ins = nc.tensor.matmul(bank(po, lo=jl * 128, hi=jl * 128 + 128)[e * 64:(e + 1) * 64, :],
                                                   lhsT=Vc[:, h * 64:(h + 1) * 64], rhs=att[:, (e * 2 + jl) * 128:(e * 2 + jl) * 128 + 128],
                                                   start=(pc == I and jl == 0), stop=(pc == 0), skip_group_check=True)
                    return ins
                S.pe(fo, r=[att_t, vt_, PSt[po]], w=[PSt[po]])
                if pc == 0:
                    S.act(lambda g=g, po=po, qs=qs: nc.scalar.copy(out=ysb[:, 2 * g:2 * g + 2, qs],
                                                                    in_=bank(po, hi=256).rearrange("p (a b) -> p a b", b=128)),
                          r=[PSt[po]], w=[ysbt])


def bc_last(x, n):
    return bass.AP(x.tensor, x.offset, [list(x.ap[0]), list(x.ap[1]), [0, n]])


def rwkv_branch(nc, S, ar, bank, PSt, PS, l, i, xn, w_comb, w_vres, lora_wa, g_lora, v_lora, yrw, yrwt, slab, load_w,
                proj_fm, cvc, cs_):
    ident, blk, rmask = cs_["ident"], cs_["blk"], cs_["rmask"]
    rw_carry, rw_state, vfirst, omka, ones64 = cs_["rw_carry"], cs_["rw_state"], cs_["vfirst"], cs_["omka"], cs_["ones64"]
    NCH = TT // CH
    m_st = rmask[0:64, 0:512]
    m_inc = rmask[0:64, 512:1024]
    m_lo = rmask[0:64, 1024:1536]
    m_id = rmask[0:64, 1536:2048]
    Sst = rw_state[:, l * 256:(l + 1) * 256].rearrange("p (j v) -> p j v", v=64)
    identb, rw_sbf = cs_["identb"], cs_["rw_sbf"]
    Sbf = rw_sbf[:, l * 256:(l + 1) * 256].rearrange("p (j v) -> p j v", v=64)

    rt, rtt = ar.alloc((4, TT), BF16)
    kpt, kptt = ar.alloc((4, TT), BF16)
    kt, ktt = ar.alloc((4, TT), BF16)
    bt, btt = ar.alloc((4, TT), BF16)
    vT, vTt = ar.alloc((4, TT), BF16)
    bonus, bont = ar.alloc((4, TT), BF16)
    gC, gCt = ar.alloc((4, NCH), F32)
    xwa, xwat = ar.alloc((TT,), BF16)
    sxg, sxgt = ar.alloc((TT,), BF16)
    xv, xvt = ar.alloc((TT,), BF16)
    lwa, lwat = ar.alloc((W,), BF16)
    glo, glot = ar.alloc((W,), BF16)
    vlo, vlot = ar.alloc((W,), BF16)
    mark = ar.off
    S.dma("pool", lambda: nc.gpsimd.dma_start(out=lwa, in_=lora_wa[l]), w=[lwat])
    S.dma("pool", lambda: nc.gpsimd.dma_start(out=glo, in_=g_lora[l]), w=[glot])
    if l > 0:
        S.dma("pool", lambda: nc.gpsimd.dma_start(out=vlo[0:32, :], in_=v_lora[0]), w=[vlot])

    bufs = [ar.alloc((TT + 1,), F32) for _ in range(3)]
    tmps = [ar.alloc((TT,), F32) for _ in range(3)]
    st = {"n": 0}

    def lerp(pb, parts, cc, mu_ap, dst, dtok):
        k = st["n"] % 3
        st["n"] += 1
        buf, bft = bufs[k]
        tmp, tmt = tmps[k]
        car = rw_carry[0:parts, l * 16 + cc:l * 16 + cc + 1]
        bfc = bft + "c"
        S.dve(lambda: nc.vector.tensor_copy(out=buf[0:parts, 0:1], in_=car), r=["rw_carry%d_%d" % (l, cc)], w=[bfc])
        S.act(lambda: nc.scalar.copy(out=buf[0:parts, 1:TT + 1], in_=bank(pb, parts=parts)), r=[PSt[pb]], w=[bft])
        S.dve(lambda: nc.vector.tensor_copy(out=car, in_=buf[0:parts, TT:TT + 1]), r=[bft, bfc], w=["rw_carry%d_%d" % (l, cc)])
        S.dve(lambda: nc.vector.tensor_tensor(out=tmp[0:parts, :], in0=buf[0:parts, 0:TT], in1=buf[0:parts, 1:TT + 1], op=ALU.subtract),
              r=[bft, bfc], w=[tmt])
        S.dve(lambda: nc.vector.scalar_tensor_tensor(out=dst, in0=tmp[0:parts, :], scalar=mu_ap, in1=buf[0:parts, 1:TT + 1],
                                                     op0=ALU.mult, op1=ALU.add), r=[tmt, bft, "cvec"], w=[dtok])

    sle, te = load_w(w_comb[l][:, OFF_RWKV + 1536:OFF_RWKV + 1792].rearrange("(c p) n -> p c n", p=128), None)
    ex, ext = ar.alloc((TT,), F32)
    proj_fm(bank(0), sle, te, 0, 128, PSt[0])
    lerp(0, 128, 12, cvc(("mu", l), 12), ex, ext)
    S.act(lambda: nc.scalar.activation(out=xwa[0:64, :], in_=ex[0:64, :], func=AF.Tanh), r=[ext], w=[xwat])
    S.act(lambda: nc.scalar.copy(out=xwa[64:128, :], in_=ex[64:128, :]), r=[ext], w=[xwat])
    ex2, ext2 = ar.alloc((TT,), F32)
    proj_fm(bank(1), sle, te, 128, 128, PSt[1])
    lerp(1, 128, 13, cvc(("mu", l), 13), ex2, ext2)
    S.act(lambda: nc.scalar.activation(out=sxg, in_=ex2, func=AF.Sigmoid), r=[ext2], w=[sxgt])
    if l > 0:
        slv_, tv_ = load_w(w_vres[0].rearrange("(c p) n -> p c n", p=128), None)
        ex3, ext3 = ar.alloc((TT,), F32)
        proj_fm(bank(2, parts=32), slv_, tv_, 0, 32, PSt[2])
        lerp(2, 32, 14, cvc(("mu_vres", 1), 0, 1, parts=32), ex3[0:32, :], ext3)
        S.act(lambda: nc.scalar.copy(out=xv[0:32, :], in_=ex3[0:32, :]), r=[ext3], w=[xvt])

    slr, tr = slab(w_comb[l], OFF_RWKV)
    slk, tk = slab(w_comb[l], OFF_RWKV + 512)
    slv, tv = slab(w_comb[l], OFF_RWKV + 1024)
    names = ["r", "k", "v", "sg", "a", "lw", "L", "kk", "t1", "t2", "kap", "kp", "b", "eL", "eNL"]
    dbl = ("r", "k", "v", "sg", "a", "lw", "L")
    T0 = {nm: ar.alloc((TT,), F32) for nm in names}
    T1 = {nm: (ar.alloc((TT,), F32) if nm in dbl else T0[nm]) for nm in names}
    cur = {"T": T0}

    def A(nm):
        return cur["T"][nm][0]

    def K_(nm):
        return cur["T"][nm][1]

    for j in range(4):
        cur["T"] = T0 if j % 2 == 0 else T1
        A = (lambda T: (lambda nm: T[nm][0]))(cur["T"])
        proj_fm(bank(0), slr, tr, j * 128, 128, PSt[0])
        lerp(0, 128, j, cvc(("mu", l), j), A("r"), K_("r"))
        proj_fm(bank(1), slk, tk, j * 128, 128, PSt[1])
        lerp(1, 128, 4 + j, cvc(("mu", l), 4 + j), A("k"), K_("k"))
        proj_fm(bank(2), slv, tv, j * 128, 128, PSt[2])
        lerp(2, 128, 8 + j, cvc(("mu", l), 8 + j), A("v"), K_("v"))
        js = slice(j * 128, (j + 1) * 128)
        S.pe(lambda js=js: nc.tensor.matmul(bank(3), lhsT=lwa[0:64, js], rhs=xwa[0:64, :], start=True, stop=True), r=[lwat, xwat], w=[PSt[3]])
        S.act(lambda j=j, A=A: nc.scalar.activation(out=A("sg"), in_=bank(3), func=AF.Sigmoid, bias=cvc(("w0", l), j)), r=[PSt[3], "cvec"], w=[K_("sg")])
        S.dve(lambda A=A: nc.vector.tensor_scalar(out=A("lw"), in0=A("sg"), scalar1=-DS, scalar2=None, op0=ALU.mult), r=[K_("sg")], w=[K_("lw")])
        for ch in range(NCH):
            cs = slice(ch * CH, (ch + 1) * CH)
            S.dve(lambda cs=cs, A=A: nc.vector.tensor_tensor_scan(out=A("L")[:, cs], data0=ones64[:, :], data1=A("lw")[:, cs], initial=0.0,
                                                             op0=ALU.mult, op1=ALU.add), r=[K_("lw"), "ones64"], w=[K_("L")])
        S.pe(lambda js=js: nc.tensor.matmul(bank(4), lhsT=lwa[64:128, js], rhs=xwa[64:128, :], start=True, stop=True), r=[lwat, xwat], w=[PSt[4]])
        S.act(lambda j=j, A=A: nc.scalar.activation(out=A("a"), in_=bank(4), func=AF.Sigmoid, bias=cvc(("a0", l), j)), r=[PSt[4], "cvec"], w=[K_("a")])
        if l > 0:
            S.pe(lambda js=js: nc.tensor.matmul(bank(5), lhsT=vlo[0:32, js], rhs=xv[0:32, :], start=True, stop=True), r=[vlot, xvt], w=[PSt[5]])
            S.act(lambda j=j, A=A: nc.scalar.activation(out=A("t1"), in_=bank(5), func=AF.Sigmoid, bias=cvc(("v0", 1), j)), r=[PSt[5], "cvec"], w=[K_("t1")])
            S.dve(lambda j=j, A=A: nc.vector.tensor_tensor(out=A("t2"), in0=vfirst[:, j, :], in1=A("v"), op=ALU.subtract), r=["vfirst", K_("v")], w=[K_("t2")])
            S.dve(lambda A=A: nc.vector.tensor_tensor(out=A("t2"), in0=A("t2"), in1=A("t1"), op=ALU.mult), r=[K_("t2"), K_("t1")], w=[K_("t2")])
            S.dve(lambda j=j, A=A: nc.vector.tensor_tensor(out=vT[:, j, :], in0=A("v"), in1=A("t2"), op=ALU.add), r=[K_("v"), K_("t2")], w=[vTt])
        else:
            S.act(lambda j=j, A=A: nc.scalar.copy(out=vT[:, j, :], in_=A("v")), r=[K_("v")], w=[vTt])
            S.act(lambda j=j, A=A: nc.scalar.copy(out=vfirst[:, j, :], in_=A("v")), r=[K_("v")], w=["vfirst"])
        S.dve(lambda j=j, A=A: nc.vector.tensor_scalar(out=A("kk"), in0=A("k"), scalar1=cvc(("k_k", l), j), scalar2=None, op0=ALU.mult), r=[K_("k"), "cvec"], w=[K_("kk")])
        S.dve(lambda A=A: nc.vector.tensor_tensor(out=A("t1"), in0=A("kk"), in1=A("kk"), op=ALU.mult), r=[K_("kk"), K_("t1")], w=[K_("t1")])
        S.pe(lambda A=A: nc.tensor.matmul(bank(6), lhsT=blk[:, :], rhs=A("t1"), start=True, stop=True), r=["blk", K_("t1")], w=[PSt[6]])
        S.act(lambda A=A: nc.scalar.activation(out=A("t2"), in_=bank(6), func=AF.Ln, bias=1e-24), r=[PSt[6], K_("t2")], w=[K_("t2")])
        S.act(lambda A=A: nc.scalar.activation(out=A("t2"), in_=A("t2"), func=AF.Exp, scale=-0.5), r=[K_("t2")], w=[K_("t2")])
        S.dve(lambda A=A: nc.vector.tensor_tensor(out=A("kap"), in0=A("kk"), in1=A("t2"), op=ALU.mult), r=[K_("kk"), K_("t2")], w=[K_("kap")])
        S.dve(lambda j=j, A=A: nc.vector.tensor_scalar(out=A("t1"), in0=A("a"), scalar1=cvc(("k_a", l), j), scalar2=omka[:, l * 4 + j:l * 4 + j + 1],
                                                  op0=ALU.mult, op1=ALU.add), r=[K_("a"), "cvec", "omka", K_("t1")], w=[K_("t1")])
        S.dve(lambda A=A: nc.vector.tensor_tensor(out=A("kp"), in0=A("k"), in1=A("t1"), op=ALU.mult), r=[K_("k"), K_("t1")], w=[K_("kp")])
        S.dve(lambda A=A: nc.vector.tensor_tensor(out=A("b"), in0=A("kap"), in1=A("a"), op=ALU.mult), r=[K_("kap"), K_("a")], w=[K_("b")])
        S.dve(lambda j=j, A=A: nc.vector.scalar_tensor_tensor(out=A("t2"), in0=A("r"), scalar=cvc(("r_k", l), j), in1=A("kp"), op0=ALU.mult, op1=ALU.mult),
              r=[K_("r"), K_("kp"), "cvec", K_("t2")], w=[K_("t2")])
        S.pe(lambda A=A: nc.tensor.matmul(bank(7), lhsT=blk[:, :], rhs=A("t2"), start=True, stop=True), r=["blk", K_("t2")], w=[PSt[7]])
        S.dve(lambda j=j: nc.vector.tensor_tensor(out=bonus[:, j, :], in0=bank(7), in1=vT[:, j, :], op=ALU.mult), r=[PSt[7], vTt], w=[bont])
        S.act(lambda A=A: nc.scalar.activation(out=A("eL"), in_=A("L"), func=AF.Exp), r=[K_("L")], w=[K_("eL")])
        S.act(lambda A=A: nc.scalar.activation(out=A("eNL"), in_=A("L"), func=AF.Exp, scale=-1.0), r=[K_("L")], w=[K_("eNL")])
        S.dve(lambda A=A: nc.vector.tensor_tensor(out=A("t1"), in0=A("L"), in1=A("lw"), op=ALU.subtract), r=[K_("L"), K_("lw"), K_("t1")], w=[K_("t1")])
        S.act(lambda A=A: nc.scalar.activation(out=A("t1"), in_=A("t1"), func=AF.Exp), r=[K_("t1")], w=[K_("t1")])
        S.dve(lambda j=j, A=A: nc.vector.tensor_copy(out=gC[:, j, :], in_=A("eL").rearrange("p (c x) -> p c x", x=CH)[:, :, CH - 1]), r=[K_("eL")], w=[gCt])
        S.dve(lambda j=j, A=A: nc.vector.tensor_tensor(out=rt[:, j, :], in0=A("r"), in1=A("eL"), op=ALU.mult), r=[K_("r"), K_("eL")], w=[rtt])
        S.dve(lambda j=j, A=A: nc.vector.tensor_tensor(out=kpt[:, j, :], in0=A("kap"), in1=A("t1"), op=ALU.mult), r=[K_("kap"), K_("t1")], w=[kptt])
        S.dve(lambda j=j, A=A: nc.vector.tensor_tensor(out=kt[:, j, :], in0=A("kp"), in1=A("eNL"), op=ALU.mult), r=[K_("kp"), K_("eNL")], w=[ktt])
        S.dve(lambda j=j, A=A: nc.vector.tensor_tensor(out=bt[:, j, :], in0=A("b"), in1=A("eNL"), op=ALU.mult), r=[K_("b"), K_("eNL")], w=[btt])
    S.barrier()
    ar.off = mark

    def t64(dt=BF16):
        return ar.alloc((512,), dt)

    P0, P0t = t64()
    P0T, P0Tt = t64()
    P1, P1t = t64()
    P1T, P1Tt = t64()
    MT, MTt = t64()
    AkkT, AkkTt = t64()
    ArkT, ArkTt = t64()
    nArbT, nArbTt = t64()
    Vt, Vtt = t64()
    Kto, Ktot = t64()
    nBto, nBtot = t64()
    KPto, KPtot = t64()
    X1, X1t = t64()
    W1, W1t = t64(F32)
    KtT, KtTt = ar.alloc((4, 64), BF16)
    U, Ut = t64()
    Ors, Orst = t64(F32)
    O, Ot = t64(F32)
    Osq, Osqt = t64(F32)
    Obf, Obft = t64(BF16)
    stat, statt = ar.alloc((32,), F32)
    tmpS, tmpSt = ar.alloc((4, 64), F32)
    gnT, gnTt = ar.alloc((4, TT), F32)

    def esplit(b0):
        return PS[0:64, b0 * 512:(b0 + 2) * 512].rearrange("p (e x) -> p e x", e=2)[:, :, 0:256]

    def v3(x):
        return x[0:64, :].rearrange("p (e x) -> p e x", e=2)

    def hbv(x, hb):
        return x[0:64, hb * 64:(hb + 1) * 64]

    def amat(b0, lhs, rhs, cs, ltok, rtok):
        def f():
            ins = None
            for e in range(2):
                for j in range(4):
                    ins = nc.tensor.matmul(bank(b0 + e, parts=64, lo=j * 64, hi=j * 64 + 64),
                                           lhsT=lhs[e * 64:(e + 1) * 64, j, cs], rhs=rhs[e * 64:(e + 1) * 64, j, cs], start=True, stop=True)
            return ins
        S.pe(f, r=[ltok, rtok], w=[PSt[b0], PSt[b0 + 1]])

    def bmm(b, lhs, rhs, ltok, rtok):
        def f():
            ins = None
            for hb in range(8):
                ins = nc.tensor.matmul(bank(b, parts=64, lo=hb * 64, hi=hb * 64 + 64), lhsT=hbv(lhs, hb), rhs=hbv(rhs, hb), start=True, stop=True)
            return ins
        S.pe(f, r=[ltok, rtok], w=[PSt[b]])

    def tpose(b, src, stok, cs):
        def f():
            ins = None
            ov = bank(b, parts=64).rearrange("p (e j k) -> p e j k", e=2, j=4)
            for j in range(4):
                ins = nc.tensor.matmul(ov[:, :, j, :], lhsT=src[:, j, cs], rhs=identb[:, :].rearrange("p (e k) -> p e k", e=2), start=True, stop=True)
            return ins
        S.pe(f, r=[stok, "identb"], w=[PSt[b]])

    pend_tail = []
    for ch in range(NCH):
        cs = slice(ch * CH, (ch + 1) * CH)
        amat(0, bt, kpt, cs, btt, kptt)
        amat(2, kpt, bt, cs, kptt, btt)
        S.dve(lambda: nc.vector.tensor_tensor(out=v3(P0T), in0=esplit(0), in1=v3(m_st), op=ALU.mult), r=[PSt[0], PSt[1], "rmask"], w=[P0Tt])
        S.dve(lambda: nc.vector.tensor_tensor(out=v3(P0), in0=esplit(2), in1=v3(m_lo), op=ALU.mult), r=[PSt[2], PSt[3], "rmask"], w=[P0t])
        S.dve(lambda: nc.vector.scalar_tensor_tensor(out=MT[0:64, :], in0=P0T[0:64, :], scalar=-1.0, in1=m_id, op0=ALU.mult, op1=ALU.add),
              r=[P0Tt, "rmask"], w=[MTt])
        Pc, Pct, PcT, PcTt = P0, P0t, P0T, P0Tt
        Pn, Pnt, PnT, PnTt = P1, P1t, P1T, P1Tt
        for k in range(1, 6):
            bmm(4, PcT, Pc, PcTt, Pct)
            if k < 5:
                bmm(5, Pc, PcT, Pct, PcTt)
            S.act(lambda Pn=Pn: nc.scalar.copy(out=Pn[0:64, :], in_=bank(4, parts=64)), r=[PSt[4]], w=[Pnt])
            if k < 5:
                S.dve(lambda PnT=PnT: nc.vector.tensor_copy(out=PnT[0:64, :], in_=bank(5, parts=64)), r=[PSt[5]], w=[PnTt])
            bmm(6, Pn, MT, Pnt, MTt)
            S.dve(lambda: nc.vector.tensor_tensor(out=MT[0:64, :], in0=MT[0:64, :], in1=bank(6, parts=64), op=ALU.add), r=[PSt[6], MTt], w=[MTt])
            Pc, Pct, PcT, PcTt, Pn, Pnt, PnT, PnTt = Pn, Pnt, PnT, PnTt, Pc, Pct, PcT, PcTt
            for _ in range(3):
                if pend_tail:
                    pend_tail.pop(0)()
        while pend_tail:
            pend_tail.pop(0)()
        amat(0, kt, kpt, cs, ktt, kptt)
        S.dve(lambda: nc.vector.tensor_tensor(out=v3(AkkT), in0=esplit(0), in1=v3(m_st), op=ALU.mult), r=[PSt[0], PSt[1], "rmask"], w=[AkkTt])
        amat(2, kt, rt, cs, ktt, rtt)
        S.dve(lambda: nc.vector.tensor_tensor(out=v3(ArkT), in0=esplit(2), in1=v3(m_inc), op=ALU.mult), r=[PSt[2], PSt[3], "rmask"], w=[ArkTt])
        amat(0, bt, rt, cs, btt, rtt)
        S.dve(lambda: nc.vector.scalar_tensor_tensor(out=v3(nArbT), in0=esplit(0), scalar=-1.0, in1=v3(m_inc), op0=ALU.mult, op1=ALU.mult),
              r=[PSt[0], PSt[1], "rmask"], w=[nArbTt])
        tpose(4, vT, vTt, cs)
        S.act(lambda: nc.scalar.copy(out=Vt[0:64, :], in_=bank(4, parts=64)), r=[PSt[4]], w=[Vtt])
        tpose(5, kt, ktt, cs)
        S.act(lambda: nc.scalar.copy(out=Kto[0:64, :], in_=bank(5, parts=64)), r=[PSt[5]], w=[Ktot])
        tpose(6, bt, btt, cs)
        S.act(lambda: nc.scalar.mul(out=nBto[0:64, :], in_=bank(6, parts=64), mul=-1.0), r=[PSt[6]], w=[nBtot])
        tpose(7, kpt, kptt, cs)
        S.act(lambda: nc.scalar.copy(out=KPto[0:64, :], in_=bank(7, parts=64)), r=[PSt[7]], w=[KPtot])
        bmm(0, AkkT, Vt, AkkTt, Vtt)
        S.act(lambda: nc.scalar.copy(out=X1[0:64, :], in_=bank(0, parts=64)), r=[PSt[0]], w=[X1t])
        bmm(1, MT, X1, MTt, X1t)
        S.act(lambda: nc.scalar.copy(out=W1[0:64, :], in_=bank(1, parts=64)), r=[PSt[1]], w=[W1t])

        def fk():
            ins = None
            for e in range(2):
                for j in range(4):
                    hb = e * 4 + j
                    ins = nc.tensor.matmul(PS[e * 64:(e + 1) * 64, 2 * 512 + j * 64:2 * 512 + j * 64 + 64], lhsT=hbv(KPto, hb), rhs=hbv(MT, hb),
                                           start=True, stop=True)
            return ins
        S.pe(fk, r=[KPtot, MTt], w=[PSt[2]])
        S.act(lambda: nc.scalar.copy(out=KtT, in_=bank(2, hi=256).rearrange("p (j t) -> p j t", t=64)), r=[PSt[2]], w=[KtTt])
        def fu():
            ins = None
            for e in range(2):
                for j in range(4):
                    ins = nc.tensor.matmul(bank(4 + e, parts=64, lo=j * 64, hi=j * 64 + 64), lhsT=KtT[e * 64:(e + 1) * 64, j, :],
                                           rhs=Sbf[e * 64:(e + 1) * 64, j, :], start=True, stop=True)
            return ins
        S.pe(fu, r=[KtTt, "rw_sbf"], w=[PSt[4], PSt[5]])
        S.dve(lambda: nc.vector.tensor_tensor(out=v3(U), in0=esplit(4), in1=v3(W1), op=ALU.add), r=[PSt[4], PSt[5], W1t], w=[Ut])
        def fo1(cs=cs):
            ins = None
            for e in range(2):
                for j in range(4):
                    ins = nc.tensor.matmul(bank(6 + e, parts=64, lo=j * 64, hi=j * 64 + 64), lhsT=rt[e * 64:(e + 1) * 64, j, cs],
                                           rhs=Sbf[e * 64:(e + 1) * 64, j, :], start=True, stop=True)
            return ins
        S.pe(fo1, r=[rtt, "rw_sbf"], w=[PSt[6], PSt[7]])
        S.act(lambda: nc.scalar.copy(out=v3(Ors), in_=esplit(6)), r=[PSt[6], PSt[7]], w=[Orst])

        def fo2():
            ins = None
            for hb in range(8):
                nc.tensor.matmul(bank(0, parts=64, lo=hb * 64, hi=hb * 64 + 64), lhsT=hbv(ArkT, hb), rhs=hbv(Vt, hb), start=(hb == 0), stop=False,
                                 skip_group_check=True)
                ins = nc.tensor.matmul(bank(0, parts=64, lo=hb * 64, hi=hb * 64 + 64), lhsT=hbv(nArbT, hb), rhs=hbv(U, hb), start=False, stop=True,
                                       skip_group_check=True)
            return ins
        S.pe(fo2, r=[ArkTt, Vtt, nArbTt, Ut], w=[PSt[0]])
        S.dve(lambda: nc.vector.tensor_tensor(out=O[0:64, :].rearrange("p (j e v) -> p e j v", j=4, e=2),
                                              in0=bank(0, parts=64).rearrange("p (e j v) -> p e j v", e=2, j=4),
                                              in1=Ors[0:64, :].rearrange("p (e j v) -> p e j v", e=2, j=4), op=ALU.add), r=[PSt[0], Orst], w=[Ot])
        def fs():
            ins = None
            for e in range(2):
                for j in range(4):
                    hb = e * 4 + j
                    o = PS[e * 64:(e + 1) * 64, 1 * 512 + j * 64:1 * 512 + j * 64 + 64]
                    nc.tensor.matmul(o, lhsT=hbv(Kto, hb), rhs=hbv(Vt, hb), start=(j == 0), stop=False, skip_group_check=True)
                    ins = nc.tensor.matmul(o, lhsT=hbv(nBto, hb), rhs=hbv(U, hb), start=False, stop=True, skip_group_check=True)
            return ins
        S.pe(fs, r=[Ktot, Vtt, nBtot, Ut], w=[PSt[1]])
        S.dve(lambda: nc.vector.tensor_tensor(out=tmpS, in0=Sst, in1=bank(1, hi=256).rearrange("p (j v) -> p j v", v=64), op=ALU.add),
              r=["rw_state", PSt[1]], w=[tmpSt])
        S.dve(lambda ch=ch: nc.vector.tensor_tensor(out=Sst, in0=tmpS, in1=bc_last(gC[:, :, ch], 64), op=ALU.mult), r=[tmpSt, gCt, "rw_state"], w=["rw_state"])
        S.act(lambda: nc.scalar.copy(out=Sbf, in_=Sst), r=["rw_state"], w=["rw_sbf"])
        tail = []
        O3 = O[0:64, :].rearrange("p (h v) -> p h v", v=64)
        Q3 = Osq[0:64, :].rearrange("p (h v) -> p h v", v=64)
        mean = stat[0:64, 0:8]
        ex2_ = stat[0:64, 8:16]
        var = stat[0:64, 16:24]
        rstd = stat[0:64, 24:32]
        tail.append(lambda cs=cs: S.dve(lambda: nc.vector.tensor_reduce(out=mean, in_=O3, op=ALU.add, axis=AX.X), r=[Ot], w=[statt]))
        tail.append(lambda cs=cs: S.act(lambda: nc.scalar.activation(out=Osq[0:64, :], in_=O[0:64, :], func=AF.Square), r=[Ot], w=[Osqt]))
        tail.append(lambda cs=cs: S.dve(lambda: nc.vector.tensor_reduce(out=ex2_, in_=Q3, op=ALU.add, axis=AX.X), r=[Osqt, statt], w=[statt]))
        tail.append(lambda cs=cs: S.dve(lambda: nc.vector.tensor_scalar(out=mean, in0=mean, scalar1=1.0 / 64, scalar2=None, op0=ALU.mult), r=[statt], w=[statt]))
        tail.append(lambda cs=cs: S.dve(lambda: nc.vector.tensor_tensor(out=var, in0=mean, in1=mean, op=ALU.mult), r=[statt], w=[statt]))
        tail.append(lambda cs=cs: S.dve(lambda: nc.vector.scalar_tensor_tensor(out=var, in0=ex2_, scalar=1.0 / 64, in1=var, op0=ALU.mult, op1=ALU.subtract), r=[statt], w=[statt]))
        tail.append(lambda cs=cs: S.act(lambda: nc.scalar.activation(out=rstd, in_=var, func=AF.Ln, bias=64e-5), r=[statt], w=[statt]))
        tail.append(lambda cs=cs: S.act(lambda: nc.scalar.activation(out=rstd, in_=rstd, func=AF.Exp, scale=-0.5), r=[statt], w=[statt]))
        tail.append(lambda cs=cs: S.dve(lambda: nc.vector.tensor_tensor(out=O3, in0=O3, in1=bc_last(mean, 64), op=ALU.subtract), r=[Ot, statt], w=[Ot]))
        tail.append(lambda cs=cs: S.dve(lambda: nc.vector.tensor_tensor(out=Obf[0:64, :].rearrange("p (h v) -> p h v", v=64), in0=O3, in1=bc_last(rstd, 64), op=ALU.mult),
              r=[Ot, statt], w=[Obft]))
        def fT():
            ins = None
            for j in range(4):
                ins = nc.tensor.matmul(bank(3, lo=j * 64, hi=j * 64 + 64), lhsT=Obf[0:64, j * 128:(j + 1) * 128], rhs=identb[0:64, 0:64], start=True, stop=True)
            return ins
        tail.append(lambda cs=cs: S.pe(fT, r=[Obft, "identb"], w=[PSt[3]]))
        tail.append(lambda cs=cs: S.act(lambda cs=cs: nc.scalar.copy(out=gnT[:, :, cs], in_=bank(3, hi=256).rearrange("p (j t) -> p j t", t=64)), r=[PSt[3]], w=[gnTt]))
        pend_tail = tail
    while pend_tail:
        pend_tail.pop(0)()
    for j in range(4):
        js = slice(j * 128, (j + 1) * 128)
        pb = 4 + j % 2
        S.pe(lambda js=js, pb=pb: nc.tensor.matmul(bank(pb), lhsT=glo[:, js], rhs=sxg, start=True, stop=True), r=[glot, sxgt], w=[PSt[pb]])
        S.dve(lambda j=j: nc.vector.tensor_scalar(out=gnT[:, j, :], in0=gnT[:, j, :], scalar1=cvc(("lnx_w", l), j), scalar2=cvc(("lnx_b", l), j),
                                                  op0=ALU.mult, op1=ALU.add), r=[gnTt, "cvec"], w=[gnTt])
        S.dve(lambda j=j: nc.vector.tensor_tensor(out=gnT[:, j, :], in0=gnT[:, j, :], in1=bonus[:, j, :], op=ALU.add), r=[gnTt, bont], w=[gnTt])
        S.dve(lambda j=j, pb=pb: nc.vector.tensor_tensor(out=yrw[:, j, :], in0=gnT[:, j, :], in1=bank(pb), op=ALU.mult), r=[gnTt, PSt[pb]], w=[yrwt])


_CACHE = {}


def _prep_inputs(inp, b, depth, T):
    f = lambda a: np.ascontiguousarray(np.asarray(a, np.float32))
    m = {}
    m["xT"] = f(np.asarray(inp["x"][b])[:T].T)
    m["memT"] = f(np.asarray(inp["mem"][b]).T)
    return m


def _shared_inputs(inp, depth):
    f = lambda a: np.ascontiguousarray(np.asarray(a, np.float32))
    m = {}
    m["cvec"] = _build_cvec(inp, depth)
    for k in ("w_comb", "branch_proj", "w_mix_out", "ca_wq", "ca_wkv", "ca_wo", "ffn_up", "ffn_down", "g_lora"):
        m[k] = f(np.asarray(inp[k])[:depth])
    m["w_vres"] = f(inp["w_vres"])
    m["v_lora"] = f(inp["v_lora"])
    m["lora_wa"] = f(np.concatenate([np.asarray(inp["w_lora"])[:depth], np.asarray(inp["a_lora"])[:depth]], axis=1))
    for k, v in _build_consts().items():
        m["c_" + k] = v
    return m


def kernel(**inputs):
    depth, T = 2, 4096
    key = (depth, T)
    if key not in _CACHE:
        _CACHE[key] = build_program(T=T, depth=depth)
    nc = _CACHE[key]
    shared = _shared_inputs(inputs, depth)
    in_maps = []
    for b in range(8):
        m = dict(shared)
        m.update(_prep_inputs(inputs, b, depth, T))
        in_maps.append(m)
    res = run_bass_kernel_spmd(nc, in_maps, core_ids=list(range(8)))
    out = np.stack([np.asarray(r["outT"]).T for r in res.results], axis=0)
    return np.ascontiguousarray(out.astype(np.float32))
```
